# Optimizing a Trainium2 kernel written in Bass

```python
import jax, jax.numpy as jnp
from jax import lax
import numpy as np

D_MODEL = 2048
BATCH = 8
SEQ = 4096
DEPTH = 2

HA_Q = 8
HA_KV = 2
HEAD_DIM = 128
WINDOW = 128
BLOCK = 128
HB = 8
QK_NOPE = 128
QK_ROPE = 64
V_DIM = 128
Q_RANK = 512
KV_RANK = 512
D_FF = 5632
ROPE_THETA = 10000.0
EPS = 1e-6
N_MOD = 9

COLS_QA = HA_Q * HEAD_DIM
COLS_KA = HA_KV * HEAD_DIM
COLS_VA = HA_KV * HEAD_DIM
COLS_CQ = Q_RANK
COLS_CKV = KV_RANK
COLS_KR = QK_ROPE
COLS_GATE = 2 * D_MODEL
IN_SIZES = (COLS_QA, COLS_KA, COLS_VA, COLS_CQ, COLS_CKV, COLS_KR, COLS_GATE)
IN_COLS = sum(IN_SIZES)
IN_SPLITS = tuple(int(s) for s in np.cumsum(IN_SIZES)[:-1])

kernel_name = "hybrid_gated_swa_mla_macaron_adaln"


def rmsnorm(x, g):
    xf = x.astype(jnp.float32)
    y = xf * lax.rsqrt(jnp.mean(xf * xf, axis=-1, keepdims=True) + EPS)
    return (y * g.astype(jnp.float32)).astype(x.dtype)


def modulate(h, shift, scale):
    return h * (1 + scale[:, None, :]) + shift[:, None, :]


def rope_tables(positions, dim):
    freqs = ROPE_THETA ** (-jnp.arange(0, dim, 2, dtype=jnp.float32) / dim)
    ang = positions.astype(jnp.float32)[..., None] * freqs
    return jnp.cos(ang), jnp.sin(ang)


def apply_rope(x, cos, sin):
    xf = x.astype(jnp.float32)
    x1, x2 = jnp.split(xf, 2, axis=-1)
    return jnp.concatenate([x1 * cos - x2 * sin, x2 * cos + x1 * sin], axis=-1).astype(x.dtype)


def swiglu(h, w_gu, w_d):
    gu = h @ w_gu
    g, u = jnp.split(gu, 2, axis=-1)
    return (jax.nn.silu(g) * u) @ w_d


def window_gqa_sink(q, k, v, sink):
    B, S = q.shape[0], q.shape[1]
    nb = S // BLOCK
    G = HA_Q // HA_KV
    qb = q.reshape(B, nb, BLOCK, HA_KV, G, HEAD_DIM)
    pad = ((0, 0), (BLOCK, BLOCK), (0, 0), (0, 0))
    kp = jnp.pad(k, pad).reshape(B, nb + 2, BLOCK, HA_KV, HEAD_DIM)
    vp = jnp.pad(v, pad).reshape(B, nb + 2, BLOCK, HA_KV, HEAD_DIM)
    kb = jnp.concatenate([kp[:, :-2], kp[:, 1:-1], kp[:, 2:]], axis=2)
    vb = jnp.concatenate([vp[:, :-2], vp[:, 1:-1], vp[:, 2:]], axis=2)
    s = jnp.einsum('bnqhgd,bnkhd->bnhgqk', qb, kb).astype(jnp.float32) * (HEAD_DIM ** -0.5)
    r = jnp.arange(BLOCK)[:, None]
    j = jnp.arange(3 * BLOCK)[None, :]
    rel_ok = jnp.abs(j - BLOCK - r) <= WINDOW
    kpos = jnp.arange(nb)[:, None] * BLOCK - BLOCK + jnp.arange(3 * BLOCK)[None, :]
    pos_ok = (kpos >= 0) & (kpos < S)
    mask = rel_ok[None, :, :] & pos_ok[:, None, :]
    s = jnp.where(mask[None, :, None, None, :, :], s, -1e30)
    sk = sink.astype(jnp.float32).reshape(HA_KV, G)[None, None, :, :, None, None]
    m = jnp.maximum(jnp.max(s, axis=-1, keepdims=True), sk)
    p = jnp.exp(s - m)
    denom = jnp.sum(p, axis=-1, keepdims=True) + jnp.exp(sk - m)
    p = (p / denom).astype(v.dtype)
    o = jnp.einsum('bnhgqk,bnkhd->bnqhgd', p, vb)
    return o.reshape(B, S, HA_Q * HEAD_DIM)


def mla_attention(q_nope, q_rope, k_nope, k_rope, v):
    B, S = q_nope.shape[0], q_nope.shape[1]
    nb = S // BLOCK
    scale = (QK_NOPE + QK_ROPE) ** -0.5
    qn = q_nope.reshape(B, nb, BLOCK, HB, QK_NOPE).transpose(1, 0, 2, 3, 4)
    qr = q_rope.reshape(B, nb, BLOCK, HB, QK_ROPE).transpose(1, 0, 2, 3, 4)

    def one_block(args):
        qn_b, qr_b = args
        s = (jnp.einsum('bqhd,bkhd->bhqk', qn_b, k_nope)
             + jnp.einsum('bqhr,bkr->bhqk', qr_b, k_rope)).astype(jnp.float32) * scale
        p = jax.nn.softmax(s, axis=-1).astype(v.dtype)
        return jnp.einsum('bhqk,bkhd->bqhd', p, v)

    o = lax.map(one_block, (qn, qr))
    return o.transpose(1, 0, 2, 3, 4).reshape(B, S, HB * V_DIM)


def token_mixer(h, cos_a, sin_a, cos_r, sin_r, w_in, b_gate, sink, g_cq, g_ckv,
                w_uq, w_ukv, w_oa, w_ob, w_out):
    B, S, _ = h.shape
    qa, ka, va, cq, ckv, kr, gates = jnp.split(h @ w_in, IN_SPLITS, axis=-1)
    qa = apply_rope(qa.reshape(B, S, HA_Q, HEAD_DIM), cos_a, sin_a)
    ka = apply_rope(ka.reshape(B, S, HA_KV, HEAD_DIM), cos_a, sin_a)
    va = va.reshape(B, S, HA_KV, HEAD_DIM)
    y_a = window_gqa_sink(qa, ka, va, sink) @ w_oa
    q = (rmsnorm(cq, g_cq) @ w_uq).reshape(B, S, HB, QK_NOPE + QK_ROPE)
    q_nope, q_rope = q[..., :QK_NOPE], q[..., QK_NOPE:]
    q_rope = apply_rope(q_rope, cos_r[:, :, None, :], sin_r[:, :, None, :])
    kv = (rmsnorm(ckv, g_ckv) @ w_ukv).reshape(B, S, HB, QK_NOPE + V_DIM)
    k_nope, v_b = kv[..., :QK_NOPE], kv[..., QK_NOPE:]
    k_rope = apply_rope(kr, cos_r, sin_r)
    y_b = mla_attention(q_nope, q_rope, k_nope, k_rope, v_b) @ w_ob
    g_a, g_b = jnp.split(jax.nn.sigmoid(gates + b_gate), 2, axis=-1)
    return (g_a * y_a + g_b * y_b) @ w_out


def setup_inputs(seed: int = 0) -> dict:
    key = jax.random.key(seed)
    ks = jax.random.split(key, 24)
    f32 = jnp.float32
    D = D_MODEL

    def nrm(k, shape, s):
        return jax.random.normal(k, shape, f32) * s

    x = nrm(ks[0], (BATCH, SEQ, D), 1.0)
    c = nrm(ks[1], (BATCH, D), 1.0)
    offsets = jax.random.randint(ks[2], (BATCH, 1), 0, 1024, dtype=jnp.int32)
    positions = offsets + jnp.arange(SEQ, dtype=jnp.int32)[None, :]
    return {
        "x": x,
        "c": c,
        "positions": positions,
        "norm_g": 1.0 + nrm(ks[3], (DEPTH, 3, D), 0.1),
        "w_ada": nrm(ks[4], (DEPTH, D, N_MOD * D), 0.5 * D ** -0.5),
        "b_ada": nrm(ks[5], (DEPTH, N_MOD * D), 0.1),
        "w_ffn1_gu": nrm(ks[6], (DEPTH, D, 2 * D_FF), D ** -0.5),
        "w_ffn1_d": nrm(ks[7], (DEPTH, D_FF, D), D_FF ** -0.5),
        "w_ffn2_gu": nrm(ks[8], (DEPTH, D, 2 * D_FF), D ** -0.5),
        "w_ffn2_d": nrm(ks[9], (DEPTH, D_FF, D), D_FF ** -0.5),
        "w_in": nrm(ks[10], (DEPTH, D, IN_COLS), D ** -0.5),
        "b_gate": nrm(ks[11], (DEPTH, 2 * D), 0.1),
        "sink": nrm(ks[12], (DEPTH, HA_Q), 1.0),
        "g_cq": 1.0 + nrm(ks[13], (DEPTH, Q_RANK), 0.1),
        "g_ckv": 1.0 + nrm(ks[14], (DEPTH, KV_RANK), 0.1),
        "w_uq": nrm(ks[15], (DEPTH, Q_RANK, HB * (QK_NOPE + QK_ROPE)), Q_RANK ** -0.5),
        "w_ukv": nrm(ks[16], (DEPTH, KV_RANK, HB * (QK_NOPE + V_DIM)), KV_RANK ** -0.5),
        "w_oa": nrm(ks[17], (DEPTH, HA_Q * HEAD_DIM, D), (HA_Q * HEAD_DIM) ** -0.5),
        "w_ob": nrm(ks[18], (DEPTH, HB * V_DIM, D), (HB * V_DIM) ** -0.5),
        "w_out": nrm(ks[19], (DEPTH, D, D), D ** -0.5),
        "g_final": 1.0 + nrm(ks[20], (D,), 0.1),
    }


def reference(x, c, positions, norm_g, w_ada, b_ada, w_ffn1_gu, w_ffn1_d, w_ffn2_gu, w_ffn2_d,
              w_in, b_gate, sink, g_cq, g_ckv, w_uq, w_ukv, w_oa, w_ob, w_out, g_final):
    cos_a, sin_a = rope_tables(positions, HEAD_DIM)
    cos_a, sin_a = cos_a[:, :, None, :], sin_a[:, :, None, :]
    cos_r, sin_r = rope_tables(positions, QK_ROPE)
    c_act = jax.nn.silu(c)
    for l in range(DEPTH):
        sh1, sc1, gt1, sh2, sc2, gt2, sh3, sc3, gt3 = jnp.split(c_act @ w_ada[l] + b_ada[l], N_MOD, axis=-1)
        h = modulate(rmsnorm(x, norm_g[l, 0]), sh1, sc1)
        x = x + 0.5 * gt1[:, None, :] * swiglu(h, w_ffn1_gu[l], w_ffn1_d[l])
        h = modulate(rmsnorm(x, norm_g[l, 1]), sh2, sc2)
        x = x + gt2[:, None, :] * token_mixer(h, cos_a, sin_a, cos_r, sin_r, w_in[l], b_gate[l], sink[l],
                                              g_cq[l], g_ckv[l], w_uq[l], w_ukv[l], w_oa[l], w_ob[l], w_out[l])
        h = modulate(rmsnorm(x, norm_g[l, 2]), sh3, sc3)
        x = x + 0.5 * gt3[:, None, :] * swiglu(h, w_ffn2_gu[l], w_ffn2_d[l])
    return rmsnorm(x, g_final)
```

```python
import numpy as np
from contextlib import ExitStack
import concourse.bass as bass
import concourse.mybir as mybir
from concourse.bass_utils import run_bass_kernel_spmd

F32 = mybir.dt.float32
BF16 = mybir.dt.bfloat16
I32 = mybir.dt.int32
AF = mybir.ActivationFunctionType
ALU = mybir.AluOpType

D = 2048
S = 4096
T = 512
NT = S // T
KC = D // 128
DFF = 5632
FC = DFF // 128
DEPTH = 2
NCORES = 8
EPS = 1e-6
INV2PI = float(np.float32(1.0 / (2.0 * np.pi)))
TWOPI_S = 6.28318
MAGIC = 12582912.0
SC_A = 128.0 ** -0.5
SC_B = 192.0 ** -0.5

DBG_B = None
MLA_HEADS = 8
MLA_KB = 32
STAGES = None


class Buf:
    __slots__ = ("w", "r")

    def __init__(self):
        self.w = {}
        self.r = {}


class TL:
    def __init__(self, t, n=1):
        self.t = t
        self.b = [Buf() for _ in range(n)]


class Ring:
    def __init__(self, tiles):
        self.t = tiles
        self.i = 0

    def next(self):
        x = self.t[self.i % len(self.t)]
        self.i += 1
        return x


ENG = ("pe", "act", "dve", "pool", "sp")


class Prog:
    def __init__(self, nc, es):
        self.nc = nc
        self.es = es
        self.sem = {}
        self.cnt = {}
        self.ops = {e: [] for e in ENG}
        self.known = {e: {} for e in ENG}
        for k in ("pe", "act", "dve", "pool"):
            self.newsem(k)
        self.rings = {}
        for q, n in (("sp", 16), ("pool", 8)):
            self.rings[q] = [self.newsem("d%s%d" % (q, i)) for i in range(n)]
        self.ri = {q: 0 for q in self.rings}

    def newsem(self, k):
        self.sem[k] = self.es.enter_context(self.nc.semaphore(k))
        self.cnt[k] = 0
        return k

    def _deps(self, eng, reads, writes, extra=None):
        w = {}

        def acc(d):
            for k, v in d.items():
                if w.get(k, 0) < v:
                    w[k] = v

        for b in reads:
            acc(b.w)
        for b in writes:
            acc(b.w)
            acc(b.r)
        if extra:
            acc(extra)
        kn = self.known[eng]
        need = []
        for k, v in w.items():
            if eng == "pe" and k == "pe":
                continue
            if kn.get(k, 0) < v:
                kn[k] = v
                need.append((k, v))
        return need

    def _mark(self, key, val, reads, writes):
        for b in reads:
            if b.r.get(key, 0) < val:
                b.r[key] = val
        for b in writes:
            b.w = {key: val}
            b.r = {}

    def op(self, eng, fn, reads=(), writes=()):
        need = self._deps(eng, reads, writes)
        self.cnt[eng] += 1
        val = self.cnt[eng]
        self.ops[eng].append((need, fn, eng, 1))
        self._mark(eng, val, reads, writes)

    def dma(self, q, fn, reads=(), writes=()):
        ring = self.rings[q]
        key = ring[self.ri[q] % len(ring)]
        self.ri[q] += 1
        extra = {key: self.cnt[key]} if self.cnt[key] else None
        need = self._deps(q, reads, writes, extra)
        self.cnt[key] += 16
        val = self.cnt[key]
        self.ops[q].append((need, fn, key, 16))
        self._mark(key, val, reads, writes)

    def flush(self, drain=True):
        if drain:
            need = []
            for q in self.rings:
                for k in self.rings[q]:
                    if self.cnt[k] and self.known["sp"].get(k, 0) < self.cnt[k]:
                        self.known["sp"][k] = self.cnt[k]
                        need.append((k, self.cnt[k]))
            if need:
                self.ops["sp"].append((need, None, None, 0))
        with self.nc.Block() as blk:
            for eng, reg in (("pe", blk.tensor), ("act", blk.scalar), ("dve", blk.vector),
                             ("pool", blk.gpsimd), ("sp", blk.sync)):
                lst = self.ops[eng]
                if not lst:
                    continue

                def body(e, lst=lst):
                    for need, fn, key, inc in lst:
                        for k, v in need:
                            e.wait_ge(self.sem[k], v)
                        if fn is not None:
                            fn(e).then_inc(self.sem[key], inc)

                reg(body)
                self.ops[eng] = []


def mm_group(P, out_ps, pairs, reads, writes):
    n = len(pairs)

    def fn(e):
        ins = None
        for i, (l, r) in enumerate(pairs):
            ins = e.matmul(out_ps, l, r, start=(i == 0), stop=(i == n - 1))
        return ins

    P.op("pe", fn, reads, writes)


OFF_QA, OFF_KA, OFF_VA, OFF_CQ, OFF_CKV, OFF_KR, OFF_GATE = 0, 1024, 1280, 1536, 2048, 2560, 2624
IN_QA = 0
IN_QASW = 8
IN_KA = 16
IN_KASW = 18
IN_CQ = 20
IN_CKV = 24
IN_KR = 28
IN_KRSW = 29
IN_GATE = 30
IN_N = 62


def w_in_segments(j):
    if j < IN_QASW:
        return [(0, OFF_QA + 128 * j, 128)]
    if j < IN_KA:
        h = j - IN_QASW
        return [(0, OFF_QA + 128 * h + 64, 64), (64, OFF_QA + 128 * h, 64)]
    if j < IN_KASW:
        return [(0, OFF_KA + 128 * (j - IN_KA), 128)]
    if j < IN_CQ:
        h = j - IN_KASW
        return [(0, OFF_KA + 128 * h + 64, 64), (64, OFF_KA + 128 * h, 64)]
    if j < IN_CKV:
        return [(0, OFF_CQ + 128 * (j - IN_CQ), 128)]
    if j < IN_KR:
        return [(0, OFF_CKV + 128 * (j - IN_CKV), 128)]
    if j == IN_KR:
        return [(0, OFF_KR, 64), (64, OFF_KR, 64)]
    if j == IN_KRSW:
        return [(0, OFF_KR + 32, 32), (32, OFF_KR, 32), (64, OFF_KR + 32, 32), (96, OFF_KR, 32)]
    return [(0, OFF_GATE + 128 * (j - IN_GATE), 128)]


UQ_QN, UQ_QR, UQ_QRSW, UQ_N = 0, 8, 12, 16


def w_uq_segments(j):
    if j < UQ_QR:
        return [(0, 192 * j, 128)]
    if j < UQ_QRSW:
        p = j - UQ_QR
        return [(0, 192 * (2 * p) + 128, 64), (64, 192 * (2 * p + 1) + 128, 64)]
    p = j - UQ_QRSW
    a = 192 * (2 * p) + 128
    b = 192 * (2 * p + 1) + 128
    return [(0, a + 32, 32), (32, a, 32), (64, b + 32, 32), (96, b, 32)]


def build_nc():
    nc = bass.Bass("TRN2", target_bir_lowering=False)

    declared = []
    nc._declared_inputs = declared

    def run(name):
        if STAGES is None:
            return True
        if name in STAGES:
            return True
        if name in ("tables", "mixerA", "mixerB") and "mixer" in STAGES:
            return True
        if name == "mixer":
            return any(k in STAGES for k in ("tables", "mixerA", "mixerB"))
        return False

    def din(name, shape, dt=F32):
        declared.append(name)
        return nc.dram_tensor(name, list(shape), dt, kind="ExternalInput").ap()

    def dscr(name, shape, dt):
        return nc.dram_tensor(name, list(shape), dt, kind="Internal").ap()

    xT = din("xT", [D, S])
    cT = din("cT", [128, KC])
    posb = din("posb", [128, S], I32)
    NVEC = DEPTH * (48 + 144 + 32 + 4 + 4) + 16
    vecs_d = din("vecs", [128, NVEC])
    consts_d = din("consts", [128, 4])
    masks_d = din("masks", [128, 2, 512])
    sinkrow_d = din("sinkrow", [1, DEPTH * 2 * 512])
    w_ada = din("w_ada", [DEPTH, D, 9 * D])
    w_f_gu = [din("w_ffn%d_gu" % (f + 1), [DEPTH, D, 2 * DFF]) if run("ffn%d" % (f + 1)) else None for f in range(2)]
    w_f_d = [din("w_ffn%d_d" % (f + 1), [DEPTH, DFF, D]) if run("ffn%d" % (f + 1)) else None for f in range(2)]
    w_in = din("w_in", [DEPTH, D, 6720])
    w_uq = din("w_uq", [DEPTH, 512, 1536])
    w_ukv = din("w_ukv", [DEPTH, 512, 2048])
    w_oa = din("w_oa", [DEPTH, 1024, D])
    w_ob = din("w_ob", [DEPTH, 1024, D])
    w_out = din("w_out", [DEPTH, D, D])
    outT = nc.dram_tensor("outT", [D, S], F32, kind="ExternalOutput").ap()

    xres = dscr("xres", [D, S], F32)
    ada_b = [[dscr("ada_b%d_%d" % (l, j), [48, 128, KC, 128], BF16) for j in range(3)] for l in range(DEPTH)]
    tabs_d = dscr("tabs", [4, 128, S], F32)
    gu_b = [[dscr("gu_b%d_%d" % (l, f), [FC, 128, KC, 256], BF16) for f in range(2)] for l in range(DEPTH)]
    d_b = [[dscr("d_b%d_%d" % (l, f), [KC, 128, FC, 128], BF16) for f in range(2)] for l in range(DEPTH)]
    in_b = [dscr("in_b%d" % l, [IN_N, 128, KC, 128], BF16) for l in range(DEPTH)]
    va_b = [dscr("va_b%d" % l, [128, KC, 256], BF16) for l in range(DEPTH)]
    uq_b = [dscr("uq_b%d" % l, [UQ_N, 128, 4, 128], BF16) for l in range(DEPTH)]
    kn_b = [dscr("kn_b%d" % l, [8, 128, 4, 128], BF16) for l in range(DEPTH)]
    vbw_b = [dscr("vbw_b%d" % l, [128, 4, 1024], BF16) for l in range(DEPTH)]
    oa_b = [dscr("oa_b%d" % l, [KC, 128, 8, 128], BF16) for l in range(DEPTH)]
    ob_b = [dscr("ob_b%d" % l, [KC, 128, 8, 128], BF16) for l in range(DEPTH)]
    out_b = [dscr("out_b%d" % l, [KC, 128, KC, 128], BF16) for l in range(DEPTH)]
    kaT_d = dscr("kaT_d", [2, 128, S], BF16)
    va_d = dscr("va_d", [32, 128, 256], BF16)
    knT_d = dscr("knT_d", [8, 128, S], BF16)
    krT_d = dscr("krT_d", [128, S], BF16)
    vb_d = dscr("vb_d", [8, 128, 32, 128], BF16)

    with ExitStack() as es:
        P = Prog(nc, es)

        def psb(name, shape, dt):
            return es.enter_context(nc.sbuf_tensor(name, list(shape), dt))

        vecs = TL(psb("vecs_sb", [128, NVEC], F32))
        modv = TL(psb("modv", [128, DEPTH * 9 * KC], F32))
        ones_bf = TL(psb("ones_bf", [128, 128], BF16))
        ones_f = TL(psb("ones_f", [1, 128], F32))
        consts = TL(psb("consts_sb", [128, 4], F32))

        def vcol(l, kind, c):
            base = l * 232
            if kind == "ng":
                return base + c
            if kind == "ada":
                return base + 48 + c
            if kind == "bg":
                return base + 192 + c
            if kind == "gcq":
                return base + 224 + c
            if kind == "gckv":
                return base + 228 + c
            raise ValueError

        GFIN = DEPTH * 232

        def mcol(l, j, c):
            return (l * 9 + j) * KC + c

        xres_b = [[Buf() for _ in range(KC)] for _ in range(NT)]
        ada_bufs = [[[Buf() for _ in range(48)] for _ in range(3)] for _ in range(DEPTH)]
        tabs_buf = Buf()
        gu_bufs = [[[(Buf(), Buf()) for _ in range(FC)] for _ in range(2)] for _ in range(DEPTH)]
        d_bufs = [[[Buf() for _ in range(KC)] for _ in range(2)] for _ in range(DEPTH)]
        in_bufs = [[Buf() for _ in range(IN_N)] for _ in range(DEPTH)]
        va_bufs = [Buf() for _ in range(DEPTH)]
        uq_bufs = [[Buf() for _ in range(UQ_N)] for _ in range(DEPTH)]
        kn_bufs = [[Buf() for _ in range(8)] for _ in range(DEPTH)]
        vbw_bufs = [Buf() for _ in range(DEPTH)]
        oa_bufs = [[Buf() for _ in range(KC)] for _ in range(DEPTH)]
        ob_bufs = [[Buf() for _ in range(KC)] for _ in range(DEPTH)]
        out_bufs = [[Buf() for _ in range(KC)] for _ in range(DEPTH)]
        kaT_buf = [[Buf() for _ in range(2)] for _ in range(NT)]
        va_buf = [[Buf() for _ in range(4)] for _ in range(NT)]
        knT_buf = [[Buf() for _ in range(8)] for _ in range(NT)]
        krT_buf = [Buf() for _ in range(NT)]
        vb_buf = [[Buf() for _ in range(4)] for _ in range(NT)]

        def conv(dst, src, wb):
            P.dma("pool", lambda e, dst=dst, src=src: e.dma_start(out=dst, in_=src), writes=[wb])

        def conv_ffn(l, f):
            wv = w_f_gu[f][l].rearrange("(k p) f -> p k f", p=128)
            for t in range(FC):
                for half in range(2):
                    conv(gu_b[l][f][t, :, :, half * 128:(half + 1) * 128],
                         wv[:, :, half * DFF + t * 128: half * DFF + (t + 1) * 128], gu_bufs[l][f][t][half])
            dv = w_f_d[f][l].rearrange("(k p) f -> p k f", p=128)
            for r in range(KC):
                conv(d_b[l][f][r], dv[:, :, r * 128:(r + 1) * 128], d_bufs[l][f][r])

        def conv_mixer(l):
            wv = w_in[l].rearrange("(k p) f -> p k f", p=128)
            for j in range(IN_N):
                segs = w_in_segments(j)
                if len(segs) == 1:
                    conv(in_b[l][j], wv[:, :, segs[0][1]:segs[0][1] + 128], in_bufs[l][j])
                else:
                    bl = []
                    for (dc, sc, n) in segs:
                        b = Buf()
                        conv(in_b[l][j][:, :, dc:dc + n], wv[:, :, sc:sc + n], b)
                        bl.append(b)
                    jb = in_bufs[l][j]
                    for b in bl:
                        for k, v in b.w.items():
                            if jb.w.get(k, 0) < v:
                                jb.w[k] = v
            conv(va_b[l], wv[:, :, OFF_VA:OFF_VA + 256], va_bufs[l])
            uv = w_uq[l].rearrange("(k p) f -> p k f", p=128)
            for j in range(UQ_N):
                bl = []
                for (dc, sc, n) in w_uq_segments(j):
                    b = Buf()
                    conv(uq_b[l][j][:, :, dc:dc + n], uv[:, :, sc:sc + n], b)
                    bl.append(b)
                jb = uq_bufs[l][j]
                for b in bl:
                    for k, v in b.w.items():
                        if jb.w.get(k, 0) < v:
                            jb.w[k] = v
            kv = w_ukv[l].rearrange("(k p) f -> p k f", p=128)
            for h in range(8):
                conv(kn_b[l][h], kv[:, :, 256 * h:256 * h + 128], kn_bufs[l][h])
            bl = []
            for h in range(8):
                b = Buf()
                conv(vbw_b[l][:, :, 128 * h:128 * h + 128], kv[:, :, 256 * h + 128:256 * h + 256], b)
                bl.append(b)
            for b in bl:
                for k, v in b.w.items():
                    if vbw_bufs[l].w.get(k, 0) < v:
                        vbw_bufs[l].w[k] = v
            av = w_oa[l].rearrange("(k p) f -> p k f", p=128)
            bv = w_ob[l].rearrange("(k p) f -> p k f", p=128)
            ov = w_out[l].rearrange("(k p) f -> p k f", p=128)
            for c in range(KC):
                conv(oa_b[l][c], av[:, :, c * 128:(c + 1) * 128], oa_bufs[l][c])
                conv(ob_b[l][c], bv[:, :, c * 128:(c + 1) * 128], ob_bufs[l][c])
                conv(out_b[l][c], ov[:, :, c * 128:(c + 1) * 128], out_bufs[l][c])

        def conv_ada(l, j):
            wav = w_ada[l].rearrange("(k p) f -> p k f", p=128)
            for ch in range(48):
                col = (48 * j + ch) * 128
                conv(ada_b[l][j][ch], wav[:, :, col:col + 128], ada_bufs[l][j][ch])

        conv_jobs = []
        for l in range(DEPTH):
            if run("ffn1"):
                conv_jobs.append(lambda l=l: (conv_ada(l, 0), conv_ffn(l, 0)))
            if run("mixer"):
                conv_jobs.append(lambda l=l: (conv_ada(l, 1), conv_mixer(l)))
            if run("ffn2"):
                conv_jobs.append(lambda l=l: (conv_ada(l, 2), conv_ffn(l, 1)))
        conv_pos = [0]

        def emit_next_conv():
            if conv_pos[0] < len(conv_jobs):
                conv_jobs[conv_pos[0]]()
                conv_pos[0] += 1

        emit_next_conv()

        cact = TL(psb("cact", [128, KC], BF16))
        ada_ctr = [0]

        def adaln_ops(l, j, sb, ps):
            ada_ctr[0] += 1
            pf = "ad%d_" % ada_ctr[0]
            wts = [TL(sb(pf + "w%d" % i, [128, 8, KC, 128], BF16)) for i in range(3)]
            mps = TL(ps.enter_context(nc.psum_tensor(pf + "mps", [128, 512], F32)))
            modraw = TL(sb(pf + "modraw", [128, 48], F32))
            tmp16 = TL(sb(pf + "tmp16", [128, KC], F32))
            for g in range(6):
                wt = wts[g % 3]
                P.dma("sp", lambda e, wt=wt, g=g: e.dma_start(out=wt.t[:], in_=ada_b[l][j][8 * g:8 * g + 8].rearrange("c p k f -> p c k f")),
                      reads=ada_bufs[l][j][8 * g:8 * g + 8], writes=wt.b)
                for cc in range(8):
                    ch = 8 * g + cc
                    mm_group(P, mps.t[:, ch:ch + 1], [(wt.t[:, cc, k, :], cact.t[:, k:k + 1]) for k in range(KC)], reads=wt.b + cact.b, writes=mps.b)
            c0 = vcol(l, "ada", 48 * j)
            P.op("dve", lambda e: e.tensor_tensor(out=modraw.t[:], in0=mps.t[:, 0:48], in1=vecs.t[:, c0:c0 + 48], op=ALU.add),
                 reads=mps.b + vecs.b, writes=modraw.b)
            sh = modraw.t[:, 0:KC]
            sc = modraw.t[:, KC:2 * KC]
            gt = modraw.t[:, 2 * KC:3 * KC]
            g0 = vcol(l, "ng", j * KC)
            a0 = mcol(l, 3 * j, 0)
            P.op("dve", lambda e: e.tensor_scalar(out=tmp16.t[:], in0=sc, scalar1=1.0, scalar2=None, op0=ALU.add), reads=modraw.b, writes=tmp16.b)
            P.op("dve", lambda e: e.tensor_tensor(out=modv.t[:, a0:a0 + KC], in0=tmp16.t[:], in1=vecs.t[:, g0:g0 + KC], op=ALU.mult),
                 reads=tmp16.b + vecs.b, writes=modv.b)
            P.op("dve", lambda e: e.tensor_copy(out=modv.t[:, a0 + KC:a0 + 2 * KC], in_=sh), reads=modraw.b, writes=modv.b)
            fac = 1.0 if j == 1 else 0.5
            P.op("dve", lambda e: e.tensor_scalar(out=modv.t[:, a0 + 2 * KC:a0 + 3 * KC], in0=gt, scalar1=fac, scalar2=None, op0=ALU.mult),
                 reads=modraw.b, writes=modv.b)

        def adaln_phase(l, j):
            with ExitStack() as ps:
                def sb(name, shape, dt):
                    return ps.enter_context(nc.sbuf_tensor(name, list(shape), dt))
                adaln_ops(l, j, sb, ps)
                P.flush()

        class BgAda:
            def __init__(self, l, j, sb, mps):
                self.l, self.j, self.mps = l, j, mps
                self.ring = [TL(sb("bga%d" % k, [128, KC, 128], BF16)) for k in range(4)]
                self.modraw = TL(sb("bgmodraw", [128, 48], F32))
                self.tmp16 = TL(sb("bgtmp16", [128, KC], F32))
                self.nl = 0
                self.nm = 0
                self._load(2)

            def _load(self, n):
                l, j = self.l, self.j
                for _ in range(n):
                    ch = self.nl
                    if ch >= 48:
                        return
                    wt = self.ring[ch % 4]
                    P.dma("sp", lambda e, wt=wt, ch=ch: e.dma_start(out=wt.t[:], in_=ada_b[l][j][ch]), reads=[ada_bufs[l][j][ch]], writes=wt.b)
                    self.nl += 1

            def step(self):
                for _ in range(2):
                    ch = self.nm
                    if ch >= 48:
                        return
                    wt = self.ring[ch % 4]
                    mm_group(P, self.mps.t[:, ch:ch + 1], [(wt.t[:, k, :], cact.t[:, k:k + 1]) for k in range(KC)], reads=wt.b + cact.b, writes=self.mps.b)
                    self.nm += 1
                self._load(2)

            def finish(self):
                while self.nm < 48:
                    self.step()
                l, j, mps, modraw, tmp16 = self.l, self.j, self.mps, self.modraw, self.tmp16
                c0 = vcol(l, "ada", 48 * j)
                P.op("dve", lambda e: e.tensor_tensor(out=modraw.t[:], in0=mps.t[:, 0:48], in1=vecs.t[:, c0:c0 + 48], op=ALU.add),
                     reads=mps.b + vecs.b, writes=modraw.b)
                sh = modraw.t[:, 0:KC]
                sc = modraw.t[:, KC:2 * KC]
                gt = modraw.t[:, 2 * KC:3 * KC]
                g0 = vcol(l, "ng", j * KC)
                a0 = mcol(l, 3 * j, 0)
                P.op("dve", lambda e: e.tensor_scalar(out=tmp16.t[:], in0=sc, scalar1=1.0, scalar2=None, op0=ALU.add), reads=modraw.b, writes=tmp16.b)
                P.op("dve", lambda e: e.tensor_tensor(out=modv.t[:, a0:a0 + KC], in0=tmp16.t[:], in1=vecs.t[:, g0:g0 + KC], op=ALU.mult),
                     reads=tmp16.b + vecs.b, writes=modv.b)
                P.op("dve", lambda e: e.tensor_copy(out=modv.t[:, a0 + KC:a0 + 2 * KC], in_=sh), reads=modraw.b, writes=modv.b)
                fac = 1.0 if j == 1 else 0.5
                P.op("dve", lambda e: e.tensor_scalar(out=modv.t[:, a0 + 2 * KC:a0 + 3 * KC], in0=gt, scalar1=fac, scalar2=None, op0=ALU.mult),
                     reads=modraw.b, writes=modv.b)

        stage_of = {0: "ffn1", 1: "mixer", 2: "ffn2"}
        pieces = [(l, j) for l in range(DEPTH) for j in range(3) if run(stage_of[j])]
        first_piece = pieces[0][1]

        with ExitStack() as ps:
            def sb(name, shape, dt):
                return ps.enter_context(nc.sbuf_tensor(name, list(shape), dt))

            P.dma("sp", lambda e: e.dma_start(out=vecs.t[:], in_=vecs_d[:, :]), writes=vecs.b)
            P.dma("sp", lambda e: e.dma_start(out=consts.t[:], in_=consts_d[:, :]), writes=consts.b)
            P.op("dve", lambda e: e.memset(ones_bf.t[:], 1.0), writes=ones_bf.b)
            P.op("dve", lambda e: e.memset(ones_f.t[:], 1.0), writes=ones_f.b)
            cin = TL(sb("cin", [128, KC], F32))
            P.dma("sp", lambda e: e.dma_start(out=cin.t[:], in_=cT[:, :]), writes=cin.b)
            P.op("act", lambda e: e.activation(out=cact.t[:], in_=cin.t[:], func=AF.Silu), reads=cin.b, writes=cact.b)
            if run("tables"):
                HS = S // 2
                pos_i = TL(sb("pos_i", [128, HS], I32))
                posf = TL(sb("posf", [128, HS], F32))
                u = TL(sb("u", [128, HS], F32))
                t1 = TL(sb("t1", [128, HS], F32))
                kk = TL(sb("kk", [128, HS], F32))
                tb = [TL(sb("tb%d" % i, [128, HS], F32)) for i in range(2)]
                n = 0
                for hf in range(2):
                    P.dma("sp", lambda e, hf=hf: e.dma_start(out=pos_i.t[:], in_=posb[:, hf * HS:(hf + 1) * HS]), writes=pos_i.b)
                    P.op("dve", lambda e: e.tensor_copy(out=posf.t[:], in_=pos_i.t[:]), reads=pos_i.b, writes=posf.b)
                    for fam in range(2):
                        for cs in range(2):
                            P.op("dve", lambda e, fam=fam, cs=cs: e.tensor_scalar(
                                out=u.t[:], in0=posf.t[:], scalar1=consts.t[:, fam:fam + 1], scalar2=INV2PI, op0=ALU.mult, op1=ALU.mult),
                                reads=posf.b + consts.b, writes=u.b)
                            if cs == 0:
                                P.op("dve", lambda e: e.tensor_scalar(out=u.t[:], in0=u.t[:], scalar1=0.25, scalar2=None, op0=ALU.add),
                                     reads=u.b, writes=u.b)
                            P.op("dve", lambda e: e.tensor_scalar(out=t1.t[:], in0=u.t[:], scalar1=MAGIC, scalar2=None, op0=ALU.add),
                                 reads=u.b, writes=t1.b)
                            P.op("dve", lambda e: e.tensor_scalar(out=kk.t[:], in0=t1.t[:], scalar1=MAGIC, scalar2=None, op0=ALU.subtract),
                                 reads=t1.b, writes=kk.b)
                            P.op("dve", lambda e: e.tensor_tensor(out=u.t[:], in0=u.t[:], in1=kk.t[:], op=ALU.subtract),
                                 reads=u.b + kk.b, writes=u.b)
                            ot = tb[n % 2]
                            if cs == 0:
                                P.op("act", lambda e, ot=ot: e.activation(out=ot.t[:], in_=u.t[:], func=AF.Sin, scale=TWOPI_S),
                                     reads=u.b, writes=ot.b)
                            else:
                                P.op("act", lambda e, ot=ot, fam=fam: e.activation(out=ot.t[:], in_=u.t[:], func=AF.Sin, scale=consts.t[:, 2 + fam:3 + fam]),
                                     reads=u.b + consts.b, writes=ot.b)
                            ti = fam * 2 + cs
                            bb = Buf()
                            P.dma("sp", lambda e, ot=ot, ti=ti, hf=hf: e.dma_start(out=tabs_d[ti][:, hf * HS:(hf + 1) * HS], in_=ot.t[:]), reads=ot.b, writes=[bb])
                            for k, v in bb.w.items():
                                tabs_buf.w[k] = max(tabs_buf.w.get(k, 0), v)
                            n += 1
            adaln_ops(0, first_piece, sb, ps)
            P.flush()

        class Env:
            pass

        phase_ctr = [0]

        def make_env(ps, nxr=6, ntt=4, nxo=3):
            phase_ctr[0] += 1
            pfx = "p%d_" % phase_ctr[0]

            def sb(name, shape, dt):
                return ps.enter_context(nc.sbuf_tensor(pfx + name, list(shape), dt))

            E = Env()
            E.sb = sb
            E.xr = Ring([TL(sb("xr%d" % i, [128, T], F32)) for i in range(nxr)])
            E.sq = Ring([TL(sb("sq%d" % i, [128, T], BF16)) for i in range(2)])
            E.tt = Ring([TL(sb("tt%d" % i, [128, T], F32)) for i in range(ntt)])
            E.xo = Ring([TL(sb("xo%d" % i, [128, T], F32)) for i in range(nxo)])
            E.lnv = TL(sb("lnv", [128, T], F32))
            E.rstd = TL(sb("rstd", [128, T], F32))
            E.psum = [TL(ps.enter_context(nc.psum_tensor(pfx + "ps%d" % i, [128, 512], F32))) for i in range(8)]
            return E

        def ld_x(E, src_ap, src_bufs, i, c):
            xs = E.xr.next()
            rb = [src_bufs[i][c]] if src_bufs is not None else []
            P.dma("sp", lambda e, xs=xs: e.dma_start(out=xs.t[:], in_=src_ap[c * 128:(c + 1) * 128, i * T:(i + 1) * T]),
                  reads=rb, writes=xs.b)
            return xs

        def rstd_from(E, ss, n_feat):
            P.op("act", lambda e: e.activation(out=E.lnv.t[:], in_=ss.t[:], func=AF.Ln, bias=EPS, scale=1.0 / n_feat),
                 reads=ss.b, writes=E.lnv.b)
            P.op("act", lambda e: e.activation(out=E.rstd.t[:], in_=E.lnv.t[:], func=AF.Exp, scale=-0.5),
                 reads=E.lnv.b, writes=E.rstd.b)

        class _XV:
            def __init__(self, ap, b):
                self.ap = ap
                self.b = b

        def emit_norm(E, i, src_ap, src_bufs, ss, hb, acol0, bcol0, xpre=None):
            if xpre is not None:
                for c in range(KC):
                    s = E.sq.next()
                    P.op("act", lambda e, c=c, s=s: e.activation(out=s.t[:], in_=xpre.t[:, c, :], func=AF.Square), reads=[xpre.b[c]], writes=s.b)
                    P.op("pe", lambda e, s=s, c=c: e.matmul(ss.t[:], ones_bf.t[:], s.t[:], start=(c == 0), stop=(c == KC - 1)),
                         reads=s.b + ones_bf.b, writes=ss.b)
                rstd_from(E, ss, D)
                for c in range(KC):
                    t = E.tt.next()
                    P.op("dve", lambda e, t=t, c=c: e.scalar_tensor_tensor(
                        out=t.t[:], in0=xpre.t[:, c, :], scalar=modv.t[:, acol0 + c:acol0 + c + 1], in1=E.rstd.t[:], op0=ALU.mult, op1=ALU.mult),
                        reads=[xpre.b[c]] + modv.b + E.rstd.b, writes=t.b)
                    P.op("act", lambda e, t=t, c=c: e.activation(out=hb.t[:, c, :], in_=t.t[:], func=AF.Identity,
                                                                 bias=modv.t[:, bcol0 + c:bcol0 + c + 1], scale=1.0),
                         reads=t.b + modv.b, writes=[hb.b[c]])
                return
            for c in range(KC):
                xs = ld_x(E, src_ap, src_bufs, i, c)
                s = E.sq.next()
                P.op("act", lambda e, xs=xs, s=s: e.activation(out=s.t[:], in_=xs.t[:], func=AF.Square), reads=xs.b, writes=s.b)
                P.op("pe", lambda e, s=s, c=c: e.matmul(ss.t[:], ones_bf.t[:], s.t[:], start=(c == 0), stop=(c == KC - 1)),
                     reads=s.b + ones_bf.b, writes=ss.b)
            rstd_from(E, ss, D)
            for c in range(KC):
                xs = ld_x(E, src_ap, src_bufs, i, c)
                t = E.tt.next()
                P.op("dve", lambda e, xs=xs, t=t, c=c: e.scalar_tensor_tensor(
                    out=t.t[:], in0=xs.t[:], scalar=modv.t[:, acol0 + c:acol0 + c + 1], in1=E.rstd.t[:], op0=ALU.mult, op1=ALU.mult),
                    reads=xs.b + modv.b + E.rstd.b, writes=t.b)
                P.op("act", lambda e, t=t, c=c: e.activation(out=hb.t[:, c, :], in_=t.t[:], func=AF.Identity,
                                                             bias=modv.t[:, bcol0 + c:bcol0 + c + 1], scale=1.0),
                     reads=t.b + modv.b, writes=[hb.b[c]])

        def ffn_phase(l, f, src_ap, src_bufs, bgp=None):
            with ExitStack() as ps:
                emit_next_conv()
                E = make_env(ps, nxr=4)
                sb = E.sb
                h = [TL(sb("h%d" % i, [128, KC, T], BF16), KC) for i in range(2)]
                a = TL(sb("a", [128, FC, T], BF16), FC)
                NW = 4
                wgu = [TL(sb("wgu%d" % i, [128, KC, 256], BF16)) for i in range(NW)]
                wd = [TL(sb("wd%d" % i, [128, FC, 128], BF16)) for i in range(NW)]
                ss = E.psum[0]
                pgu = [(E.psum[1], E.psum[2]), (E.psum[3], E.psum[4])]
                py = [E.psum[5], E.psum[6]] if bgp is not None else [E.psum[5], E.psum[6], E.psum[7]]
                bg = BgAda(bgp[0], bgp[1], sb, E.psum[7]) if bgp is not None else None
                jm = 3 * (0 if f == 0 else 2)
                acol0, bcol0, gcol0 = mcol(l, jm, 0), mcol(l, jm + 1, 0), mcol(l, jm + 2, 0)

                seq = []
                for i in range(NT):
                    seq += [("gu", t) for t in range(FC)] + [("d", r) for r in range(KC)]
                cnt = {"gu": 0, "d": 0}
                slot_of = {}

                def issue(u):
                    if u >= len(seq):
                        return
                    kind, idx = seq[u]
                    n = cnt[kind]
                    cnt[kind] += 1
                    if kind == "gu":
                        wt = wgu[n % NW]
                        P.dma("sp", lambda e, wt=wt, idx=idx: e.dma_start(out=wt.t[:], in_=gu_b[l][f][idx]),
                              reads=list(gu_bufs[l][f][idx]), writes=wt.b)
                    else:
                        wt = wd[n % NW]
                        P.dma("sp", lambda e, wt=wt, idx=idx: e.dma_start(out=wt.t[:], in_=d_b[l][f][idx]),
                              reads=[d_bufs[l][f][idx]], writes=wt.b)
                    slot_of[u] = wt

                LOOK = 3
                for u in range(LOOK):
                    issue(u)
                upos = [0]

                def next_w():
                    u = upos[0]
                    upos[0] += 1
                    issue(u + LOOK)
                    return slot_of.pop(u)

                emit_norm(E, 0, src_ap, src_bufs, ss, h[0], acol0, bcol0)
                npair = 0
                ny = 0
                for i in range(NT):
                    hb = h[i % 2]
                    for t in range(FC):
                        wt = next_w()
                        pg, pu = pgu[npair % 2]
                        npair += 1
                        mm_group(P, pg.t[:], [(wt.t[:, k, 0:128], hb.t[:, k, :]) for k in range(KC)], reads=wt.b + hb.b, writes=pg.b)
                        mm_group(P, pu.t[:], [(wt.t[:, k, 128:256], hb.t[:, k, :]) for k in range(KC)], reads=wt.b + hb.b, writes=pu.b)
                        sg = E.tt.next()
                        P.op("act", lambda e, sg=sg, pg=pg: e.activation(out=sg.t[:], in_=pg.t[:], func=AF.Silu), reads=pg.b, writes=sg.b)
                        P.op("dve", lambda e, sg=sg, pu=pu, t=t: e.tensor_tensor(out=a.t[:, t, :], in0=sg.t[:], in1=pu.t[:], op=ALU.mult),
                             reads=sg.b + pu.b, writes=[a.b[t]])
                        if bg is not None and t in (10, 25, 40):
                            bg.step()
                    if i + 1 < NT:
                        emit_norm(E, i + 1, src_ap, src_bufs, ss, h[(i + 1) % 2], acol0, bcol0)
                    for r in range(KC):
                        wt = next_w()
                        yp = py[ny % len(py)]
                        ny += 1
                        mm_group(P, yp.t[:], [(wt.t[:, k, :], a.t[:, k, :]) for k in range(FC)], reads=wt.b + a.b, writes=yp.b)
                        xs = ld_x(E, src_ap, src_bufs, i, r)
                        xo = E.xo.next()
                        P.op("dve", lambda e, xo=xo, yp=yp, xs=xs, r=r: e.scalar_tensor_tensor(
                            out=xo.t[:], in0=yp.t[:], scalar=modv.t[:, gcol0 + r:gcol0 + r + 1], in1=xs.t[:], op0=ALU.mult, op1=ALU.add),
                            reads=yp.b + xs.b + modv.b, writes=xo.b)
                        P.dma("sp", lambda e, xo=xo, r=r, i=i: e.dma_start(out=xres[r * 128:(r + 1) * 128, i * T:(i + 1) * T], in_=xo.t[:]),
                              reads=xo.b, writes=[xres_b[i][r]])
                if bg is not None:
                    bg.finish()
                P.flush()

        def final_phase(src_ap, src_bufs):
            out_bufs_l = []
            with ExitStack() as ps:
                E = make_env(ps)
                ss = E.psum[0]
                xt = TL(E.sb("xt", [128, KC, T], F32), KC)
                for i in range(NT):
                    for c in range(KC):
                        rb = [src_bufs[i][c]] if src_bufs is not None else []
                        P.dma("sp", lambda e, i=i, c=c: e.dma_start(out=xt.t[:, c, :], in_=src_ap[c * 128:(c + 1) * 128, i * T:(i + 1) * T]),
                              reads=rb, writes=[xt.b[c]])
                        s = E.sq.next()
                        P.op("act", lambda e, c=c, s=s: e.activation(out=s.t[:], in_=xt.t[:, c, :], func=AF.Square), reads=[xt.b[c]], writes=s.b)
                        P.op("pe", lambda e, s=s, c=c: e.matmul(ss.t[:], ones_bf.t[:], s.t[:], start=(c == 0), stop=(c == KC - 1)),
                             reads=s.b + ones_bf.b, writes=ss.b)
                    rstd_from(E, ss, D)
                    for c in range(KC):
                        xo = E.xo.next()
                        P.op("dve", lambda e, xo=xo, c=c: e.scalar_tensor_tensor(
                            out=xo.t[:], in0=xt.t[:, c, :], scalar=vecs.t[:, GFIN + c:GFIN + c + 1], in1=E.rstd.t[:], op0=ALU.mult, op1=ALU.mult),
                            reads=[xt.b[c]] + vecs.b + E.rstd.b, writes=xo.b)
                        ob = Buf()
                        out_bufs_l.append(ob)
                        P.dma("sp", lambda e, xo=xo, c=c, i=i: e.dma_start(out=outT[c * 128:(c + 1) * 128, i * T:(i + 1) * T], in_=xo.t[:]),
                              reads=xo.b, writes=[ob])
                P.flush()

        def rope3(E, p1, p2, tc, tsn, out_ap, out_bufs):
            t1 = E.tt.next()
            t2 = E.tt.next()
            P.op("dve", lambda e: e.tensor_tensor(out=t1.t[:], in0=p1.t[:], in1=tc.t[:], op=ALU.mult), reads=p1.b + tc.b, writes=t1.b)
            P.op("dve", lambda e: e.tensor_tensor(out=t2.t[:], in0=p2.t[:], in1=tsn.t[:], op=ALU.mult), reads=p2.b + tsn.b, writes=t2.b)
            P.op("dve", lambda e: e.tensor_tensor(out=out_ap, in0=t1.t[:], in1=t2.t[:], op=ALU.add), reads=t1.b + t2.b, writes=out_bufs)

        def load_tabs(tabs, i):
            for k in range(4):
                P.dma("sp", lambda e, k=k: e.dma_start(out=tabs[k].t[:], in_=tabs_d[k][:, i * T:(i + 1) * T]),
                      reads=[tabs_buf], writes=tabs[k].b)

        def mixer_pass_a(l, src_ap, src_bufs, bgp=None):
            with ExitStack() as ps:
                emit_next_conv()
                E = make_env(ps, nxr=4, ntt=4, nxo=1)
                sb = E.sb
                hh = [TL(sb("h%d" % k_, [128, KC, T], BF16), KC) for k_ in range(2)]
                win = {}
                for j in [IN_KA, IN_KA + 1, IN_KASW, IN_KASW + 1, IN_CKV, IN_CKV + 1, IN_CKV + 2, IN_CKV + 3, IN_KR, IN_KRSW]:
                    w = TL(sb("win%d" % j, [128, KC, 128], BF16))
                    P.dma("sp", lambda e, w=w, j=j: e.dma_start(out=w.t[:], in_=in_b[l][j]), reads=[in_bufs[l][j]], writes=w.b)
                    win[j] = w
                wva = TL(sb("wva", [128, KC, 256], BF16))
                P.dma("sp", lambda e: e.dma_start(out=wva.t[:], in_=va_b[l]), reads=[va_bufs[l]], writes=wva.b)
                wkn = []
                for hd in range(8):
                    w = TL(sb("wkn%d" % hd, [128, 4, 128], BF16))
                    P.dma("sp", lambda e, w=w, hd=hd: e.dma_start(out=w.t[:], in_=kn_b[l][hd]), reads=[kn_bufs[l][hd]], writes=w.b)
                    wkn.append(w)
                wvb = TL(sb("wvb", [128, 4, 1024], BF16))
                P.dma("sp", lambda e: e.dma_start(out=wvb.t[:], in_=vbw_b[l]), reads=[vbw_bufs[l]], writes=wvb.b)
                tabs = [TL(sb("tab%d" % k, [128, T], F32)) for k in range(4)]
                ckvf = TL(sb("ckvf", [128, 4, T], F32), 4)
                ckvn = TL(sb("ckvn", [128, 4, T], BF16), 4)
                stg = Ring([TL(sb("stg%d" % k, [128, T], BF16)) for k in range(4)])
                vstg = Ring([TL(sb("vstg%d" % k, [128, 1024], BF16), 2) for k in range(2)])
                ss = E.psum[0]
                pr = Ring(E.psum[1:7]) if bgp is not None else Ring(E.psum[1:8])
                bg = BgAda(bgp[0], bgp[1], sb, E.psum[7]) if bgp is not None else None
                acol0, bcol0 = mcol(l, 3, 0), mcol(l, 4, 0)
                xset = TL(sb("xset", [128, KC, T], F32), KC)

                def load_xset(i_):
                    for c in range(KC):
                        rb = [src_bufs[i_][c]] if src_bufs is not None else []
                        P.dma("sp", lambda e, c=c, i_=i_: e.dma_start(out=xset.t[:, c, :], in_=src_ap[c * 128:(c + 1) * 128, i_ * T:(i_ + 1) * T]),
                              reads=rb, writes=[xset.b[c]])

                load_xset(0)
                emit_norm(E, 0, src_ap, src_bufs, ss, hh[0], acol0, bcol0, xpre=xset)
                for i in range(NT):
                    h = hh[i % 2]
                    if i + 1 < NT:
                        load_xset(i + 1)
                    load_tabs(tabs, i)

                    def proj(j, h=h):
                        p = pr.next()
                        w = win[j]
                        mm_group(P, p.t[:], [(w.t[:, k, :], h.t[:, k, :]) for k in range(KC)], reads=w.b + h.b, writes=p.b)
                        return p

                    for kv in range(2):
                        p1 = proj(IN_KA + kv)
                        p2 = proj(IN_KASW + kv)
                        o = stg.next()
                        rope3(E, p1, p2, tabs[0], tabs[1], o.t[:], o.b)
                        P.dma("sp", lambda e, o=o, kv=kv, i=i: e.dma_start(out=kaT_d[kv][:, i * T:(i + 1) * T], in_=o.t[:]),
                              reads=o.b, writes=[kaT_buf[i][kv]])
                    p1 = proj(IN_KR)
                    p2 = proj(IN_KRSW)
                    o = stg.next()
                    rope3(E, p1, p2, tabs[2], tabs[3], o.t[:], o.b)
                    P.dma("sp", lambda e, o=o, i=i: e.dma_start(out=krT_d[:, i * T:(i + 1) * T], in_=o.t[:]), reads=o.b, writes=[krT_buf[i]])
                    for j in range(4):
                        p = pr.next()
                        mm_group(P, p.t[:, 0:256], [(h.t[:, k, j * 128:(j + 1) * 128], wva.t[:, k, :]) for k in range(KC)],
                                 reads=wva.b + h.b, writes=p.b)
                        o = vstg.next()
                        P.op("act", lambda e, o=o, p=p: e.activation(out=o.t[:, 0:256], in_=p.t[:, 0:256], func=AF.Identity),
                             reads=p.b, writes=o.b)
                        P.dma("sp", lambda e, o=o, i=i, j=j: e.dma_start(out=va_d[4 * i + j], in_=o.t[:, 0:256]), reads=o.b, writes=[va_buf[i][j]])
                    for j in range(4):
                        p = proj(IN_CKV + j)
                        P.op("act", lambda e, p=p, j=j: e.activation(out=ckvf.t[:, j, :], in_=p.t[:], func=AF.Identity), reads=p.b, writes=[ckvf.b[j]])
                        s = E.sq.next()
                        P.op("act", lambda e, s=s, j=j: e.activation(out=s.t[:], in_=ckvf.t[:, j, :], func=AF.Square), reads=[ckvf.b[j]], writes=s.b)
                        P.op("pe", lambda e, s=s, j=j: e.matmul(ss.t[:], ones_bf.t[:], s.t[:], start=(j == 0), stop=(j == 3)),
                             reads=s.b + ones_bf.b, writes=ss.b)
                    rstd_from(E, ss, 512)
                    for j in range(4):
                        gc = vcol(l, "gckv", j)
                        P.op("dve", lambda e, j=j, gc=gc: e.scalar_tensor_tensor(
                            out=ckvn.t[:, j, :], in0=ckvf.t[:, j, :], scalar=vecs.t[:, gc:gc + 1], in1=E.rstd.t[:], op0=ALU.mult, op1=ALU.mult),
                            reads=[ckvf.b[j]] + vecs.b + E.rstd.b, writes=[ckvn.b[j]])
                    if bg is not None:
                        bg.step()
                        bg.step()
                    if i + 1 < NT:
                        emit_norm(E, i + 1, src_ap, src_bufs, ss, hh[(i + 1) % 2], acol0, bcol0, xpre=xset)
                    if bg is not None:
                        bg.step()
                    for hd in range(8):
                        p = pr.next()
                        mm_group(P, p.t[:], [(wkn[hd].t[:, k, :], ckvn.t[:, k, :]) for k in range(4)], reads=wkn[hd].b + ckvn.b, writes=p.b)
                        o = stg.next()
                        P.op("act", lambda e, o=o, p=p: e.activation(out=o.t[:], in_=p.t[:], func=AF.Identity), reads=p.b, writes=o.b)
                        P.dma("sp", lambda e, o=o, hd=hd, i=i: e.dma_start(out=knT_d[hd][:, i * T:(i + 1) * T], in_=o.t[:]),
                              reads=o.b, writes=[knT_buf[i][hd]])
                    for j in range(4):
                        o = vstg.next()
                        for half in range(2):
                            p = pr.next()
                            mm_group(P, p.t[:], [(ckvn.t[:, k, j * 128:(j + 1) * 128], wvb.t[:, k, half * 512:(half + 1) * 512]) for k in range(4)],
                                     reads=wvb.b + ckvn.b, writes=p.b)
                            P.op("act", lambda e, o=o, p=p, half=half: e.activation(out=o.t[:, half * 512:(half + 1) * 512], in_=p.t[:], func=AF.Identity),
                                 reads=p.b, writes=[o.b[half]])
                        P.dma("sp", lambda e, o=o, i=i, j=j: e.dma_start(
                            out=vb_d[:, :, 4 * i + j, :].rearrange("h p d -> p h d"), in_=o.t[:].rearrange("p (h d) -> p h d", h=8)),
                            reads=o.b, writes=[vb_buf[i][j]])
                if bg is not None:
                    bg.finish()
                P.flush()

        def mixer_pass_b(l, src_ap, src_bufs):
            with ExitStack() as ps:
                E = make_env(ps, nxr=4, ntt=5, nxo=2)
                sb = E.sb
                hh = [TL(sb("h%d" % k_, [128, KC, T], BF16), KC) for k_ in range(2)]
                accE = TL(sb("accE", [128, T], F32))
                accO = TL(sb("accO", [128, T], F32))
                dhi = TL(sb("dhi", [128, T], BF16))
                dlo = TL(sb("dlo", [128, T], BF16))
                rlnv = TL(sb("rlnv", [128, T], F32))
                rrec = TL(sb("rrec", [128, T], F32))
                tabs = [TL(sb("tab%d" % k, [128, T], F32)) for k in range(4)]
                scr = TL(sb("scr", [128, KC, T], BF16), KC)
                cqf = TL(sb("cqf", [128, 4, T], F32), 4)
                cqn = TL(sb("cqn", [128, 4, T], BF16), 4)
                qn = TL(sb("qn", [128, 8, T], BF16), 8)
                qrp = TL(sb("qrp", [128, 8, T], BF16), 8)
                Oa = TL(sb("Oa", [128, 8, T], BF16), 8)
                Ob = TL(sb("Ob", [128, 8, T], BF16), 8)
                kawin = TL(sb("kawin", [128, 2, 768], BF16), 2)
                vawin = TL(sb("vawin", [128, 6, 256], BF16))
                krT = TL(sb("krT", [128, S], BF16))
                NKV = 4
                knq = [TL(sb("knq%d" % k, [128, 1024], BF16)) for k in range(NKV)]
                vbq = [TL(sb("vbq%d" % k, [128, 8, 128], BF16)) for k in range(NKV)]
                NWR = 4
                wring = [TL(sb("wr%d" % k, [128, KC, 128], BF16)) for k in range(NWR)]
                maskb = TL(sb("maskb", [128, 2, 512], BF16))
                ss = E.psum[0]
                pr = Ring(E.psum[1:5] + [E.psum[7]])
                Xacc = E.psum[5]
                prw = Ring(E.psum[1:4])
                wacc = [(E.psum[6], E.psum[7]), (E.psum[4], E.psum[5])]
                ptrw = Ring([TL(sb("ptw%d" % k, [128, T], BF16)) for k in range(7)])
                ptr = Ring(ptrw.t[0:7])
                pr7 = Ring(E.psum[1:8])
                Oacc, Dacc = E.psum[6], E.psum[7]
                acol0, bcol0, gcol0 = mcol(l, 3, 0), mcol(l, 4, 0), mcol(l, 5, 0)

                P.dma("sp", lambda e: e.dma_start(out=krT.t[:], in_=krT_d[:, :]), reads=list(krT_buf), writes=krT.b)
                P.op("dve", lambda e: e.memset(qrp.t[:], 0.0), writes=qrp.b)
                for which in range(2):
                    mt = E.tt.next()
                    P.dma("sp", lambda e, mt=mt, which=which: e.dma_start(out=mt.t[:], in_=masks_d[:, which, :]), writes=mt.b)
                    P.op("dve", lambda e, mt=mt, which=which: e.tensor_copy(out=maskb.t[:, which, :], in_=mt.t[:]), reads=mt.b, writes=maskb.b)
                es_hi = TL(sb("es_hi", [1, 1024], BF16), 2)
                es_lo = TL(sb("es_lo", [1, 1024], BF16), 2)
                for kv in range(2):
                    stg_ = E.lnv if kv == 0 else E.rstd
                    tmp_ = E.tt.next()
                    c0_ = l * 1024 + kv * 512
                    P.dma("sp", lambda e, stg_=stg_, c0_=c0_: e.dma_start(out=stg_.t[0:1, :], in_=sinkrow_d[0:1, c0_:c0_ + 512]), writes=stg_.b)
                    P.op("act", lambda e, stg_=stg_: e.activation(out=stg_.t[0:1, :], in_=stg_.t[0:1, :], func=AF.Exp), reads=stg_.b, writes=stg_.b)
                    P.op("dve", lambda e, stg_=stg_, kv=kv: e.tensor_copy(out=es_hi.t[0:1, kv * 512:(kv + 1) * 512], in_=stg_.t[0:1, :]),
                         reads=stg_.b, writes=[es_hi.b[kv]])
                    P.op("dve", lambda e, stg_=stg_, tmp_=tmp_, kv=kv: e.tensor_tensor(out=tmp_.t[0:1, :], in0=stg_.t[0:1, :], in1=es_hi.t[0:1, kv * 512:(kv + 1) * 512], op=ALU.subtract),
                         reads=stg_.b + [es_hi.b[kv]], writes=tmp_.b)
                    P.op("dve", lambda e, tmp_=tmp_, kv=kv: e.tensor_copy(out=es_lo.t[0:1, kv * 512:(kv + 1) * 512], in_=tmp_.t[0:1, :]),
                         reads=tmp_.b, writes=[es_lo.b[kv]])

                wseq = []
                for i in range(NT):
                    for hd in range(8):
                        wseq.append((in_b[l][IN_QA + hd], in_bufs[l][IN_QA + hd], KC))
                        wseq.append((in_b[l][IN_QASW + hd], in_bufs[l][IN_QASW + hd], KC))
                    for j in range(4):
                        wseq.append((in_b[l][IN_CQ + j], in_bufs[l][IN_CQ + j], KC))
                    for hd in range(8):
                        wseq.append((uq_b[l][UQ_QN + hd], uq_bufs[l][UQ_QN + hd], 4))
                    for jj in range(4):
                        wseq.append((uq_b[l][UQ_QR + jj], uq_bufs[l][UQ_QR + jj], 4))
                        wseq.append((uq_b[l][UQ_QRSW + jj], uq_bufs[l][UQ_QRSW + jj], 4))
                    for c in range(KC):
                        wseq.append((in_b[l][IN_GATE + c], in_bufs[l][IN_GATE + c], KC))
                        wseq.append((in_b[l][IN_GATE + KC + c], in_bufs[l][IN_GATE + KC + c], KC))
                        wseq.append((oa_b[l][c], oa_bufs[l][c], 8))
                        wseq.append((ob_b[l][c], ob_bufs[l][c], 8))
                    for c in range(KC):
                        wseq.append((out_b[l][c], out_bufs[l][c], KC))
                wslot = {}
                wpos = [0]
                LOOKW = NWR - 1

                def wissue(u):
                    if u >= len(wseq):
                        return
                    ap, bf, kc = wseq[u]
                    wt = wring[u % NWR]
                    P.dma("sp", lambda e, wt=wt, ap=ap, kc=kc: e.dma_start(out=wt.t[:, 0:kc, :], in_=ap), reads=[bf], writes=wt.b)
                    wslot[u] = wt

                for u in range(LOOKW):
                    wissue(u)

                def next_w():
                    u = wpos[0]
                    wpos[0] += 1
                    wissue(u + LOOKW)
                    return wslot.pop(u)

                kvseq = [(i, hd, q4) for i in range(NT) for hd in range(8) for q4 in range(4)]
                kvslot = {}
                kvpos = [0]
                LOOKKV = NKV - 2

                def kvissue(u):
                    if u >= len(kvseq):
                        return
                    _, hd, q4 = kvseq[u]
                    kt = knq[u % NKV]
                    vt = vbq[u % NKV]
                    P.dma("sp", lambda e, kt=kt, hd=hd, q4=q4: e.dma_start(out=kt.t[:], in_=knT_d[hd][:, q4 * 1024:(q4 + 1) * 1024]),
                          reads=[knT_buf[2 * q4][hd], knT_buf[2 * q4 + 1][hd]], writes=kt.b)
                    P.dma("sp", lambda e, vt=vt, hd=hd, q4=q4: e.dma_start(out=vt.t[:], in_=vb_d[hd][:, 8 * q4:8 * q4 + 8, :]),
                          reads=vb_buf[2 * q4] + vb_buf[2 * q4 + 1], writes=vt.b)
                    kvslot[u] = (kt, vt)

                for u in range(LOOKKV):
                    kvissue(u)

                def next_kv():
                    u = kvpos[0]
                    kvpos[0] += 1
                    kvissue(u + LOOKKV)
                    return kvslot.pop(u)

                def rden_from(dacc):
                    P.op("act", lambda e: e.activation(out=rlnv.t[:], in_=dacc.t[:], func=AF.Ln), reads=dacc.b, writes=rlnv.b)
                    P.op("act", lambda e: e.activation(out=rrec.t[:], in_=rlnv.t[:], func=AF.Exp, scale=-1.0), reads=rlnv.b, writes=rrec.b)

                emit_norm(E, 0, src_ap, src_bufs, ss, hh[0], acol0, bcol0)
                for i in range(NT):
                    h = hh[i % 2]
                    load_tabs(tabs, i)
                    kb0 = max(4 * i - 1, 0)
                    kb1 = min(4 * i + 4, 31)
                    nkb = kb1 - kb0 + 1
                    tl_ = sorted(set(kb // 4 for kb in range(kb0, kb1 + 1)))
                    for kv in range(2):
                        P.dma("sp", lambda e, kv=kv, kb0=kb0, nkb=nkb: e.dma_start(out=kawin.t[:, kv, 0:nkb * 128], in_=kaT_d[kv][:, kb0 * 128:(kb0 + nkb) * 128]),
                              reads=[kaT_buf[t_][kv] for t_ in tl_], writes=[kawin.b[kv]])
                    P.dma("sp", lambda e, kb0=kb0, nkb=nkb: e.dma_start(out=vawin.t[:, 0:nkb, :], in_=va_d[kb0:kb0 + nkb].rearrange("b p d -> p b d")),
                          reads=[b_ for t_ in tl_ for b_ in va_buf[t_]], writes=vawin.b)

                    def projw(w, h=h):
                        p = pr.next()
                        mm_group(P, p.t[:], [(w.t[:, k, :], h.t[:, k, :]) for k in range(KC)], reads=w.b + h.b, writes=p.b)
                        return p

                    for hd in range(8):
                        p1 = projw(next_w())
                        p2 = projw(next_w())
                        rope3(E, p1, p2, tabs[0], tabs[1], scr.t[:, hd, :], [scr.b[hd]])
                    for j in range(4):
                        p = projw(next_w())
                        P.op("act", lambda e, p=p, j=j: e.activation(out=cqf.t[:, j, :], in_=p.t[:], func=AF.Identity), reads=p.b, writes=[cqf.b[j]])
                        s = E.sq.next()
                        P.op("act", lambda e, s=s, j=j: e.activation(out=s.t[:], in_=cqf.t[:, j, :], func=AF.Square), reads=[cqf.b[j]], writes=s.b)
                        P.op("pe", lambda e, s=s, j=j: e.matmul(ss.t[:], ones_bf.t[:], s.t[:], start=(j == 0), stop=(j == 3)),
                             reads=s.b + ones_bf.b, writes=ss.b)
                    rstd_from(E, ss, 512)
                    for j in range(4):
                        gc = vcol(l, "gcq", j)
                        P.op("dve", lambda e, j=j, gc=gc: e.scalar_tensor_tensor(
                            out=cqn.t[:, j, :], in0=cqf.t[:, j, :], scalar=vecs.t[:, gc:gc + 1], in1=E.rstd.t[:], op0=ALU.mult, op1=ALU.mult),
                            reads=[cqf.b[j]] + vecs.b + E.rstd.b, writes=[cqn.b[j]])
                    for hd in range(8):
                        p = pr.next()
                        w = next_w()
                        mm_group(P, p.t[:], [(w.t[:, k, :], cqn.t[:, k, :]) for k in range(4)], reads=w.b + cqn.b, writes=p.b)
                        P.op("act", lambda e, p=p, hd=hd: e.activation(out=qn.t[:, hd, :], in_=p.t[:], func=AF.Identity), reads=p.b, writes=[qn.b[hd]])
                    for jj in range(4):
                        p1 = pr.next()
                        w = next_w()
                        mm_group(P, p1.t[:], [(w.t[:, k, :], cqn.t[:, k, :]) for k in range(4)], reads=w.b + cqn.b, writes=p1.b)
                        p2 = pr.next()
                        w = next_w()
                        mm_group(P, p2.t[:], [(w.t[:, k, :], cqn.t[:, k, :]) for k in range(4)], reads=w.b + cqn.b, writes=p2.b)
                        t1_ = E.tt.next()
                        t2_ = E.tt.next()
                        P.op("dve", lambda e, t1_=t1_, p1=p1: e.tensor_tensor(out=t1_.t[:], in0=p1.t[:], in1=tabs[2].t[:], op=ALU.mult), reads=p1.b + tabs[2].b, writes=t1_.b)
                        P.op("dve", lambda e, t2_=t2_, p2=p2: e.tensor_tensor(out=t2_.t[:], in0=p2.t[:], in1=tabs[3].t[:], op=ALU.mult), reads=p2.b + tabs[3].b, writes=t2_.b)
                        for par in range(2):
                            lo_ = 64 * par
                            P.op("dve", lambda e, t1_=t1_, t2_=t2_, lo_=lo_, jj=jj, par=par: e.tensor_tensor(
                                out=qrp.t[lo_:lo_ + 64, 2 * jj + par, :], in0=t1_.t[lo_:lo_ + 64, :], in1=t2_.t[lo_:lo_ + 64, :], op=ALU.add),
                                reads=t1_.b + t2_.b, writes=[qrp.b[2 * jj + par]])

                    combos = [(kv, j) for kv in range(2) for j in range(4)] if (DBG_B is None or "win" in DBG_B) else []

                    def win_front(k):
                        kv, j = combos[k]
                        n = 4 * i + j
                        outl = []
                        for kb in [kb for kb in (n - 1, n, n + 1) if 0 <= kb <= 31]:
                            sp_ = prw.next()
                            ko = (kb - kb0) * 128
                            P.op("pe", lambda e, sp_=sp_, kv=kv, ko=ko, j=j: e.matmul(
                                sp_.t[:].rearrange("p (g q) -> p g q", g=4), kawin.t[:, kv, ko:ko + 128],
                                scr.t[:, 4 * kv:4 * kv + 4, j * 128:(j + 1) * 128], start=True, stop=True),
                                reads=[kawin.b[kv]] + scr.b[4 * kv:4 * kv + 4], writes=sp_.b)
                            pt = ptrw.next()
                            P.op("act", lambda e, pt=pt, sp_=sp_: e.activation(out=pt.t[:], in_=sp_.t[:], func=AF.Exp, scale=SC_A), reads=sp_.b, writes=pt.b)
                            if kb != n:
                                which = 0 if kb == n - 1 else 1
                                P.op("dve", lambda e, pt=pt, which=which: e.tensor_tensor(out=pt.t[:], in0=pt.t[:], in1=maskb.t[:, which, :], op=ALU.mult),
                                     reads=pt.b + maskb.b, writes=pt.b)
                            outl.append((kb, pt))
                        return outl

                    def win_back(k, pts):
                        kv, j = combos[k]
                        Oa_, Da_ = wacc[k % 2]
                        nk = len(pts)
                        for idx, (kb, pt) in enumerate(pts):
                            vo = kb - kb0
                            P.op("pe", lambda e, pt=pt, vo=vo, kv=kv, idx=idx, nk=nk: e.matmul(
                                Oa_.t[:], vawin.t[:, vo, kv * 128:(kv + 1) * 128], pt.t[:], start=(idx == 0), stop=(idx == nk - 1)),
                                reads=pt.b + vawin.b, writes=Oa_.b)
                            P.op("pe", lambda e, pt=pt, idx=idx: e.matmul(Da_.t[:], ones_bf.t[:], pt.t[:], start=(idx == 0), stop=False),
                                 reads=pt.b + ones_bf.b, writes=Da_.b)
                        P.op("pe", lambda e, kv=kv: e.matmul(Da_.t[:], ones_bf.t[0:1, :], es_hi.t[0:1, kv * 512:(kv + 1) * 512], start=False, stop=False),
                             reads=[es_hi.b[kv]] + ones_bf.b, writes=Da_.b)
                        P.op("pe", lambda e, kv=kv: e.matmul(Da_.t[:], ones_bf.t[0:1, :], es_lo.t[0:1, kv * 512:(kv + 1) * 512], start=False, stop=True),
                             reads=[es_lo.b[kv]] + ones_bf.b, writes=Da_.b)
                        rden_from(Da_)
                        P.op("dve", lambda e, kv=kv, j=j: e.tensor_tensor(
                            out=Oa.t[:, 4 * kv:4 * kv + 4, j * 128:(j + 1) * 128], in0=Oa_.t[:].rearrange("p (g q) -> p g q", g=4),
                            in1=rrec.t[:].rearrange("p (g q) -> p g q", g=4), op=ALU.mult),
                            reads=Oa_.b + rrec.b, writes=Oa.b[4 * kv:4 * kv + 4])

                    if combos:
                        front = win_front(0)
                        for k in range(len(combos)):
                            nxt_ = win_front(k + 1) if k + 1 < len(combos) else None
                            win_back(k, front)
                            front = nxt_

                    for hd in (range(MLA_HEADS) if (DBG_B is None or "mla" in DBG_B) else []):
                        ph = 64 * (hd % 2)
                        slots = [next_kv() for _ in range(1)]
                        cur = {"q4": 0, "kv": slots[0]}

                        def emit_s(kb, hd=hd, ph=ph):
                            q4, kbl = kb // 8, kb % 8
                            if q4 != cur["q4"]:
                                cur["q4"] = q4
                                cur["kv"] = next_kv()
                            kt, vt = cur["kv"]
                            sp_ = pr.next()
                            mm_group(P, sp_.t[:], [(kt.t[:, kbl * 128:(kbl + 1) * 128], qn.t[:, hd, :]),
                                                   (krT.t[:, kb * 128:(kb + 1) * 128], qrp.t[:, hd, :])],
                                     reads=kt.b + [qn.b[hd], qrp.b[hd]] + krT.b, writes=sp_.b)
                            return sp_, vt, kbl

                        sq_ = [emit_s(0), emit_s(1), emit_s(2)]
                        for kb in range(32):
                            sp_, vt, kbl = sq_.pop(0)
                            if kb + 3 < 32:
                                sq_.append(emit_s(kb + 3))
                            pt = ptr.next()
                            P.op("act", lambda e, pt=pt, sp_=sp_: e.activation(out=pt.t[:], in_=sp_.t[:], func=AF.Exp, scale=SC_B), reads=sp_.b, writes=pt.b)
                            P.op("pe", lambda e, pt=pt, vt=vt, kbl=kbl, kb=kb: e.matmul(Oacc.t[:], vt.t[:, kbl, :], pt.t[:], start=(kb == 0), stop=(kb == 31)),
                                 reads=pt.b + vt.b, writes=Oacc.b)
                            if kb % 2 == 1:
                                ae_, an_, first_ = accO, "pool", (kb == 1)
                            else:
                                ae_, an_, first_ = Xacc, "dve", (kb == 0)
                            if first_:
                                P.op(an_, lambda e, ae_=ae_, pt=pt: e.tensor_copy(out=ae_.t[:], in_=pt.t[:]), reads=pt.b, writes=ae_.b)
                            else:
                                P.op(an_, lambda e, ae_=ae_, pt=pt: e.tensor_tensor(out=ae_.t[:], in0=ae_.t[:], in1=pt.t[:], op=ALU.add),
                                     reads=pt.b + ae_.b, writes=ae_.b)
                        P.op("dve", lambda e: e.tensor_tensor(out=accE.t[:], in0=Xacc.t[:], in1=accO.t[:], op=ALU.add), reads=Xacc.b + accO.b, writes=accE.b)
                        P.op("dve", lambda e: e.tensor_copy(out=dhi.t[:], in_=accE.t[:]), reads=accE.b, writes=dhi.b)
                        P.op("dve", lambda e: e.tensor_tensor(out=accO.t[:], in0=accE.t[:], in1=dhi.t[:], op=ALU.subtract), reads=accE.b + dhi.b, writes=accO.b)
                        P.op("dve", lambda e: e.tensor_copy(out=dlo.t[:], in_=accO.t[:]), reads=accO.b, writes=dlo.b)
                        dps_ = pr.next()
                        mm_group(P, dps_.t[:], [(ones_bf.t[:], dhi.t[:]), (ones_bf.t[:], dlo.t[:])], reads=dhi.b + dlo.b + ones_bf.b, writes=dps_.b)
                        rden_from(dps_)
                        P.op("dve", lambda e, hd=hd: e.tensor_tensor(out=Ob.t[:, hd, :], in0=Oacc.t[:], in1=rrec.t[:], op=ALU.mult),
                             reads=Oacc.b + rrec.b, writes=[Ob.b[hd]])
                        if hd == 3 and i + 1 < NT:
                            emit_norm(E, i + 1, src_ap, src_bufs, ss, hh[(i + 1) % 2], acol0, bcol0)

                    pr_save = pr
                    pr = pr7
                    for c in (range(KC) if (DBG_B is None or "merge" in DBG_B) else []):
                        pga = projw(next_w())
                        pgb = projw(next_w())
                        wa_ = next_w()
                        pya = pr.next()
                        mm_group(P, pya.t[:], [(wa_.t[:, k, :], Oa.t[:, k, :]) for k in range(8)], reads=wa_.b + Oa.b, writes=pya.b)
                        wb_ = next_w()
                        pyb = pr.next()
                        mm_group(P, pyb.t[:], [(wb_.t[:, k, :], Ob.t[:, k, :]) for k in range(8)], reads=wb_.b + Ob.b, writes=pyb.b)
                        ga = E.tt.next()
                        gb = E.tt.next()
                        bga = vcol(l, "bg", c)
                        bgb = vcol(l, "bg", KC + c)
                        P.op("act", lambda e, ga=ga, pga=pga, bga=bga: e.activation(out=ga.t[:], in_=pga.t[:], func=AF.Sigmoid, bias=vecs.t[:, bga:bga + 1], scale=1.0),
                             reads=pga.b + vecs.b, writes=ga.b)
                        P.op("act", lambda e, gb=gb, pgb=pgb, bgb=bgb: e.activation(out=gb.t[:], in_=pgb.t[:], func=AF.Sigmoid, bias=vecs.t[:, bgb:bgb + 1], scale=1.0),
                             reads=pgb.b + vecs.b, writes=gb.b)
                        P.op("dve", lambda e, ga=ga, pya=pya: e.tensor_tensor(out=ga.t[:], in0=pya.t[:], in1=ga.t[:], op=ALU.mult), reads=pya.b + ga.b, writes=ga.b)
                        P.op("dve", lambda e, gb=gb, pyb=pyb: e.tensor_tensor(out=gb.t[:], in0=pyb.t[:], in1=gb.t[:], op=ALU.mult), reads=pyb.b + gb.b, writes=gb.b)
                        P.op("dve", lambda e, ga=ga, gb=gb, c=c: e.tensor_tensor(out=scr.t[:, c, :], in0=ga.t[:], in1=gb.t[:], op=ALU.add),
                             reads=ga.b + gb.b, writes=[scr.b[c]])
                    xs_q = [ld_x(E, src_ap, src_bufs, i, c) for c in range(3)]
                    for c in (range(KC) if (DBG_B is None or "out" in DBG_B) else []):
                        wo_ = next_w()
                        if c + 3 < KC:
                            xs_q.append(ld_x(E, src_ap, src_bufs, i, c + 3))
                        po = pr.next()
                        mm_group(P, po.t[:], [(wo_.t[:, k, :], scr.t[:, k, :]) for k in range(KC)], reads=wo_.b + scr.b, writes=po.b)
                        xs = xs_q.pop(0)
                        xo = E.xo.next()
                        P.op("dve", lambda e, xo=xo, po=po, xs=xs, c=c: e.scalar_tensor_tensor(
                            out=xo.t[:], in0=po.t[:], scalar=modv.t[:, gcol0 + c:gcol0 + c + 1], in1=xs.t[:], op0=ALU.mult, op1=ALU.add),
                            reads=po.b + xs.b + modv.b, writes=xo.b)
                        P.dma("sp", lambda e, xo=xo, c=c, i=i: e.dma_start(out=xres[c * 128:(c + 1) * 128, i * T:(i + 1) * T], in_=xo.t[:]),
                              reads=xo.b, writes=[xres_b[i][c]])
                    pr = pr_save
                P.flush()

        cur_ap, cur_bufs = xT, None
        FOLD = STAGES is None
        for l in range(DEPTH):
            if run("ffn1"):
                if (l, 0) != pieces[0] and not FOLD:
                    adaln_phase(l, 0)
                ffn_phase(l, 0, cur_ap, cur_bufs, bgp=((l, 1) if FOLD else None))
                cur_ap, cur_bufs = xres, xres_b
            if run("mixer"):
                if (l, 1) != pieces[0] and not FOLD:
                    adaln_phase(l, 1)
                if run("mixerA"):
                    mixer_pass_a(l, cur_ap, cur_bufs, bgp=((l, 2) if FOLD else None))
                if run("mixerB"):
                    mixer_pass_b(l, cur_ap, cur_bufs)
                cur_ap, cur_bufs = xres, xres_b
            if run("ffn2"):
                if (l, 2) != pieces[0] and not FOLD:
                    adaln_phase(l, 2)
                ffn_phase(l, 1, cur_ap, cur_bufs, bgp=((l + 1, 0) if (FOLD and l + 1 < DEPTH) else None))
                cur_ap, cur_bufs = xres, xres_b
        final_phase(cur_ap, cur_bufs)
    return nc


def _cols(v):
    v = np.asarray(v, np.float32).reshape(-1, 128)
    return np.ascontiguousarray(v.T)


def make_in_maps(inputs):
    x = np.asarray(inputs["x"], np.float32)
    c = np.asarray(inputs["c"], np.float32)
    pos = np.asarray(inputs["positions"], np.int32)
    cols = []
    for l in range(DEPTH):
        cols.append(_cols(inputs["norm_g"][l].reshape(-1)))
        cols.append(_cols(inputs["b_ada"][l]))
        cols.append(_cols(inputs["b_gate"][l]))
        cols.append(_cols(inputs["g_cq"][l]))
        cols.append(_cols(inputs["g_ckv"][l]))
    cols.append(_cols(inputs["g_final"]))
    vecs = np.ascontiguousarray(np.concatenate(cols, axis=1), np.float32)
    fa = (np.float32(10000.0) ** (-np.arange(0, 128, 2, dtype=np.float32) / np.float32(128))).astype(np.float32)
    fr = (np.float32(10000.0) ** (-np.arange(0, 64, 2, dtype=np.float32) / np.float32(64))).astype(np.float32)
    p = np.arange(128)
    consts = np.zeros((128, 4), np.float32)
    consts[:, 0] = fa[p % 64]
    consts[:, 1] = fr[p % 32]
    consts[:, 2] = np.where((p // 64) % 2 == 0, -TWOPI_S, TWOPI_S)
    consts[:, 3] = np.where((p // 32) % 2 == 0, -TWOPI_S, TWOPI_S)
    kk = np.arange(128)[:, None]
    qq = np.arange(128)[None, :]
    m_prev = (kk >= qq).astype(np.float32)
    m_next = (kk <= qq).astype(np.float32)
    masks = np.stack([np.tile(m_prev, (1, 4)), np.tile(m_next, (1, 4))], axis=1).astype(np.float32)
    sink = np.asarray(inputs["sink"], np.float32)
    sinkrow = np.repeat(sink.reshape(DEPTH, 2, 4), 128, axis=2).reshape(1, -1).astype(np.float32)
    shared = {
        "vecs": vecs, "consts": consts, "masks": np.ascontiguousarray(masks), "sinkrow": np.ascontiguousarray(sinkrow),
    }
    for k in ("w_ada", "w_ffn1_gu", "w_ffn1_d", "w_ffn2_gu", "w_ffn2_d", "w_in", "w_uq", "w_ukv", "w_oa", "w_ob", "w_out"):
        shared[k] = np.ascontiguousarray(np.asarray(inputs[k], np.float32))
    maps = []
    for b in range(NCORES):
        m = dict(shared)
        m["xT"] = np.ascontiguousarray(x[b].T)
        m["cT"] = _cols(c[b])
        m["posb"] = np.ascontiguousarray(np.broadcast_to(pos[b][None, :], (128, S)))
        maps.append(m)
    return maps


_NC_CACHE = {}


def kernel(**inputs):
    maps = make_in_maps(inputs)
    if "nc" not in _NC_CACHE:
        _NC_CACHE["nc"] = build_nc()
    nc = _NC_CACHE["nc"]
    maps = [{k: v for k, v in m.items() if k in nc._declared_inputs} for m in maps]
    res = run_bass_kernel_spmd(nc, maps, core_ids=list(range(NCORES)))
    out = np.stack([np.ascontiguousarray(np.asarray(r["outT"], np.float32).T) for r in res.results], axis=0)
    return out
```

```python
import numpy as np
from contextlib import ExitStack
import concourse.bass as bass
import concourse.mybir as mybir
from concourse.bass_utils import run_bass_kernel_spmd

F32 = mybir.dt.float32
BF16 = mybir.dt.bfloat16
I32 = mybir.dt.int32
AF = mybir.ActivationFunctionType
ALU = mybir.AluOpType

D = 2048
S = 4096
T = 512
NT = S // T
KC = D // 128
DFF = 5632
FC = DFF // 128
DEPTH = 2
NCORES = 8
EPS = 1e-6
INV2PI = float(np.float32(1.0 / (2.0 * np.pi)))
TWOPI_S = 6.28318
MAGIC = 12582912.0
SC_A = 128.0 ** -0.5
SC_B = 192.0 ** -0.5

DBG_B = None
MLA_HEADS = 8
MLA_KB = 32
STAGES = None


class Buf:
    __slots__ = ("w", "r")

    def __init__(self):
        self.w = {}
        self.r = {}


class TL:
    def __init__(self, t, n=1):
        self.t = t
        self.b = [Buf() for _ in range(n)]


class Ring:
    def __init__(self, tiles):
        self.t = tiles
        self.i = 0

    def next(self):
        x = self.t[self.i % len(self.t)]
        self.i += 1
        return x


ENG = ("pe", "act", "dve", "pool", "sp")


class Prog:
    def __init__(self, nc, es):
        self.nc = nc
        self.es = es
        self.sem = {}
        self.cnt = {}
        self.ops = {e: [] for e in ENG}
        self.known = {e: {} for e in ENG}
        for k in ("pe", "act", "dve", "pool"):
            self.newsem(k)
        self.rings = {}
        for q, n in (("sp", 16), ("pool", 8)):
            self.rings[q] = [self.newsem("d%s%d" % (q, i)) for i in range(n)]
        self.ri = {q: 0 for q in self.rings}

    def newsem(self, k):
        self.sem[k] = self.es.enter_context(self.nc.semaphore(k))
        self.cnt[k] = 0
        return k

    def _deps(self, eng, reads, writes, extra=None):
        w = {}

        def acc(d):
            for k, v in d.items():
                if w.get(k, 0) < v:
                    w[k] = v

        for b in reads:
            acc(b.w)
        for b in writes:
            acc(b.w)
            acc(b.r)
        if extra:
            acc(extra)
        kn = self.known[eng]
        need = []
        for k, v in w.items():
            if eng == "pe" and k == "pe":
                continue
            if kn.get(k, 0) < v:
                kn[k] = v
                need.append((k, v))
        return need

    def _mark(self, key, val, reads, writes):
        for b in reads:
            if b.r.get(key, 0) < val:
                b.r[key] = val
        for b in writes:
            b.w = {key: val}
            b.r = {}

    def op(self, eng, fn, reads=(), writes=()):
        need = self._deps(eng, reads, writes)
        self.cnt[eng] += 1
        val = self.cnt[eng]
        self.ops[eng].append((need, fn, eng, 1))
        self._mark(eng, val, reads, writes)

    def dma(self, q, fn, reads=(), writes=()):
        ring = self.rings[q]
        key = ring[self.ri[q] % len(ring)]
        self.ri[q] += 1
        extra = {key: self.cnt[key]} if self.cnt[key] else None
        need = self._deps(q, reads, writes, extra)
        self.cnt[key] += 16
        val = self.cnt[key]
        self.ops[q].append((need, fn, key, 16))
        self._mark(key, val, reads, writes)

    def flush(self, drain=True):
        if drain:
            need = []
            for q in self.rings:
                for k in self.rings[q]:
                    if self.cnt[k] and self.known["sp"].get(k, 0) < self.cnt[k]:
                        self.known["sp"][k] = self.cnt[k]
                        need.append((k, self.cnt[k]))
            if need:
                self.ops["sp"].append((need, None, None, 0))
        with self.nc.Block() as blk:
            for eng, reg in (("pe", blk.tensor), ("act", blk.scalar), ("dve", blk.vector),
                             ("pool", blk.gpsimd), ("sp", blk.sync)):
                lst = self.ops[eng]
                if not lst:
                    continue

                def body(e, lst=lst):
                    for need, fn, key, inc in lst:
                        for k, v in need:
                            e.wait_ge(self.sem[k], v)
                        if fn is not None:
                            fn(e).then_inc(self.sem[key], inc)

                reg(body)
                self.ops[eng] = []


def mm_group(P, out_ps, pairs, reads, writes):
    n = len(pairs)

    def fn(e):
        ins = None
        for i, (l, r) in enumerate(pairs):
            ins = e.matmul(out_ps, l, r, start=(i == 0), stop=(i == n - 1))
        return ins

    P.op("pe", fn, reads, writes)


OFF_QA, OFF_KA, OFF_VA, OFF_CQ, OFF_CKV, OFF_KR, OFF_GATE = 0, 1024, 1280, 1536, 2048, 2560, 2624
IN_QA = 0
IN_QASW = 8
IN_KA = 16
IN_KASW = 18
IN_CQ = 20
IN_CKV = 24
IN_KR = 28
IN_KRSW = 29
IN_GATE = 30
IN_N = 62


def w_in_segments(j):
    if j < IN_QASW:
        return [(0, OFF_QA + 128 * j, 128)]
    if j < IN_KA:
        h = j - IN_QASW
        return [(0, OFF_QA + 128 * h + 64, 64), (64, OFF_QA + 128 * h, 64)]
    if j < IN_KASW:
        return [(0, OFF_KA + 128 * (j - IN_KA), 128)]
    if j < IN_CQ:
        h = j - IN_KASW
        return [(0, OFF_KA + 128 * h + 64, 64), (64, OFF_KA + 128 * h, 64)]
    if j < IN_CKV:
        return [(0, OFF_CQ + 128 * (j - IN_CQ), 128)]
    if j < IN_KR:
        return [(0, OFF_CKV + 128 * (j - IN_CKV), 128)]
    if j == IN_KR:
        return [(0, OFF_KR, 64), (64, OFF_KR, 64)]
    if j == IN_KRSW:
        return [(0, OFF_KR + 32, 32), (32, OFF_KR, 32), (64, OFF_KR + 32, 32), (96, OFF_KR, 32)]
    return [(0, OFF_GATE + 128 * (j - IN_GATE), 128)]


UQ_QN, UQ_QR, UQ_QRSW, UQ_N = 0, 8, 12, 16


def w_uq_segments(j):
    if j < UQ_QR:
        return [(0, 192 * j, 128)]
    if j < UQ_QRSW:
        p = j - UQ_QR
        return [(0, 192 * (2 * p) + 128, 64), (64, 192 * (2 * p + 1) + 128, 64)]
    p = j - UQ_QRSW
    a = 192 * (2 * p) + 128
    b = 192 * (2 * p + 1) + 128
    return [(0, a + 32, 32), (32, a, 32), (64, b + 32, 32), (96, b, 32)]


def build_nc():
    nc = bass.Bass("TRN2", target_bir_lowering=False)

    declared = []
    nc._declared_inputs = declared

    def run(name):
        if STAGES is None:
            return True
        if name in STAGES:
            return True
        if name in ("tables", "mixerA", "mixerB") and "mixer" in STAGES:
            return True
        if name == "mixer":
            return any(k in STAGES for k in ("tables", "mixerA", "mixerB"))
        return False

    def din(name, shape, dt=F32):
        declared.append(name)
        return nc.dram_tensor(name, list(shape), dt, kind="ExternalInput").ap()

    def dscr(name, shape, dt):
        return nc.dram_tensor(name, list(shape), dt, kind="Internal").ap()

    xT = din("xT", [D, S])
    cT = din("cT", [128, KC])
    posb = din("posb", [128, S], I32)
    NVEC = DEPTH * (48 + 144 + 32 + 4 + 4) + 16
    vecs_d = din("vecs", [128, NVEC])
    consts_d = din("consts", [128, 4])
    masks_d = din("masks", [128, 2, 512])
    sinkrow_d = din("sinkrow", [1, DEPTH * 2 * 512])
    w_ada = din("w_ada", [DEPTH, D, 9 * D])
    w_f_gu = [din("w_ffn%d_gu" % (f + 1), [DEPTH, D, 2 * DFF]) if run("ffn%d" % (f + 1)) else None for f in range(2)]
    w_f_d = [din("w_ffn%d_d" % (f + 1), [DEPTH, DFF, D]) if run("ffn%d" % (f + 1)) else None for f in range(2)]
    w_in = din("w_in", [DEPTH, D, 6720])
    w_uq = din("w_uq", [DEPTH, 512, 1536])
    w_ukv = din("w_ukv", [DEPTH, 512, 2048])
    w_oa = din("w_oa", [DEPTH, 1024, D])
    w_ob = din("w_ob", [DEPTH, 1024, D])
    w_out = din("w_out", [DEPTH, D, D])
    outT = nc.dram_tensor("outT", [D, S], F32, kind="ExternalOutput").ap()

    xres = dscr("xres", [D, S], F32)
    ada_b = [[dscr("ada_b%d_%d" % (l, j), [48, 128, KC, 128], BF16) for j in range(3)] for l in range(DEPTH)]
    tabs_d = dscr("tabs", [4, 128, S], F32)
    gu_b = [[dscr("gu_b%d_%d" % (l, f), [FC, 128, KC, 256], BF16) for f in range(2)] for l in range(DEPTH)]
    d_b = [[dscr("d_b%d_%d" % (l, f), [KC, 128, FC, 128], BF16) for f in range(2)] for l in range(DEPTH)]
    in_b = [dscr("in_b%d" % l, [IN_N, 128, KC, 128], BF16) for l in range(DEPTH)]
    va_b = [dscr("va_b%d" % l, [128, KC, 256], BF16) for l in range(DEPTH)]
    uq_b = [dscr("uq_b%d" % l, [UQ_N, 128, 4, 128], BF16) for l in range(DEPTH)]
    kn_b = [dscr("kn_b%d" % l, [8, 128, 4, 128], BF16) for l in range(DEPTH)]
    vbw_b = [dscr("vbw_b%d" % l, [128, 4, 1024], BF16) for l in range(DEPTH)]
    oa_b = [dscr("oa_b%d" % l, [KC, 128, 8, 128], BF16) for l in range(DEPTH)]
    ob_b = [dscr("ob_b%d" % l, [KC, 128, 8, 128], BF16) for l in range(DEPTH)]
    out_b = [dscr("out_b%d" % l, [KC, 128, KC, 128], BF16) for l in range(DEPTH)]
    kaT_d = dscr("kaT_d", [2, 128, S], BF16)
    va_d = dscr("va_d", [32, 128, 256], BF16)
    knT_d = dscr("knT_d", [8, 128, S], BF16)
    krT_d = dscr("krT_d", [128, S], BF16)
    vb_d = dscr("vb_d", [8, 128, 32, 128], BF16)

    with ExitStack() as es:
        P = Prog(nc, es)

        def psb(name, shape, dt):
            return es.enter_context(nc.sbuf_tensor(name, list(shape), dt))

        vecs = TL(psb("vecs_sb", [128, NVEC], F32))
        modv = TL(psb("modv", [128, DEPTH * 9 * KC], F32))
        ones_bf = TL(psb("ones_bf", [128, 128], BF16))
        ones_f = TL(psb("ones_f", [1, 128], F32))
        consts = TL(psb("consts_sb", [128, 4], F32))

        def vcol(l, kind, c):
            base = l * 232
            if kind == "ng":
                return base + c
            if kind == "ada":
                return base + 48 + c
            if kind == "bg":
                return base + 192 + c
            if kind == "gcq":
                return base + 224 + c
            if kind == "gckv":
                return base + 228 + c
            raise ValueError

        GFIN = DEPTH * 232

        def mcol(l, j, c):
            return (l * 9 + j) * KC + c

        xres_b = [[Buf() for _ in range(KC)] for _ in range(NT)]
        ada_bufs = [[[Buf() for _ in range(48)] for _ in range(3)] for _ in range(DEPTH)]
        tabs_buf = Buf()
        gu_bufs = [[[(Buf(), Buf()) for _ in range(FC)] for _ in range(2)] for _ in range(DEPTH)]
        d_bufs = [[[Buf() for _ in range(KC)] for _ in range(2)] for _ in range(DEPTH)]
        in_bufs = [[Buf() for _ in range(IN_N)] for _ in range(DEPTH)]
        va_bufs = [Buf() for _ in range(DEPTH)]
        uq_bufs = [[Buf() for _ in range(UQ_N)] for _ in range(DEPTH)]
        kn_bufs = [[Buf() for _ in range(8)] for _ in range(DEPTH)]
        vbw_bufs = [Buf() for _ in range(DEPTH)]
        oa_bufs = [[Buf() for _ in range(KC)] for _ in range(DEPTH)]
        ob_bufs = [[Buf() for _ in range(KC)] for _ in range(DEPTH)]
        out_bufs = [[Buf() for _ in range(KC)] for _ in range(DEPTH)]
        kaT_buf = [[Buf() for _ in range(2)] for _ in range(NT)]
        va_buf = [[Buf() for _ in range(4)] for _ in range(NT)]
        knT_buf = [[Buf() for _ in range(8)] for _ in range(NT)]
        krT_buf = [Buf() for _ in range(NT)]
        vb_buf = [[Buf() for _ in range(4)] for _ in range(NT)]

        def conv(dst, src, wb):
            P.dma("pool", lambda e, dst=dst, src=src: e.dma_start(out=dst, in_=src), writes=[wb])

        def conv_ffn(l, f):
            wv = w_f_gu[f][l].rearrange("(k p) f -> p k f", p=128)
            for t in range(FC):
                for half in range(2):
                    conv(gu_b[l][f][t, :, :, half * 128:(half + 1) * 128],
                         wv[:, :, half * DFF + t * 128: half * DFF + (t + 1) * 128], gu_bufs[l][f][t][half])
            dv = w_f_d[f][l].rearrange("(k p) f -> p k f", p=128)
            for r in range(KC):
                conv(d_b[l][f][r], dv[:, :, r * 128:(r + 1) * 128], d_bufs[l][f][r])

        def conv_mixer(l):
            wv = w_in[l].rearrange("(k p) f -> p k f", p=128)
            for j in range(IN_N):
                segs = w_in_segments(j)
                if len(segs) == 1:
                    conv(in_b[l][j], wv[:, :, segs[0][1]:segs[0][1] + 128], in_bufs[l][j])
                else:
                    bl = []
                    for (dc, sc, n) in segs:
                        b = Buf()
                        conv(in_b[l][j][:, :, dc:dc + n], wv[:, :, sc:sc + n], b)
                        bl.append(b)
                    jb = in_bufs[l][j]
                    for b in bl:
                        for k, v in b.w.items():
                            if jb.w.get(k, 0) < v:
                                jb.w[k] = v
            conv(va_b[l], wv[:, :, OFF_VA:OFF_VA + 256], va_bufs[l])
            uv = w_uq[l].rearrange("(k p) f -> p k f", p=128)
            for j in range(UQ_N):
                bl = []
                for (dc, sc, n) in w_uq_segments(j):
                    b = Buf()
                    conv(uq_b[l][j][:, :, dc:dc + n], uv[:, :, sc:sc + n], b)
                    bl.append(b)
                jb = uq_bufs[l][j]
                for b in bl:
                    for k, v in b.w.items():
                        if jb.w.get(k, 0) < v:
                            jb.w[k] = v
            kv = w_ukv[l].rearrange("(k p) f -> p k f", p=128)
            for h in range(8):
                conv(kn_b[l][h], kv[:, :, 256 * h:256 * h + 128], kn_bufs[l][h])
            bl = []
            for h in range(8):
                b = Buf()
                conv(vbw_b[l][:, :, 128 * h:128 * h + 128], kv[:, :, 256 * h + 128:256 * h + 256], b)
                bl.append(b)
            for b in bl:
                for k, v in b.w.items():
                    if vbw_bufs[l].w.get(k, 0) < v:
                        vbw_bufs[l].w[k] = v
            av = w_oa[l].rearrange("(k p) f -> p k f", p=128)
            bv = w_ob[l].rearrange("(k p) f -> p k f", p=128)
            ov = w_out[l].rearrange("(k p) f -> p k f", p=128)
            for c in range(KC):
                conv(oa_b[l][c], av[:, :, c * 128:(c + 1) * 128], oa_bufs[l][c])
                conv(ob_b[l][c], bv[:, :, c * 128:(c + 1) * 128], ob_bufs[l][c])
                conv(out_b[l][c], ov[:, :, c * 128:(c + 1) * 128], out_bufs[l][c])

        def conv_ada(l, j):
            wav = w_ada[l].rearrange("(k p) f -> p k f", p=128)
            for ch in range(48):
                col = (48 * j + ch) * 128
                conv(ada_b[l][j][ch], wav[:, :, col:col + 128], ada_bufs[l][j][ch])

        conv_jobs = []
        for l in range(DEPTH):
            if run("ffn1"):
                conv_jobs.append(lambda l=l: (conv_ada(l, 0), conv_ffn(l, 0)))
            if run("mixer"):
                conv_jobs.append(lambda l=l: (conv_ada(l, 1), conv_mixer(l)))
            if run("ffn2"):
                conv_jobs.append(lambda l=l: (conv_ada(l, 2), conv_ffn(l, 1)))
        conv_pos = [0]

        def emit_next_conv():
            if conv_pos[0] < len(conv_jobs):
                conv_jobs[conv_pos[0]]()
                conv_pos[0] += 1

        emit_next_conv()

        cact = TL(psb("cact", [128, KC], BF16))
        ada_ctr = [0]

        def adaln_ops(l, j, sb, ps):
            ada_ctr[0] += 1
            pf = "ad%d_" % ada_ctr[0]
            wts = [TL(sb(pf + "w%d" % i, [128, 8, KC, 128], BF16)) for i in range(3)]
            mps = TL(ps.enter_context(nc.psum_tensor(pf + "mps", [128, 512], F32)))
            modraw = TL(sb(pf + "modraw", [128, 48], F32))
            tmp16 = TL(sb(pf + "tmp16", [128, KC], F32))
            for g in range(6):
                wt = wts[g % 3]
                P.dma("sp", lambda e, wt=wt, g=g: e.dma_start(out=wt.t[:], in_=ada_b[l][j][8 * g:8 * g + 8].rearrange("c p k f -> p c k f")),
                      reads=ada_bufs[l][j][8 * g:8 * g + 8], writes=wt.b)
                for cc in range(8):
                    ch = 8 * g + cc
                    mm_group(P, mps.t[:, ch:ch + 1], [(wt.t[:, cc, k, :], cact.t[:, k:k + 1]) for k in range(KC)], reads=wt.b + cact.b, writes=mps.b)
            c0 = vcol(l, "ada", 48 * j)
            P.op("dve", lambda e: e.tensor_tensor(out=modraw.t[:], in0=mps.t[:, 0:48], in1=vecs.t[:, c0:c0 + 48], op=ALU.add),
                 reads=mps.b + vecs.b, writes=modraw.b)
            sh = modraw.t[:, 0:KC]
            sc = modraw.t[:, KC:2 * KC]
            gt = modraw.t[:, 2 * KC:3 * KC]
            g0 = vcol(l, "ng", j * KC)
            a0 = mcol(l, 3 * j, 0)
            P.op("dve", lambda e: e.tensor_scalar(out=tmp16.t[:], in0=sc, scalar1=1.0, scalar2=None, op0=ALU.add), reads=modraw.b, writes=tmp16.b)
            P.op("dve", lambda e: e.tensor_tensor(out=modv.t[:, a0:a0 + KC], in0=tmp16.t[:], in1=vecs.t[:, g0:g0 + KC], op=ALU.mult),
                 reads=tmp16.b + vecs.b, writes=modv.b)
            P.op("dve", lambda e: e.tensor_copy(out=modv.t[:, a0 + KC:a0 + 2 * KC], in_=sh), reads=modraw.b, writes=modv.b)
            fac = 1.0 if j == 1 else 0.5
            P.op("dve", lambda e: e.tensor_scalar(out=modv.t[:, a0 + 2 * KC:a0 + 3 * KC], in0=gt, scalar1=fac, scalar2=None, op0=ALU.mult),
                 reads=modraw.b, writes=modv.b)

        def adaln_phase(l, j):
            with ExitStack() as ps:
                def sb(name, shape, dt):
                    return ps.enter_context(nc.sbuf_tensor(name, list(shape), dt))
                adaln_ops(l, j, sb, ps)
                P.flush()

        class BgAda:
            def __init__(self, l, j, sb, mps):
                self.l, self.j, self.mps = l, j, mps
                self.ring = [TL(sb("bga%d" % k, [128, KC, 128], BF16)) for k in range(4)]
                self.modraw = TL(sb("bgmodraw", [128, 48], F32))
                self.tmp16 = TL(sb("bgtmp16", [128, KC], F32))
                self.nl = 0
                self.nm = 0
                self._load(2)

            def _load(self, n):
                l, j = self.l, self.j
                for _ in range(n):
                    ch = self.nl
                    if ch >= 48:
                        return
                    wt = self.ring[ch % 4]
                    P.dma("sp", lambda e, wt=wt, ch=ch: e.dma_start(out=wt.t[:], in_=ada_b[l][j][ch]), reads=[ada_bufs[l][j][ch]], writes=wt.b)
                    self.nl += 1

            def step(self):
                for _ in range(2):
                    ch = self.nm
                    if ch >= 48:
                        return
                    wt = self.ring[ch % 4]
                    mm_group(P, self.mps.t[:, ch:ch + 1], [(wt.t[:, k, :], cact.t[:, k:k + 1]) for k in range(KC)], reads=wt.b + cact.b, writes=self.mps.b)
                    self.nm += 1
                self._load(2)

            def finish(self):
                while self.nm < 48:
                    self.step()
                l, j, mps, modraw, tmp16 = self.l, self.j, self.mps, self.modraw, self.tmp16
                c0 = vcol(l, "ada", 48 * j)
                P.op("dve", lambda e: e.tensor_tensor(out=modraw.t[:], in0=mps.t[:, 0:48], in1=vecs.t[:, c0:c0 + 48], op=ALU.add),
                     reads=mps.b + vecs.b, writes=modraw.b)
                sh = modraw.t[:, 0:KC]
                sc = modraw.t[:, KC:2 * KC]
                gt = modraw.t[:, 2 * KC:3 * KC]
                g0 = vcol(l, "ng", j * KC)
                a0 = mcol(l, 3 * j, 0)
                P.op("dve", lambda e: e.tensor_scalar(out=tmp16.t[:], in0=sc, scalar1=1.0, scalar2=None, op0=ALU.add), reads=modraw.b, writes=tmp16.b)
                P.op("dve", lambda e: e.tensor_tensor(out=modv.t[:, a0:a0 + KC], in0=tmp16.t[:], in1=vecs.t[:, g0:g0 + KC], op=ALU.mult),
                     reads=tmp16.b + vecs.b, writes=modv.b)
                P.op("dve", lambda e: e.tensor_copy(out=modv.t[:, a0 + KC:a0 + 2 * KC], in_=sh), reads=modraw.b, writes=modv.b)
                fac = 1.0 if j == 1 else 0.5
                P.op("dve", lambda e: e.tensor_scalar(out=modv.t[:, a0 + 2 * KC:a0 + 3 * KC], in0=gt, scalar1=fac, scalar2=None, op0=ALU.mult),
                     reads=modraw.b, writes=modv.b)

        stage_of = {0: "ffn1", 1: "mixer", 2: "ffn2"}
        pieces = [(l, j) for l in range(DEPTH) for j in range(3) if run(stage_of[j])]
        first_piece = pieces[0][1]

        with ExitStack() as ps:
            def sb(name, shape, dt):
                return ps.enter_context(nc.sbuf_tensor(name, list(shape), dt))

            P.dma("sp", lambda e: e.dma_start(out=vecs.t[:], in_=vecs_d[:, :]), writes=vecs.b)
            P.dma("sp", lambda e: e.dma_start(out=consts.t[:], in_=consts_d[:, :]), writes=consts.b)
            P.op("dve", lambda e: e.memset(ones_bf.t[:], 1.0), writes=ones_bf.b)
            P.op("dve", lambda e: e.memset(ones_f.t[:], 1.0), writes=ones_f.b)
            cin = TL(sb("cin", [128, KC], F32))
            P.dma("sp", lambda e: e.dma_start(out=cin.t[:], in_=cT[:, :]), writes=cin.b)
            P.op("act", lambda e: e.activation(out=cact.t[:], in_=cin.t[:], func=AF.Silu), reads=cin.b, writes=cact.b)
            if run("tables"):
                HS = S // 2
                pos_i = TL(sb("pos_i", [128, HS], I32))
                posf = TL(sb("posf", [128, HS], F32))
                u = TL(sb("u", [128, HS], F32))
                t1 = TL(sb("t1", [128, HS], F32))
                kk = TL(sb("kk", [128, HS], F32))
                tb = [TL(sb("tb%d" % i, [128, HS], F32)) for i in range(2)]
                n = 0
                for hf in range(2):
                    P.dma("sp", lambda e, hf=hf: e.dma_start(out=pos_i.t[:], in_=posb[:, hf * HS:(hf + 1) * HS]), writes=pos_i.b)
                    P.op("dve", lambda e: e.tensor_copy(out=posf.t[:], in_=pos_i.t[:]), reads=pos_i.b, writes=posf.b)
                    for fam in range(2):
                        for cs in range(2):
                            P.op("dve", lambda e, fam=fam, cs=cs: e.tensor_scalar(
                                out=u.t[:], in0=posf.t[:], scalar1=consts.t[:, fam:fam + 1], scalar2=INV2PI, op0=ALU.mult, op1=ALU.mult),
                                reads=posf.b + consts.b, writes=u.b)
                            if cs == 0:
                                P.op("dve", lambda e: e.tensor_scalar(out=u.t[:], in0=u.t[:], scalar1=0.25, scalar2=None, op0=ALU.add),
                                     reads=u.b, writes=u.b)
                            P.op("dve", lambda e: e.tensor_scalar(out=t1.t[:], in0=u.t[:], scalar1=MAGIC, scalar2=None, op0=ALU.add),
                                 reads=u.b, writes=t1.b)
                            P.op("dve", lambda e: e.tensor_scalar(out=kk.t[:], in0=t1.t[:], scalar1=MAGIC, scalar2=None, op0=ALU.subtract),
                                 reads=t1.b, writes=kk.b)
                            P.op("dve", lambda e: e.tensor_tensor(out=u.t[:], in0=u.t[:], in1=kk.t[:], op=ALU.subtract),
                                 reads=u.b + kk.b, writes=u.b)
                            ot = tb[n % 2]
                            if cs == 0:
                                P.op("act", lambda e, ot=ot: e.activation(out=ot.t[:], in_=u.t[:], func=AF.Sin, scale=TWOPI_S),
                                     reads=u.b, writes=ot.b)
                            else:
                                P.op("act", lambda e, ot=ot, fam=fam: e.activation(out=ot.t[:], in_=u.t[:], func=AF.Sin, scale=consts.t[:, 2 + fam:3 + fam]),
                                     reads=u.b + consts.b, writes=ot.b)
                            ti = fam * 2 + cs
                            bb = Buf()
                            P.dma("sp", lambda e, ot=ot, ti=ti, hf=hf: e.dma_start(out=tabs_d[ti][:, hf * HS:(hf + 1) * HS], in_=ot.t[:]), reads=ot.b, writes=[bb])
                            for k, v in bb.w.items():
                                tabs_buf.w[k] = max(tabs_buf.w.get(k, 0), v)
                            n += 1
            adaln_ops(0, first_piece, sb, ps)
            P.flush()

        class Env:
            pass

        phase_ctr = [0]

        def make_env(ps, nxr=6, ntt=4, nxo=3):
            phase_ctr[0] += 1
            pfx = "p%d_" % phase_ctr[0]

            def sb(name, shape, dt):
                return ps.enter_context(nc.sbuf_tensor(pfx + name, list(shape), dt))

            E = Env()
            E.sb = sb
            E.xr = Ring([TL(sb("xr%d" % i, [128, T], F32)) for i in range(nxr)])
            E.sq = Ring([TL(sb("sq%d" % i, [128, T], BF16)) for i in range(2)])
            E.tt = Ring([TL(sb("tt%d" % i, [128, T], F32)) for i in range(ntt)])
            E.xo = Ring([TL(sb("xo%d" % i, [128, T], F32)) for i in range(nxo)])
            E.lnv = TL(sb("lnv", [128, T], F32))
            E.rstd = TL(sb("rstd", [128, T], F32))
            E.psum = [TL(ps.enter_context(nc.psum_tensor(pfx + "ps%d" % i, [128, 512], F32))) for i in range(8)]
            return E

        def ld_x(E, src_ap, src_bufs, i, c):
            xs = E.xr.next()
            rb = [src_bufs[i][c]] if src_bufs is not None else []
            P.dma("sp", lambda e, xs=xs: e.dma_start(out=xs.t[:], in_=src_ap[c * 128:(c + 1) * 128, i * T:(i + 1) * T]),
                  reads=rb, writes=xs.b)
            return xs

        def rstd_from(E, ss, n_feat):
            P.op("act", lambda e: e.activation(out=E.lnv.t[:], in_=ss.t[:], func=AF.Ln, bias=EPS, scale=1.0 / n_feat),
                 reads=ss.b, writes=E.lnv.b)
            P.op("act", lambda e: e.activation(out=E.rstd.t[:], in_=E.lnv.t[:], func=AF.Exp, scale=-0.5),
                 reads=E.lnv.b, writes=E.rstd.b)

        class _XV:
            def __init__(self, ap, b):
                self.ap = ap
                self.b = b

        def emit_norm(E, i, src_ap, src_bufs, ss, hb, acol0, bcol0, xpre=None):
            if xpre is not None:
                for c in range(KC):
                    s = E.sq.next()
                    P.op("act", lambda e, c=c, s=s: e.activation(out=s.t[:], in_=xpre.t[:, c, :], func=AF.Square), reads=[xpre.b[c]], writes=s.b)
                    P.op("pe", lambda e, s=s, c=c: e.matmul(ss.t[:], ones_bf.t[:], s.t[:], start=(c == 0), stop=(c == KC - 1)),
                         reads=s.b + ones_bf.b, writes=ss.b)
                rstd_from(E, ss, D)
                for c in range(KC):
                    t = E.tt.next()
                    P.op("dve", lambda e, t=t, c=c: e.scalar_tensor_tensor(
                        out=t.t[:], in0=xpre.t[:, c, :], scalar=modv.t[:, acol0 + c:acol0 + c + 1], in1=E.rstd.t[:], op0=ALU.mult, op1=ALU.mult),
                        reads=[xpre.b[c]] + modv.b + E.rstd.b, writes=t.b)
                    P.op("act", lambda e, t=t, c=c: e.activation(out=hb.t[:, c, :], in_=t.t[:], func=AF.Identity,
                                                                 bias=modv.t[:, bcol0 + c:bcol0 + c + 1], scale=1.0),
                         reads=t.b + modv.b, writes=[hb.b[c]])
                return
            for c in range(KC):
                xs = ld_x(E, src_ap, src_bufs, i, c)
                s = E.sq.next()
                P.op("act", lambda e, xs=xs, s=s: e.activation(out=s.t[:], in_=xs.t[:], func=AF.Square), reads=xs.b, writes=s.b)
                P.op("pe", lambda e, s=s, c=c: e.matmul(ss.t[:], ones_bf.t[:], s.t[:], start=(c == 0), stop=(c == KC - 1)),
                     reads=s.b + ones_bf.b, writes=ss.b)
            rstd_from(E, ss, D)
            for c in range(KC):
                xs = ld_x(E, src_ap, src_bufs, i, c)
                t = E.tt.next()
                P.op("dve", lambda e, xs=xs, t=t, c=c: e.scalar_tensor_tensor(
                    out=t.t[:], in0=xs.t[:], scalar=modv.t[:, acol0 + c:acol0 + c + 1], in1=E.rstd.t[:], op0=ALU.mult, op1=ALU.mult),
                    reads=xs.b + modv.b + E.rstd.b, writes=t.b)
                P.op("act", lambda e, t=t, c=c: e.activation(out=hb.t[:, c, :], in_=t.t[:], func=AF.Identity,
                                                             bias=modv.t[:, bcol0 + c:bcol0 + c + 1], scale=1.0),
                     reads=t.b + modv.b, writes=[hb.b[c]])

        def ffn_phase(l, f, src_ap, src_bufs, bgp=None):
            with ExitStack() as ps:
                emit_next_conv()
                E = make_env(ps, nxr=6)
                sb = E.sb
                h = [TL(sb("h%d" % i, [128, KC, T], BF16), KC) for i in range(2)]
                a = TL(sb("a", [128, FC, T], BF16), FC)
                NW = 4
                wgu = [TL(sb("wgu%d" % i, [128, KC, 256], BF16)) for i in range(NW)]
                wd = [TL(sb("wd%d" % i, [128, FC, 128], BF16)) for i in range(NW)]
                ss = E.psum[0]
                pgu = [(E.psum[1], E.psum[2]), (E.psum[3], E.psum[4])]
                py = [E.psum[5], E.psum[6]] if bgp is not None else [E.psum[5], E.psum[6], E.psum[7]]
                bg = BgAda(bgp[0], bgp[1], sb, E.psum[7]) if bgp is not None else None
                jm = 3 * (0 if f == 0 else 2)
                acol0, bcol0, gcol0 = mcol(l, jm, 0), mcol(l, jm + 1, 0), mcol(l, jm + 2, 0)

                seq = []
                for i in range(NT):
                    seq += [("gu", t) for t in range(FC)] + [("d", r) for r in range(KC)]
                cnt = {"gu": 0, "d": 0}
                slot_of = {}

                def issue(u):
                    if u >= len(seq):
                        return
                    kind, idx = seq[u]
                    n = cnt[kind]
                    cnt[kind] += 1
                    if kind == "gu":
                        wt = wgu[n % NW]
                        P.dma("sp", lambda e, wt=wt, idx=idx: e.dma_start(out=wt.t[:], in_=gu_b[l][f][idx]),
                              reads=list(gu_bufs[l][f][idx]), writes=wt.b)
                    else:
                        wt = wd[n % NW]
                        P.dma("sp", lambda e, wt=wt, idx=idx: e.dma_start(out=wt.t[:], in_=d_b[l][f][idx]),
                              reads=[d_bufs[l][f][idx]], writes=wt.b)
                    slot_of[u] = wt

                LOOK = 3
                for u in range(LOOK):
                    issue(u)
                upos = [0]

                def next_w():
                    u = upos[0]
                    upos[0] += 1
                    issue(u + LOOK)
                    return slot_of.pop(u)

                emit_norm(E, 0, src_ap, src_bufs, ss, h[0], acol0, bcol0)
                npair = 0
                ny = 0
                for i in range(NT):
                    hb = h[i % 2]
                    for t in range(FC):
                        wt = next_w()
                        pg, pu = pgu[npair % 2]
                        npair += 1
                        mm_group(P, pg.t[:], [(wt.t[:, k, 0:128], hb.t[:, k, :]) for k in range(KC)], reads=wt.b + hb.b, writes=pg.b)
                        mm_group(P, pu.t[:], [(wt.t[:, k, 128:256], hb.t[:, k, :]) for k in range(KC)], reads=wt.b + hb.b, writes=pu.b)
                        sg = E.tt.next()
                        P.op("act", lambda e, sg=sg, pg=pg: e.activation(out=sg.t[:], in_=pg.t[:], func=AF.Silu), reads=pg.b, writes=sg.b)
                        P.op("dve", lambda e, sg=sg, pu=pu, t=t: e.tensor_tensor(out=a.t[:, t, :], in0=sg.t[:], in1=pu.t[:], op=ALU.mult),
                             reads=sg.b + pu.b, writes=[a.b[t]])
                        if bg is not None and t in (10, 25, 40):
                            bg.step()
                    if i + 1 < NT:
                        emit_norm(E, i + 1, src_ap, src_bufs, ss, h[(i + 1) % 2], acol0, bcol0)
                    for r in range(KC):
                        wt = next_w()
                        yp = py[ny % len(py)]
                        ny += 1
                        mm_group(P, yp.t[:], [(wt.t[:, k, :], a.t[:, k, :]) for k in range(FC)], reads=wt.b + a.b, writes=yp.b)
                        xs = ld_x(E, src_ap, src_bufs, i, r)
                        xo = E.xo.next()
                        P.op("dve", lambda e, xo=xo, yp=yp, xs=xs, r=r: e.scalar_tensor_tensor(
                            out=xo.t[:], in0=yp.t[:], scalar=modv.t[:, gcol0 + r:gcol0 + r + 1], in1=xs.t[:], op0=ALU.mult, op1=ALU.add),
                            reads=yp.b + xs.b + modv.b, writes=xo.b)
                        P.dma("sp", lambda e, xo=xo, r=r, i=i: e.dma_start(out=xres[r * 128:(r + 1) * 128, i * T:(i + 1) * T], in_=xo.t[:]),
                              reads=xo.b, writes=[xres_b[i][r]])
                if bg is not None:
                    bg.finish()
                P.flush()

        def final_phase(src_ap, src_bufs):
            out_bufs_l = []
            with ExitStack() as ps:
                E = make_env(ps)
                ss = E.psum[0]
                xt = TL(E.sb("xt", [128, KC, T], F32), KC)
                for i in range(NT):
                    for c in range(KC):
                        rb = [src_bufs[i][c]] if src_bufs is not None else []
                        P.dma("sp", lambda e, i=i, c=c: e.dma_start(out=xt.t[:, c, :], in_=src_ap[c * 128:(c + 1) * 128, i * T:(i + 1) * T]),
                              reads=rb, writes=[xt.b[c]])
                        s = E.sq.next()
                        P.op("act", lambda e, c=c, s=s: e.activation(out=s.t[:], in_=xt.t[:, c, :], func=AF.Square), reads=[xt.b[c]], writes=s.b)
                        P.op("pe", lambda e, s=s, c=c: e.matmul(ss.t[:], ones_bf.t[:], s.t[:], start=(c == 0), stop=(c == KC - 1)),
                             reads=s.b + ones_bf.b, writes=ss.b)
                    rstd_from(E, ss, D)
                    for c in range(KC):
                        xo = E.xo.next()
                        P.op("dve", lambda e, xo=xo, c=c: e.scalar_tensor_tensor(
                            out=xo.t[:], in0=xt.t[:, c, :], scalar=vecs.t[:, GFIN + c:GFIN + c + 1], in1=E.rstd.t[:], op0=ALU.mult, op1=ALU.mult),
                            reads=[xt.b[c]] + vecs.b + E.rstd.b, writes=xo.b)
                        ob = Buf()
                        out_bufs_l.append(ob)
                        P.dma("sp", lambda e, xo=xo, c=c, i=i: e.dma_start(out=outT[c * 128:(c + 1) * 128, i * T:(i + 1) * T], in_=xo.t[:]),
                              reads=xo.b, writes=[ob])
                P.flush()

        def rope3(E, p1, p2, tc, tsn, out_ap, out_bufs):
            t1 = E.tt.next()
            t2 = E.tt.next()
            P.op("dve", lambda e: e.tensor_tensor(out=t1.t[:], in0=p1.t[:], in1=tc.t[:], op=ALU.mult), reads=p1.b + tc.b, writes=t1.b)
            P.op("dve", lambda e: e.tensor_tensor(out=t2.t[:], in0=p2.t[:], in1=tsn.t[:], op=ALU.mult), reads=p2.b + tsn.b, writes=t2.b)
            P.op("dve", lambda e: e.tensor_tensor(out=out_ap, in0=t1.t[:], in1=t2.t[:], op=ALU.add), reads=t1.b + t2.b, writes=out_bufs)

        def load_tabs(tabs, i):
            for k in range(4):
                P.dma("sp", lambda e, k=k: e.dma_start(out=tabs[k].t[:], in_=tabs_d[k][:, i * T:(i + 1) * T]),
                      reads=[tabs_buf], writes=tabs[k].b)

        def mixer_pass_a(l, src_ap, src_bufs, bgp=None):
            with ExitStack() as ps:
                emit_next_conv()
                E = make_env(ps, nxr=4, ntt=4, nxo=1)
                sb = E.sb
                hh = [TL(sb("h%d" % k_, [128, KC, T], BF16), KC) for k_ in range(2)]
                win = {}
                for j in [IN_KA, IN_KA + 1, IN_KASW, IN_KASW + 1, IN_CKV, IN_CKV + 1, IN_CKV + 2, IN_CKV + 3, IN_KR, IN_KRSW]:
                    w = TL(sb("win%d" % j, [128, KC, 128], BF16))
                    P.dma("sp", lambda e, w=w, j=j: e.dma_start(out=w.t[:], in_=in_b[l][j]), reads=[in_bufs[l][j]], writes=w.b)
                    win[j] = w
                wva = TL(sb("wva", [128, KC, 256], BF16))
                P.dma("sp", lambda e: e.dma_start(out=wva.t[:], in_=va_b[l]), reads=[va_bufs[l]], writes=wva.b)
                wkn = []
                for hd in range(8):
                    w = TL(sb("wkn%d" % hd, [128, 4, 128], BF16))
                    P.dma("sp", lambda e, w=w, hd=hd: e.dma_start(out=w.t[:], in_=kn_b[l][hd]), reads=[kn_bufs[l][hd]], writes=w.b)
                    wkn.append(w)
                wvb = TL(sb("wvb", [128, 4, 1024], BF16))
                P.dma("sp", lambda e: e.dma_start(out=wvb.t[:], in_=vbw_b[l]), reads=[vbw_bufs[l]], writes=wvb.b)
                tabs = [TL(sb("tab%d" % k, [128, T], F32)) for k in range(4)]
                ckvf = TL(sb("ckvf", [128, 4, T], F32), 4)
                ckvn = TL(sb("ckvn", [128, 4, T], BF16), 4)
                stg = Ring([TL(sb("stg%d" % k, [128, T], BF16)) for k in range(4)])
                vstg = Ring([TL(sb("vstg%d" % k, [128, 1024], BF16), 2) for k in range(2)])
                ss = E.psum[0]
                pr = Ring(E.psum[1:7]) if bgp is not None else Ring(E.psum[1:8])
                bg = BgAda(bgp[0], bgp[1], sb, E.psum[7]) if bgp is not None else None
                acol0, bcol0 = mcol(l, 3, 0), mcol(l, 4, 0)
                xset = TL(sb("xset", [128, KC, T], F32), KC)

                def load_xset(i_):
                    for c in range(KC):
                        rb = [src_bufs[i_][c]] if src_bufs is not None else []
                        P.dma("sp", lambda e, c=c, i_=i_: e.dma_start(out=xset.t[:, c, :], in_=src_ap[c * 128:(c + 1) * 128, i_ * T:(i_ + 1) * T]),
                              reads=rb, writes=[xset.b[c]])

                load_xset(0)
                emit_norm(E, 0, src_ap, src_bufs, ss, hh[0], acol0, bcol0, xpre=xset)
                for i in range(NT):
                    h = hh[i % 2]
                    if i + 1 < NT:
                        load_xset(i + 1)
                    load_tabs(tabs, i)

                    def proj(j, h=h):
                        p = pr.next()
                        w = win[j]
                        mm_group(P, p.t[:], [(w.t[:, k, :], h.t[:, k, :]) for k in range(KC)], reads=w.b + h.b, writes=p.b)
                        return p

                    for kv in range(2):
                        p1 = proj(IN_KA + kv)
                        p2 = proj(IN_KASW + kv)
                        o = stg.next()
                        rope3(E, p1, p2, tabs[0], tabs[1], o.t[:], o.b)
                        P.dma("sp", lambda e, o=o, kv=kv, i=i: e.dma_start(out=kaT_d[kv][:, i * T:(i + 1) * T], in_=o.t[:]),
                              reads=o.b, writes=[kaT_buf[i][kv]])
                    p1 = proj(IN_KR)
                    p2 = proj(IN_KRSW)
                    o = stg.next()
                    rope3(E, p1, p2, tabs[2], tabs[3], o.t[:], o.b)
                    P.dma("sp", lambda e, o=o, i=i: e.dma_start(out=krT_d[:, i * T:(i + 1) * T], in_=o.t[:]), reads=o.b, writes=[krT_buf[i]])
                    for j in range(4):
                        p = pr.next()
                        mm_group(P, p.t[:, 0:256], [(h.t[:, k, j * 128:(j + 1) * 128], wva.t[:, k, :]) for k in range(KC)],
                                 reads=wva.b + h.b, writes=p.b)
                        o = vstg.next()
                        P.op("act", lambda e, o=o, p=p: e.activation(out=o.t[:, 0:256], in_=p.t[:, 0:256], func=AF.Identity),
                             reads=p.b, writes=o.b)
                        P.dma("sp", lambda e, o=o, i=i, j=j: e.dma_start(out=va_d[4 * i + j], in_=o.t[:, 0:256]), reads=o.b, writes=[va_buf[i][j]])
                    for j in range(4):
                        p = proj(IN_CKV + j)
                        P.op("act", lambda e, p=p, j=j: e.activation(out=ckvf.t[:, j, :], in_=p.t[:], func=AF.Identity), reads=p.b, writes=[ckvf.b[j]])
                        s = E.sq.next()
                        P.op("act", lambda e, s=s, j=j: e.activation(out=s.t[:], in_=ckvf.t[:, j, :], func=AF.Square), reads=[ckvf.b[j]], writes=s.b)
                        P.op("pe", lambda e, s=s, j=j: e.matmul(ss.t[:], ones_bf.t[:], s.t[:], start=(j == 0), stop=(j == 3)),
                             reads=s.b + ones_bf.b, writes=ss.b)
                    rstd_from(E, ss, 512)
                    for j in range(4):
                        gc = vcol(l, "gckv", j)
                        P.op("dve", lambda e, j=j, gc=gc: e.scalar_tensor_tensor(
                            out=ckvn.t[:, j, :], in0=ckvf.t[:, j, :], scalar=vecs.t[:, gc:gc + 1], in1=E.rstd.t[:], op0=ALU.mult, op1=ALU.mult),
                            reads=[ckvf.b[j]] + vecs.b + E.rstd.b, writes=[ckvn.b[j]])
                    if bg is not None:
                        bg.step()
                        bg.step()
                    if i + 1 < NT:
                        emit_norm(E, i + 1, src_ap, src_bufs, ss, hh[(i + 1) % 2], acol0, bcol0, xpre=xset)
                    if bg is not None:
                        bg.step()
                    for hd in range(8):
                        p = pr.next()
                        mm_group(P, p.t[:], [(wkn[hd].t[:, k, :], ckvn.t[:, k, :]) for k in range(4)], reads=wkn[hd].b + ckvn.b, writes=p.b)
                        o = stg.next()
                        P.op("act", lambda e, o=o, p=p: e.activation(out=o.t[:], in_=p.t[:], func=AF.Identity), reads=p.b, writes=o.b)
                        P.dma("sp", lambda e, o=o, hd=hd, i=i: e.dma_start(out=knT_d[hd][:, i * T:(i + 1) * T], in_=o.t[:]),
                              reads=o.b, writes=[knT_buf[i][hd]])
                    for j in range(4):
                        o = vstg.next()
                        for half in range(2):
                            p = pr.next()
                            mm_group(P, p.t[:], [(ckvn.t[:, k, j * 128:(j + 1) * 128], wvb.t[:, k, half * 512:(half + 1) * 512]) for k in range(4)],
                                     reads=wvb.b + ckvn.b, writes=p.b)
                            P.op("act", lambda e, o=o, p=p, half=half: e.activation(out=o.t[:, half * 512:(half + 1) * 512], in_=p.t[:], func=AF.Identity),
                                 reads=p.b, writes=[o.b[half]])
                        P.dma("sp", lambda e, o=o, i=i, j=j: e.dma_start(
                            out=vb_d[:, :, 4 * i + j, :].rearrange("h p d -> p h d"), in_=o.t[:].rearrange("p (h d) -> p h d", h=8)),
                            reads=o.b, writes=[vb_buf[i][j]])
                if bg is not None:
                    bg.finish()
                P.flush()

        def mixer_pass_b(l, src_ap, src_bufs):
            with ExitStack() as ps:
                E = make_env(ps, nxr=4, ntt=5, nxo=2)
                sb = E.sb
                hh = [TL(sb("h%d" % k_, [128, KC, T], BF16), KC) for k_ in range(2)]
                accE = TL(sb("accE", [128, T], F32))
                accO = TL(sb("accO", [128, T], F32))
                dhi = TL(sb("dhi", [128, T], BF16))
                dlo = TL(sb("dlo", [128, T], BF16))
                rlnv = TL(sb("rlnv", [128, T], F32))
                rrec = TL(sb("rrec", [128, T], F32))
                tabs = [TL(sb("tab%d" % k, [128, T], F32)) for k in range(4)]
                scr = TL(sb("scr", [128, KC, T], BF16), KC)
                cqf = TL(sb("cqf", [128, 4, T], F32), 4)
                cqn = TL(sb("cqn", [128, 4, T], BF16), 4)
                qn = TL(sb("qn", [128, 8, T], BF16), 8)
                qrp = TL(sb("qrp", [128, 8, T], BF16), 8)
                Oa = TL(sb("Oa", [128, 8, T], BF16), 8)
                Ob = TL(sb("Ob", [128, 8, T], BF16), 8)
                kawin = TL(sb("kawin", [128, 2, 768], BF16), 2)
                vawin = TL(sb("vawin", [128, 6, 256], BF16))
                krT = TL(sb("krT", [128, S], BF16))
                NKV = 4
                knq = [TL(sb("knq%d" % k, [128, 1024], BF16)) for k in range(NKV)]
                vbq = [TL(sb("vbq%d" % k, [128, 8, 128], BF16)) for k in range(NKV)]
                NWR = 4
                wring = [TL(sb("wr%d" % k, [128, KC, 128], BF16)) for k in range(NWR)]
                maskb = TL(sb("maskb", [128, 2, 512], BF16))
                ss = E.psum[0]
                pr = Ring(E.psum[1:5] + [E.psum[7]])
                Xacc = E.psum[5]
                prw = Ring(E.psum[1:4])
                wacc = [(E.psum[6], E.psum[7]), (E.psum[4], E.psum[5])]
                ptrw = Ring([TL(sb("ptw%d" % k, [128, T], BF16)) for k in range(7)])
                ptr = Ring(ptrw.t[0:7])
                pr7 = Ring(E.psum[1:8])
                Oacc, Dacc = E.psum[6], E.psum[7]
                acol0, bcol0, gcol0 = mcol(l, 3, 0), mcol(l, 4, 0), mcol(l, 5, 0)

                P.dma("sp", lambda e: e.dma_start(out=krT.t[:], in_=krT_d[:, :]), reads=list(krT_buf), writes=krT.b)
                P.op("dve", lambda e: e.memset(qrp.t[:], 0.0), writes=qrp.b)
                for which in range(2):
                    mt = E.tt.next()
                    P.dma("sp", lambda e, mt=mt, which=which: e.dma_start(out=mt.t[:], in_=masks_d[:, which, :]), writes=mt.b)
                    P.op("dve", lambda e, mt=mt, which=which: e.tensor_copy(out=maskb.t[:, which, :], in_=mt.t[:]), reads=mt.b, writes=maskb.b)
                es_hi = TL(sb("es_hi", [1, 1024], BF16), 2)
                es_lo = TL(sb("es_lo", [1, 1024], BF16), 2)
                for kv in range(2):
                    stg_ = E.lnv if kv == 0 else E.rstd
                    tmp_ = E.tt.next()
                    c0_ = l * 1024 + kv * 512
                    P.dma("sp", lambda e, stg_=stg_, c0_=c0_: e.dma_start(out=stg_.t[0:1, :], in_=sinkrow_d[0:1, c0_:c0_ + 512]), writes=stg_.b)
                    P.op("act", lambda e, stg_=stg_: e.activation(out=stg_.t[0:1, :], in_=stg_.t[0:1, :], func=AF.Exp), reads=stg_.b, writes=stg_.b)
                    P.op("dve", lambda e, stg_=stg_, kv=kv: e.tensor_copy(out=es_hi.t[0:1, kv * 512:(kv + 1) * 512], in_=stg_.t[0:1, :]),
                         reads=stg_.b, writes=[es_hi.b[kv]])
                    P.op("dve", lambda e, stg_=stg_, tmp_=tmp_, kv=kv: e.tensor_tensor(out=tmp_.t[0:1, :], in0=stg_.t[0:1, :], in1=es_hi.t[0:1, kv * 512:(kv + 1) * 512], op=ALU.subtract),
                         reads=stg_.b + [es_hi.b[kv]], writes=tmp_.b)
                    P.op("dve", lambda e, tmp_=tmp_, kv=kv: e.tensor_copy(out=es_lo.t[0:1, kv * 512:(kv + 1) * 512], in_=tmp_.t[0:1, :]),
                         reads=tmp_.b, writes=[es_lo.b[kv]])

                wseq = []
                for i in range(NT):
                    for hd in range(8):
                        wseq.append((in_b[l][IN_QA + hd], in_bufs[l][IN_QA + hd], KC))
                        wseq.append((in_b[l][IN_QASW + hd], in_bufs[l][IN_QASW + hd], KC))
                    for j in range(4):
                        wseq.append((in_b[l][IN_CQ + j], in_bufs[l][IN_CQ + j], KC))
                    for hd in range(8):
                        wseq.append((uq_b[l][UQ_QN + hd], uq_bufs[l][UQ_QN + hd], 4))
                    for jj in range(4):
                        wseq.append((uq_b[l][UQ_QR + jj], uq_bufs[l][UQ_QR + jj], 4))
                        wseq.append((uq_b[l][UQ_QRSW + jj], uq_bufs[l][UQ_QRSW + jj], 4))
                    for c in range(KC):
                        wseq.append((in_b[l][IN_GATE + c], in_bufs[l][IN_GATE + c], KC))
                        wseq.append((in_b[l][IN_GATE + KC + c], in_bufs[l][IN_GATE + KC + c], KC))
                        wseq.append((oa_b[l][c], oa_bufs[l][c], 8))
                        wseq.append((ob_b[l][c], ob_bufs[l][c], 8))
                    for c in range(KC):
                        wseq.append((out_b[l][c], out_bufs[l][c], KC))
                wslot = {}
                wpos = [0]
                LOOKW = NWR - 1

                def wissue(u):
                    if u >= len(wseq):
                        return
                    ap, bf, kc = wseq[u]
                    wt = wring[u % NWR]
                    P.dma("sp", lambda e, wt=wt, ap=ap, kc=kc: e.dma_start(out=wt.t[:, 0:kc, :], in_=ap), reads=[bf], writes=wt.b)
                    wslot[u] = wt

                for u in range(LOOKW):
                    wissue(u)

                def next_w():
                    u = wpos[0]
                    wpos[0] += 1
                    wissue(u + LOOKW)
                    return wslot.pop(u)

                kvseq = [(i, hd, q4) for i in range(NT) for hd in range(8) for q4 in range(4)]
                kvslot = {}
                kvpos = [0]
                LOOKKV = NKV - 2

                def kvissue(u):
                    if u >= len(kvseq):
                        return
                    _, hd, q4 = kvseq[u]
                    kt = knq[u % NKV]
                    vt = vbq[u % NKV]
                    P.dma("sp", lambda e, kt=kt, hd=hd, q4=q4: e.dma_start(out=kt.t[:], in_=knT_d[hd][:, q4 * 1024:(q4 + 1) * 1024]),
                          reads=[knT_buf[2 * q4][hd], knT_buf[2 * q4 + 1][hd]], writes=kt.b)
                    P.dma("sp", lambda e, vt=vt, hd=hd, q4=q4: e.dma_start(out=vt.t[:], in_=vb_d[hd][:, 8 * q4:8 * q4 + 8, :]),
                          reads=vb_buf[2 * q4] + vb_buf[2 * q4 + 1], writes=vt.b)
                    kvslot[u] = (kt, vt)

                for u in range(LOOKKV):
                    kvissue(u)

                def next_kv():
                    u = kvpos[0]
                    kvpos[0] += 1
                    kvissue(u + LOOKKV)
                    return kvslot.pop(u)

                def rden_from(dacc):
                    P.op("act", lambda e: e.activation(out=rlnv.t[:], in_=dacc.t[:], func=AF.Ln), reads=dacc.b, writes=rlnv.b)
                    P.op("act", lambda e: e.activation(out=rrec.t[:], in_=rlnv.t[:], func=AF.Exp, scale=-1.0), reads=rlnv.b, writes=rrec.b)

                emit_norm(E, 0, src_ap, src_bufs, ss, hh[0], acol0, bcol0)
                for i in range(NT):
                    h = hh[i % 2]
                    load_tabs(tabs, i)
                    kb0 = max(4 * i - 1, 0)
                    kb1 = min(4 * i + 4, 31)
                    nkb = kb1 - kb0 + 1
                    tl_ = sorted(set(kb // 4 for kb in range(kb0, kb1 + 1)))
                    for kv in range(2):
                        P.dma("sp", lambda e, kv=kv, kb0=kb0, nkb=nkb: e.dma_start(out=kawin.t[:, kv, 0:nkb * 128], in_=kaT_d[kv][:, kb0 * 128:(kb0 + nkb) * 128]),
                              reads=[kaT_buf[t_][kv] for t_ in tl_], writes=[kawin.b[kv]])
                    P.dma("sp", lambda e, kb0=kb0, nkb=nkb: e.dma_start(out=vawin.t[:, 0:nkb, :], in_=va_d[kb0:kb0 + nkb].rearrange("b p d -> p b d")),
                          reads=[b_ for t_ in tl_ for b_ in va_buf[t_]], writes=vawin.b)

                    def projw(w, h=h):
                        p = pr.next()
                        mm_group(P, p.t[:], [(w.t[:, k, :], h.t[:, k, :]) for k in range(KC)], reads=w.b + h.b, writes=p.b)
                        return p

                    for hd in range(8):
                        p1 = projw(next_w())
                        p2 = projw(next_w())
                        rope3(E, p1, p2, tabs[0], tabs[1], scr.t[:, hd, :], [scr.b[hd]])
                    for j in range(4):
                        p = projw(next_w())
                        P.op("act", lambda e, p=p, j=j: e.activation(out=cqf.t[:, j, :], in_=p.t[:], func=AF.Identity), reads=p.b, writes=[cqf.b[j]])
                        s = E.sq.next()
                        P.op("act", lambda e, s=s, j=j: e.activation(out=s.t[:], in_=cqf.t[:, j, :], func=AF.Square), reads=[cqf.b[j]], writes=s.b)
                        P.op("pe", lambda e, s=s, j=j: e.matmul(ss.t[:], ones_bf.t[:], s.t[:], start=(j == 0), stop=(j == 3)),
                             reads=s.b + ones_bf.b, writes=ss.b)
                    rstd_from(E, ss, 512)
                    for j in range(4):
                        gc = vcol(l, "gcq", j)
                        P.op("dve", lambda e, j=j, gc=gc: e.scalar_tensor_tensor(
                            out=cqn.t[:, j, :], in0=cqf.t[:, j, :], scalar=vecs.t[:, gc:gc + 1], in1=E.rstd.t[:], op0=ALU.mult, op1=ALU.mult),
                            reads=[cqf.b[j]] + vecs.b + E.rstd.b, writes=[cqn.b[j]])
                    for hd in range(8):
                        p = pr.next()
                        w = next_w()
                        mm_group(P, p.t[:], [(w.t[:, k, :], cqn.t[:, k, :]) for k in range(4)], reads=w.b + cqn.b, writes=p.b)
                        P.op("act", lambda e, p=p, hd=hd: e.activation(out=qn.t[:, hd, :], in_=p.t[:], func=AF.Identity), reads=p.b, writes=[qn.b[hd]])
                    for jj in range(4):
                        p1 = pr.next()
                        w = next_w()
                        mm_group(P, p1.t[:], [(w.t[:, k, :], cqn.t[:, k, :]) for k in range(4)], reads=w.b + cqn.b, writes=p1.b)
                        p2 = pr.next()
                        w = next_w()
                        mm_group(P, p2.t[:], [(w.t[:, k, :], cqn.t[:, k, :]) for k in range(4)], reads=w.b + cqn.b, writes=p2.b)
                        t1_ = E.tt.next()
                        t2_ = E.tt.next()
                        P.op("dve", lambda e, t1_=t1_, p1=p1: e.tensor_tensor(out=t1_.t[:], in0=p1.t[:], in1=tabs[2].t[:], op=ALU.mult), reads=p1.b + tabs[2].b, writes=t1_.b)
                        P.op("dve", lambda e, t2_=t2_, p2=p2: e.tensor_tensor(out=t2_.t[:], in0=p2.t[:], in1=tabs[3].t[:], op=ALU.mult), reads=p2.b + tabs[3].b, writes=t2_.b)
                        for par in range(2):
                            lo_ = 64 * par
                            P.op("dve", lambda e, t1_=t1_, t2_=t2_, lo_=lo_, jj=jj, par=par: e.tensor_tensor(
                                out=qrp.t[lo_:lo_ + 64, 2 * jj + par, :], in0=t1_.t[lo_:lo_ + 64, :], in1=t2_.t[lo_:lo_ + 64, :], op=ALU.add),
                                reads=t1_.b + t2_.b, writes=[qrp.b[2 * jj + par]])

                    combos = [(kv, j) for kv in range(2) for j in range(4)] if (DBG_B is None or "win" in DBG_B) else []

                    def win_front(k):
                        kv, j = combos[k]
                        n = 4 * i + j
                        outl = []
                        for kb in [kb for kb in (n - 1, n, n + 1) if 0 <= kb <= 31]:
                            sp_ = prw.next()
                            ko = (kb - kb0) * 128
                            P.op("pe", lambda e, sp_=sp_, kv=kv, ko=ko, j=j: e.matmul(
                                sp_.t[:].rearrange("p (g q) -> p g q", g=4), kawin.t[:, kv, ko:ko + 128],
                                scr.t[:, 4 * kv:4 * kv + 4, j * 128:(j + 1) * 128], start=True, stop=True),
                                reads=[kawin.b[kv]] + scr.b[4 * kv:4 * kv + 4], writes=sp_.b)
                            pt = ptrw.next()
                            P.op("act", lambda e, pt=pt, sp_=sp_: e.activation(out=pt.t[:], in_=sp_.t[:], func=AF.Exp, scale=SC_A), reads=sp_.b, writes=pt.b)
                            if kb != n:
                                which = 0 if kb == n - 1 else 1
                                P.op("dve", lambda e, pt=pt, which=which: e.tensor_tensor(out=pt.t[:], in0=pt.t[:], in1=maskb.t[:, which, :], op=ALU.mult),
                                     reads=pt.b + maskb.b, writes=pt.b)
                            outl.append((kb, pt))
                        return outl

                    def win_back(k, pts):
                        kv, j = combos[k]
                        Oa_, Da_ = wacc[k % 2]
                        nk = len(pts)
                        for idx, (kb, pt) in enumerate(pts):
                            vo = kb - kb0
                            P.op("pe", lambda e, pt=pt, vo=vo, kv=kv, idx=idx, nk=nk: e.matmul(
                                Oa_.t[:], vawin.t[:, vo, kv * 128:(kv + 1) * 128], pt.t[:], start=(idx == 0), stop=(idx == nk - 1)),
                                reads=pt.b + vawin.b, writes=Oa_.b)
                            P.op("pe", lambda e, pt=pt, idx=idx: e.matmul(Da_.t[:], ones_bf.t[:], pt.t[:], start=(idx == 0), stop=False),
                                 reads=pt.b + ones_bf.b, writes=Da_.b)
                        P.op("pe", lambda e, kv=kv: e.matmul(Da_.t[:], ones_bf.t[0:1, :], es_hi.t[0:1, kv * 512:(kv + 1) * 512], start=False, stop=False),
                             reads=[es_hi.b[kv]] + ones_bf.b, writes=Da_.b)
                        P.op("pe", lambda e, kv=kv: e.matmul(Da_.t[:], ones_bf.t[0:1, :], es_lo.t[0:1, kv * 512:(kv + 1) * 512], start=False, stop=True),
                             reads=[es_lo.b[kv]] + ones_bf.b, writes=Da_.b)
                        rden_from(Da_)
                        P.op("dve", lambda e, kv=kv, j=j: e.tensor_tensor(
                            out=Oa.t[:, 4 * kv:4 * kv + 4, j * 128:(j + 1) * 128], in0=Oa_.t[:].rearrange("p (g q) -> p g q", g=4),
                            in1=rrec.t[:].rearrange("p (g q) -> p g q", g=4), op=ALU.mult),
                            reads=Oa_.b + rrec.b, writes=Oa.b[4 * kv:4 * kv + 4])

                    if combos:
                        front = win_front(0)
                        for k in range(len(combos)):
                            nxt_ = win_front(k + 1) if k + 1 < len(combos) else None
                            win_back(k, front)
                            front = nxt_

                    for hd in (range(MLA_HEADS) if (DBG_B is None or "mla" in DBG_B) else []):
                        ph = 64 * (hd % 2)
                        slots = [next_kv() for _ in range(1)]
                        cur = {"q4": 0, "kv": slots[0]}

                        def emit_s(kb, hd=hd, ph=ph):
                            q4, kbl = kb // 8, kb % 8
                            if q4 != cur["q4"]:
                                cur["q4"] = q4
                                cur["kv"] = next_kv()
                            kt, vt = cur["kv"]
                            sp_ = pr.next()
                            mm_group(P, sp_.t[:], [(kt.t[:, kbl * 128:(kbl + 1) * 128], qn.t[:, hd, :]),
                                                   (krT.t[:, kb * 128:(kb + 1) * 128], qrp.t[:, hd, :])],
                                     reads=kt.b + [qn.b[hd], qrp.b[hd]] + krT.b, writes=sp_.b)
                            return sp_, vt, kbl

                        sq_ = [emit_s(0), emit_s(1), emit_s(2)]
                        for kb in range(32):
                            sp_, vt, kbl = sq_.pop(0)
                            if kb + 3 < 32:
                                sq_.append(emit_s(kb + 3))
                            pt = ptr.next()
                            P.op("act", lambda e, pt=pt, sp_=sp_: e.activation(out=pt.t[:], in_=sp_.t[:], func=AF.Exp, scale=SC_B), reads=sp_.b, writes=pt.b)
                            P.op("pe", lambda e, pt=pt, vt=vt, kbl=kbl, kb=kb: e.matmul(Oacc.t[:], vt.t[:, kbl, :], pt.t[:], start=(kb == 0), stop=(kb == 31)),
                                 reads=pt.b + vt.b, writes=Oacc.b)
                            if kb % 4 == 3:
                                ae_, an_, first_ = accO, "pool", (kb == 3)
                            else:
                                ae_, an_, first_ = Xacc, "dve", (kb == 0)
                            if first_:
                                P.op(an_, lambda e, ae_=ae_, pt=pt: e.tensor_copy(out=ae_.t[:], in_=pt.t[:]), reads=pt.b, writes=ae_.b)
                            else:
                                P.op(an_, lambda e, ae_=ae_, pt=pt: e.tensor_tensor(out=ae_.t[:], in0=ae_.t[:], in1=pt.t[:], op=ALU.add),
                                     reads=pt.b + ae_.b, writes=ae_.b)
                        P.op("dve", lambda e: e.tensor_tensor(out=accE.t[:], in0=Xacc.t[:], in1=accO.t[:], op=ALU.add), reads=Xacc.b + accO.b, writes=accE.b)
                        P.op("dve", lambda e: e.tensor_copy(out=dhi.t[:], in_=accE.t[:]), reads=accE.b, writes=dhi.b)
                        P.op("dve", lambda e: e.tensor_tensor(out=accO.t[:], in0=accE.t[:], in1=dhi.t[:], op=ALU.subtract), reads=accE.b + dhi.b, writes=accO.b)
                        P.op("dve", lambda e: e.tensor_copy(out=dlo.t[:], in_=accO.t[:]), reads=accO.b, writes=dlo.b)
                        dps_ = pr.next()
                        mm_group(P, dps_.t[:], [(ones_bf.t[:], dhi.t[:]), (ones_bf.t[:], dlo.t[:])], reads=dhi.b + dlo.b + ones_bf.b, writes=dps_.b)
                        rden_from(dps_)
                        P.op("dve", lambda e, hd=hd: e.tensor_tensor(out=Ob.t[:, hd, :], in0=Oacc.t[:], in1=rrec.t[:], op=ALU.mult),
                             reads=Oacc.b + rrec.b, writes=[Ob.b[hd]])
                        if hd == 3 and i + 1 < NT:
                            emit_norm(E, i + 1, src_ap, src_bufs, ss, hh[(i + 1) % 2], acol0, bcol0)

                    pr_save = pr
                    pr = pr7
                    for c in (range(KC) if (DBG_B is None or "merge" in DBG_B) else []):
                        pga = projw(next_w())
                        pgb = projw(next_w())
                        wa_ = next_w()
                        pya = pr.next()
                        mm_group(P, pya.t[:], [(wa_.t[:, k, :], Oa.t[:, k, :]) for k in range(8)], reads=wa_.b + Oa.b, writes=pya.b)
                        wb_ = next_w()
                        pyb = pr.next()
                        mm_group(P, pyb.t[:], [(wb_.t[:, k, :], Ob.t[:, k, :]) for k in range(8)], reads=wb_.b + Ob.b, writes=pyb.b)
                        ga = E.tt.next()
                        gb = E.tt.next()
                        bga = vcol(l, "bg", c)
                        bgb = vcol(l, "bg", KC + c)
                        P.op("act", lambda e, ga=ga, pga=pga, bga=bga: e.activation(out=ga.t[:], in_=pga.t[:], func=AF.Sigmoid, bias=vecs.t[:, bga:bga + 1], scale=1.0),
                             reads=pga.b + vecs.b, writes=ga.b)
                        P.op("act", lambda e, gb=gb, pgb=pgb, bgb=bgb: e.activation(out=gb.t[:], in_=pgb.t[:], func=AF.Sigmoid, bias=vecs.t[:, bgb:bgb + 1], scale=1.0),
                             reads=pgb.b + vecs.b, writes=gb.b)
                        P.op("dve", lambda e, ga=ga, pya=pya: e.tensor_tensor(out=ga.t[:], in0=pya.t[:], in1=ga.t[:], op=ALU.mult), reads=pya.b + ga.b, writes=ga.b)
                        P.op("dve", lambda e, gb=gb, pyb=pyb: e.tensor_tensor(out=gb.t[:], in0=pyb.t[:], in1=gb.t[:], op=ALU.mult), reads=pyb.b + gb.b, writes=gb.b)
                        P.op("dve", lambda e, ga=ga, gb=gb, c=c: e.tensor_tensor(out=scr.t[:, c, :], in0=ga.t[:], in1=gb.t[:], op=ALU.add),
                             reads=ga.b + gb.b, writes=[scr.b[c]])
                    xs_q = [ld_x(E, src_ap, src_bufs, i, c) for c in range(3)]
                    for c in (range(KC) if (DBG_B is None or "out" in DBG_B) else []):
                        wo_ = next_w()
                        if c + 3 < KC:
                            xs_q.append(ld_x(E, src_ap, src_bufs, i, c + 3))
                        po = pr.next()
                        mm_group(P, po.t[:], [(wo_.t[:, k, :], scr.t[:, k, :]) for k in range(KC)], reads=wo_.b + scr.b, writes=po.b)
                        xs = xs_q.pop(0)
                        xo = E.xo.next()
                        P.op("dve", lambda e, xo=xo, po=po, xs=xs, c=c: e.scalar_tensor_tensor(
                            out=xo.t[:], in0=po.t[:], scalar=modv.t[:, gcol0 + c:gcol0 + c + 1], in1=xs.t[:], op0=ALU.mult, op1=ALU.add),
                            reads=po.b + xs.b + modv.b, writes=xo.b)
                        P.dma("sp", lambda e, xo=xo, c=c, i=i: e.dma_start(out=xres[c * 128:(c + 1) * 128, i * T:(i + 1) * T], in_=xo.t[:]),
                              reads=xo.b, writes=[xres_b[i][c]])
                    pr = pr_save
                P.flush()

        cur_ap, cur_bufs = xT, None
        FOLD = STAGES is None
        for l in range(DEPTH):
            if run("ffn1"):
                if (l, 0) != pieces[0] and not FOLD:
                    adaln_phase(l, 0)
                ffn_phase(l, 0, cur_ap, cur_bufs, bgp=((l, 1) if FOLD else None))
                cur_ap, cur_bufs = xres, xres_b
            if run("mixer"):
                if (l, 1) != pieces[0] and not FOLD:
                    adaln_phase(l, 1)
                if run("mixerA"):
                    mixer_pass_a(l, cur_ap, cur_bufs, bgp=((l, 2) if FOLD else None))
                if run("mixerB"):
                    mixer_pass_b(l, cur_ap, cur_bufs)
                cur_ap, cur_bufs = xres, xres_b
            if run("ffn2"):
                if (l, 2) != pieces[0] and not FOLD:
                    adaln_phase(l, 2)
                ffn_phase(l, 1, cur_ap, cur_bufs, bgp=((l + 1, 0) if (FOLD and l + 1 < DEPTH) else None))
                cur_ap, cur_bufs = xres, xres_b
        final_phase(cur_ap, cur_bufs)
    return nc


def _cols(v):
    v = np.asarray(v, np.float32).reshape(-1, 128)
    return np.ascontiguousarray(v.T)


def make_in_maps(inputs):
    x = np.asarray(inputs["x"], np.float32)
    c = np.asarray(inputs["c"], np.float32)
    pos = np.asarray(inputs["positions"], np.int32)
    cols = []
    for l in range(DEPTH):
        cols.append(_cols(inputs["norm_g"][l].reshape(-1)))
        cols.append(_cols(inputs["b_ada"][l]))
        cols.append(_cols(inputs["b_gate"][l]))
        cols.append(_cols(inputs["g_cq"][l]))
        cols.append(_cols(inputs["g_ckv"][l]))
    cols.append(_cols(inputs["g_final"]))
    vecs = np.ascontiguousarray(np.concatenate(cols, axis=1), np.float32)
    fa = (np.float32(10000.0) ** (-np.arange(0, 128, 2, dtype=np.float32) / np.float32(128))).astype(np.float32)
    fr = (np.float32(10000.0) ** (-np.arange(0, 64, 2, dtype=np.float32) / np.float32(64))).astype(np.float32)
    p = np.arange(128)
    consts = np.zeros((128, 4), np.float32)
    consts[:, 0] = fa[p % 64]
    consts[:, 1] = fr[p % 32]
    consts[:, 2] = np.where((p // 64) % 2 == 0, -TWOPI_S, TWOPI_S)
    consts[:, 3] = np.where((p // 32) % 2 == 0, -TWOPI_S, TWOPI_S)
    kk = np.arange(128)[:, None]
    qq = np.arange(128)[None, :]
    m_prev = (kk >= qq).astype(np.float32)
    m_next = (kk <= qq).astype(np.float32)
    masks = np.stack([np.tile(m_prev, (1, 4)), np.tile(m_next, (1, 4))], axis=1).astype(np.float32)
    sink = np.asarray(inputs["sink"], np.float32)
    sinkrow = np.repeat(sink.reshape(DEPTH, 2, 4), 128, axis=2).reshape(1, -1).astype(np.float32)
    shared = {
        "vecs": vecs, "consts": consts, "masks": np.ascontiguousarray(masks), "sinkrow": np.ascontiguousarray(sinkrow),
    }
    for k in ("w_ada", "w_ffn1_gu", "w_ffn1_d", "w_ffn2_gu", "w_ffn2_d", "w_in", "w_uq", "w_ukv", "w_oa", "w_ob", "w_out"):
        shared[k] = np.ascontiguousarray(np.asarray(inputs[k], np.float32))
    maps = []
    for b in range(NCORES):
        m = dict(shared)
        m["xT"] = np.ascontiguousarray(x[b].T)
        m["cT"] = _cols(c[b])
        m["posb"] = np.ascontiguousarray(np.broadcast_to(pos[b][None, :], (128, S)))
        maps.append(m)
    return maps


_NC_CACHE = {}


def kernel(**inputs):
    maps = make_in_maps(inputs)
    if "nc" not in _NC_CACHE:
        _NC_CACHE["nc"] = build_nc()
    nc = _NC_CACHE["nc"]
    maps = [{k: v for k, v in m.items() if k in nc._declared_inputs} for m in maps]
    res = run_bass_kernel_spmd(nc, maps, core_ids=list(range(NCORES)))
    out = np.stack([np.ascontiguousarray(np.asarray(r["outT"], np.float32).T) for r in res.results], axis=0)
    return out
```

```python
import numpy as np
from contextlib import ExitStack
import concourse.bass as bass
import concourse.mybir as mybir
from concourse.bass_utils import run_bass_kernel_spmd

F32 = mybir.dt.float32
BF16 = mybir.dt.bfloat16
I32 = mybir.dt.int32
AF = mybir.ActivationFunctionType
ALU = mybir.AluOpType

D = 2048
S = 4096
T = 512
NT = S // T
KC = D // 128
DFF = 5632
FC = DFF // 128
DEPTH = 2
NCORES = 8
EPS = 1e-6
INV2PI = float(np.float32(1.0 / (2.0 * np.pi)))
TWOPI_S = 6.28318
MAGIC = 12582912.0
SC_A = 128.0 ** -0.5
SC_B = 192.0 ** -0.5

DBG_B = None
MLA_HEADS = 8
MLA_KB = 32
STAGES = None


class Buf:
    __slots__ = ("w", "r")

    def __init__(self):
        self.w = {}
        self.r = {}


class TL:
    def __init__(self, t, n=1):
        self.t = t
        self.b = [Buf() for _ in range(n)]


class Ring:
    def __init__(self, tiles):
        self.t = tiles
        self.i = 0

    def next(self):
        x = self.t[self.i % len(self.t)]
        self.i += 1
        return x


ENG = ("pe", "act", "dve", "pool", "sp")


class Prog:
    def __init__(self, nc, es):
        self.nc = nc
        self.es = es
        self.sem = {}
        self.cnt = {}
        self.ops = {e: [] for e in ENG}
        self.known = {e: {} for e in ENG}
        for k in ("pe", "act", "dve", "pool"):
            self.newsem(k)
        self.rings = {}
        for q, n in (("sp", 16), ("pool", 8), ("act", 8)):
            self.rings[q] = [self.newsem("d%s%d" % (q, i)) for i in range(n)]
        self.ri = {q: 0 for q in self.rings}

    def newsem(self, k):
        self.sem[k] = self.es.enter_context(self.nc.semaphore(k))
        self.cnt[k] = 0
        return k

    def _deps(self, eng, reads, writes, extra=None):
        w = {}

        def acc(d):
            for k, v in d.items():
                if w.get(k, 0) < v:
                    w[k] = v

        for b in reads:
            acc(b.w)
        for b in writes:
            acc(b.w)
            acc(b.r)
        if extra:
            acc(extra)
        kn = self.known[eng]
        need = []
        for k, v in w.items():
            if eng == "pe" and k == "pe":
                continue
            if kn.get(k, 0) < v:
                kn[k] = v
                need.append((k, v))
        return need

    def _mark(self, key, val, reads, writes):
        for b in reads:
            if b.r.get(key, 0) < val:
                b.r[key] = val
        for b in writes:
            b.w = {key: val}
            b.r = {}

    def op(self, eng, fn, reads=(), writes=()):
        need = self._deps(eng, reads, writes)
        self.cnt[eng] += 1
        val = self.cnt[eng]
        self.ops[eng].append((need, fn, eng, 1))
        self._mark(eng, val, reads, writes)

    def dma(self, q, fn, reads=(), writes=()):
        ring = self.rings[q]
        key = ring[self.ri[q] % len(ring)]
        self.ri[q] += 1
        extra = {key: self.cnt[key]} if self.cnt[key] else None
        need = self._deps(q, reads, writes, extra)
        self.cnt[key] += 16
        val = self.cnt[key]
        self.ops[q].append((need, fn, key, 16))
        self._mark(key, val, reads, writes)

    def flush(self, drain=True):
        if drain:
            need = []
            for q in self.rings:
                for k in self.rings[q]:
                    if self.cnt[k] and self.known["sp"].get(k, 0) < self.cnt[k]:
                        self.known["sp"][k] = self.cnt[k]
                        need.append((k, self.cnt[k]))
            if need:
                self.ops["sp"].append((need, None, None, 0))
        with self.nc.Block() as blk:
            for eng, reg in (("pe", blk.tensor), ("act", blk.scalar), ("dve", blk.vector),
                             ("pool", blk.gpsimd), ("sp", blk.sync)):
                lst = self.ops[eng]
                if not lst:
                    continue

                def body(e, lst=lst):
                    for need, fn, key, inc in lst:
                        for k, v in need:
                            e.wait_ge(self.sem[k], v)
                        if fn is not None:
                            fn(e).then_inc(self.sem[key], inc)

                reg(body)
                self.ops[eng] = []


def mm_group(P, out_ps, pairs, reads, writes):
    n = len(pairs)

    def fn(e):
        ins = None
        for i, (l, r) in enumerate(pairs):
            ins = e.matmul(out_ps, l, r, start=(i == 0), stop=(i == n - 1))
        return ins

    P.op("pe", fn, reads, writes)


OFF_QA, OFF_KA, OFF_VA, OFF_CQ, OFF_CKV, OFF_KR, OFF_GATE = 0, 1024, 1280, 1536, 2048, 2560, 2624
IN_QA = 0
IN_QASW = 8
IN_KA = 16
IN_KASW = 18
IN_CQ = 20
IN_CKV = 24
IN_KR = 28
IN_KRSW = 29
IN_GATE = 30
IN_N = 62


def w_in_segments(j):
    if j < IN_QASW:
        return [(0, OFF_QA + 128 * j, 128)]
    if j < IN_KA:
        h = j - IN_QASW
        return [(0, OFF_QA + 128 * h + 64, 64), (64, OFF_QA + 128 * h, 64)]
    if j < IN_KASW:
        return [(0, OFF_KA + 128 * (j - IN_KA), 128)]
    if j < IN_CQ:
        h = j - IN_KASW
        return [(0, OFF_KA + 128 * h + 64, 64), (64, OFF_KA + 128 * h, 64)]
    if j < IN_CKV:
        return [(0, OFF_CQ + 128 * (j - IN_CQ), 128)]
    if j < IN_KR:
        return [(0, OFF_CKV + 128 * (j - IN_CKV), 128)]
    if j == IN_KR:
        return [(0, OFF_KR, 64), (64, OFF_KR, 64)]
    if j == IN_KRSW:
        return [(0, OFF_KR + 32, 32), (32, OFF_KR, 32), (64, OFF_KR + 32, 32), (96, OFF_KR, 32)]
    return [(0, OFF_GATE + 128 * (j - IN_GATE), 128)]


UQ_QN, UQ_QR, UQ_QRSW, UQ_N = 0, 8, 12, 16


def w_uq_segments(j):
    if j < UQ_QR:
        return [(0, 192 * j, 128)]
    if j < UQ_QRSW:
        p = j - UQ_QR
        return [(0, 192 * (2 * p) + 128, 64), (64, 192 * (2 * p + 1) + 128, 64)]
    p = j - UQ_QRSW
    a = 192 * (2 * p) + 128
    b = 192 * (2 * p + 1) + 128
    return [(0, a + 32, 32), (32, a, 32), (64, b + 32, 32), (96, b, 32)]


def build_nc():
    nc = bass.Bass("TRN2", target_bir_lowering=False)

    declared = []
    nc._declared_inputs = declared

    def run(name):
        if STAGES is None:
            return True
        if name in STAGES:
            return True
        if name in ("tables", "mixerA", "mixerB") and "mixer" in STAGES:
            return True
        if name == "mixer":
            return any(k in STAGES for k in ("tables", "mixerA", "mixerB"))
        return False

    def din(name, shape, dt=F32):
        declared.append(name)
        return nc.dram_tensor(name, list(shape), dt, kind="ExternalInput").ap()

    def dscr(name, shape, dt):
        return nc.dram_tensor(name, list(shape), dt, kind="Internal").ap()

    xT = din("xT", [D, S])
    cT = din("cT", [128, KC])
    posb = din("posb", [128, S], I32)
    NVEC = DEPTH * (48 + 144 + 32 + 4 + 4) + 16
    vecs_d = din("vecs", [128, NVEC])
    consts_d = din("consts", [128, 4])
    masks_d = din("masks", [128, 2, 512])
    sinkrow_d = din("sinkrow", [1, DEPTH * 2 * 512])
    w_ada = din("w_ada", [DEPTH, D, 9 * D])
    w_f_gu = [din("w_ffn%d_gu" % (f + 1), [DEPTH, D, 2 * DFF]) if run("ffn%d" % (f + 1)) else None for f in range(2)]
    w_f_d = [din("w_ffn%d_d" % (f + 1), [DEPTH, DFF, D]) if run("ffn%d" % (f + 1)) else None for f in range(2)]
    w_in = din("w_in", [DEPTH, D, 6720])
    w_uq = din("w_uq", [DEPTH, 512, 1536])
    w_ukv = din("w_ukv", [DEPTH, 512, 2048])
    w_oa = din("w_oa", [DEPTH, 1024, D])
    w_ob = din("w_ob", [DEPTH, 1024, D])
    w_out = din("w_out", [DEPTH, D, D])
    outT = nc.dram_tensor("outT", [D, S], F32, kind="ExternalOutput").ap()

    xres = dscr("xres", [D, S], F32)
    ada_b = [[dscr("ada_b%d_%d" % (l, j), [48, 128, KC, 128], BF16) for j in range(3)] for l in range(DEPTH)]
    tabs_d = dscr("tabs", [4, 128, S], F32)
    gu_b = [[dscr("gu_b%d_%d" % (l, f), [FC, 128, KC, 256], BF16) for f in range(2)] for l in range(DEPTH)]
    d_b = [[dscr("d_b%d_%d" % (l, f), [KC, 128, FC, 128], BF16) for f in range(2)] for l in range(DEPTH)]
    in_b = [dscr("in_b%d" % l, [IN_N, 128, KC, 128], BF16) for l in range(DEPTH)]
    va_b = [dscr("va_b%d" % l, [128, KC, 256], BF16) for l in range(DEPTH)]
    uq_b = [dscr("uq_b%d" % l, [UQ_N, 128, 4, 128], BF16) for l in range(DEPTH)]
    kn_b = [dscr("kn_b%d" % l, [8, 128, 4, 128], BF16) for l in range(DEPTH)]
    vbw_b = [dscr("vbw_b%d" % l, [128, 4, 1024], BF16) for l in range(DEPTH)]
    oa_b = [dscr("oa_b%d" % l, [KC, 128, 8, 128], BF16) for l in range(DEPTH)]
    ob_b = [dscr("ob_b%d" % l, [KC, 128, 8, 128], BF16) for l in range(DEPTH)]
    out_b = [dscr("out_b%d" % l, [KC, 128, KC, 128], BF16) for l in range(DEPTH)]
    kaT_d = dscr("kaT_d", [2, 128, S], BF16)
    va_d = dscr("va_d", [32, 128, 256], BF16)
    knT_d = dscr("knT_d", [8, 128, S], BF16)
    krT_d = dscr("krT_d", [128, S], BF16)
    vb_d = dscr("vb_d", [8, 128, 32, 128], BF16)

    with ExitStack() as es:
        P = Prog(nc, es)

        def psb(name, shape, dt):
            return es.enter_context(nc.sbuf_tensor(name, list(shape), dt))

        vecs = TL(psb("vecs_sb", [128, NVEC], F32))
        modv = TL(psb("modv", [128, DEPTH * 9 * KC], F32))
        ones_bf = TL(psb("ones_bf", [128, 128], BF16))
        ones_f = TL(psb("ones_f", [1, 128], F32))
        consts = TL(psb("consts_sb", [128, 4], F32))

        def vcol(l, kind, c):
            base = l * 232
            if kind == "ng":
                return base + c
            if kind == "ada":
                return base + 48 + c
            if kind == "bg":
                return base + 192 + c
            if kind == "gcq":
                return base + 224 + c
            if kind == "gckv":
                return base + 228 + c
            raise ValueError

        GFIN = DEPTH * 232

        def mcol(l, j, c):
            return (l * 9 + j) * KC + c

        xres_b = [[Buf() for _ in range(KC)] for _ in range(NT)]
        ada_bufs = [[[Buf() for _ in range(48)] for _ in range(3)] for _ in range(DEPTH)]
        tabs_buf = Buf()
        gu_bufs = [[[(Buf(), Buf()) for _ in range(FC)] for _ in range(2)] for _ in range(DEPTH)]
        d_bufs = [[[Buf() for _ in range(KC)] for _ in range(2)] for _ in range(DEPTH)]
        in_bufs = [[Buf() for _ in range(IN_N)] for _ in range(DEPTH)]
        va_bufs = [Buf() for _ in range(DEPTH)]
        uq_bufs = [[Buf() for _ in range(UQ_N)] for _ in range(DEPTH)]
        kn_bufs = [[Buf() for _ in range(8)] for _ in range(DEPTH)]
        vbw_bufs = [Buf() for _ in range(DEPTH)]
        oa_bufs = [[Buf() for _ in range(KC)] for _ in range(DEPTH)]
        ob_bufs = [[Buf() for _ in range(KC)] for _ in range(DEPTH)]
        out_bufs = [[Buf() for _ in range(KC)] for _ in range(DEPTH)]
        kaT_buf = [[Buf() for _ in range(2)] for _ in range(NT)]
        va_buf = [[Buf() for _ in range(4)] for _ in range(NT)]
        knT_buf = [[Buf() for _ in range(8)] for _ in range(NT)]
        krT_buf = [Buf() for _ in range(NT)]
        vb_buf = [[Buf() for _ in range(4)] for _ in range(NT)]

        def conv(dst, src, wb):
            P.dma("pool", lambda e, dst=dst, src=src: e.dma_start(out=dst, in_=src), writes=[wb])

        def conv_ffn(l, f):
            wv = w_f_gu[f][l].rearrange("(k p) f -> p k f", p=128)
            for t in range(FC):
                for half in range(2):
                    conv(gu_b[l][f][t, :, :, half * 128:(half + 1) * 128],
                         wv[:, :, half * DFF + t * 128: half * DFF + (t + 1) * 128], gu_bufs[l][f][t][half])
            dv = w_f_d[f][l].rearrange("(k p) f -> p k f", p=128)
            for r in range(KC):
                conv(d_b[l][f][r], dv[:, :, r * 128:(r + 1) * 128], d_bufs[l][f][r])

        def conv_mixer(l):
            wv = w_in[l].rearrange("(k p) f -> p k f", p=128)
            for j in range(IN_N):
                segs = w_in_segments(j)
                if len(segs) == 1:
                    conv(in_b[l][j], wv[:, :, segs[0][1]:segs[0][1] + 128], in_bufs[l][j])
                else:
                    bl = []
                    for (dc, sc, n) in segs:
                        b = Buf()
                        conv(in_b[l][j][:, :, dc:dc + n], wv[:, :, sc:sc + n], b)
                        bl.append(b)
                    jb = in_bufs[l][j]
                    for b in bl:
                        for k, v in b.w.items():
                            if jb.w.get(k, 0) < v:
                                jb.w[k] = v
            conv(va_b[l], wv[:, :, OFF_VA:OFF_VA + 256], va_bufs[l])
            uv = w_uq[l].rearrange("(k p) f -> p k f", p=128)
            for j in range(UQ_N):
                bl = []
                for (dc, sc, n) in w_uq_segments(j):
                    b = Buf()
                    conv(uq_b[l][j][:, :, dc:dc + n], uv[:, :, sc:sc + n], b)
                    bl.append(b)
                jb = uq_bufs[l][j]
                for b in bl:
                    for k, v in b.w.items():
                        if jb.w.get(k, 0) < v:
                            jb.w[k] = v
            kv = w_ukv[l].rearrange("(k p) f -> p k f", p=128)
            for h in range(8):
                conv(kn_b[l][h], kv[:, :, 256 * h:256 * h + 128], kn_bufs[l][h])
            bl = []
            for h in range(8):
                b = Buf()
                conv(vbw_b[l][:, :, 128 * h:128 * h + 128], kv[:, :, 256 * h + 128:256 * h + 256], b)
                bl.append(b)
            for b in bl:
                for k, v in b.w.items():
                    if vbw_bufs[l].w.get(k, 0) < v:
                        vbw_bufs[l].w[k] = v
            av = w_oa[l].rearrange("(k p) f -> p k f", p=128)
            bv = w_ob[l].rearrange("(k p) f -> p k f", p=128)
            ov = w_out[l].rearrange("(k p) f -> p k f", p=128)
            for c in range(KC):
                conv(oa_b[l][c], av[:, :, c * 128:(c + 1) * 128], oa_bufs[l][c])
                conv(ob_b[l][c], bv[:, :, c * 128:(c + 1) * 128], ob_bufs[l][c])
                conv(out_b[l][c], ov[:, :, c * 128:(c + 1) * 128], out_bufs[l][c])

        def conv_ada(l, j):
            wav = w_ada[l].rearrange("(k p) f -> p k f", p=128)
            for ch in range(48):
                col = (48 * j + ch) * 128
                conv(ada_b[l][j][ch], wav[:, :, col:col + 128], ada_bufs[l][j][ch])

        conv_jobs = []
        for l in range(DEPTH):
            if run("ffn1"):
                conv_jobs.append(lambda l=l: (conv_ada(l, 0), conv_ffn(l, 0)))
            if run("mixer"):
                conv_jobs.append(lambda l=l: (conv_ada(l, 1), conv_mixer(l)))
            if run("ffn2"):
                conv_jobs.append(lambda l=l: (conv_ada(l, 2), conv_ffn(l, 1)))
        conv_pos = [0]

        def emit_next_conv():
            if conv_pos[0] < len(conv_jobs):
                conv_jobs[conv_pos[0]]()
                conv_pos[0] += 1

        emit_next_conv()

        cact = TL(psb("cact", [128, KC], BF16))
        ada_ctr = [0]

        def adaln_ops(l, j, sb, ps):
            ada_ctr[0] += 1
            pf = "ad%d_" % ada_ctr[0]
            wts = [TL(sb(pf + "w%d" % i, [128, 8, KC, 128], BF16)) for i in range(3)]
            mps = TL(ps.enter_context(nc.psum_tensor(pf + "mps", [128, 512], F32)))
            modraw = TL(sb(pf + "modraw", [128, 48], F32))
            tmp16 = TL(sb(pf + "tmp16", [128, KC], F32))
            for g in range(6):
                wt = wts[g % 3]
                P.dma("sp", lambda e, wt=wt, g=g: e.dma_start(out=wt.t[:], in_=ada_b[l][j][8 * g:8 * g + 8].rearrange("c p k f -> p c k f")),
                      reads=ada_bufs[l][j][8 * g:8 * g + 8], writes=wt.b)
                for cc in range(8):
                    ch = 8 * g + cc
                    mm_group(P, mps.t[:, ch:ch + 1], [(wt.t[:, cc, k, :], cact.t[:, k:k + 1]) for k in range(KC)], reads=wt.b + cact.b, writes=mps.b)
            c0 = vcol(l, "ada", 48 * j)
            P.op("dve", lambda e: e.tensor_tensor(out=modraw.t[:], in0=mps.t[:, 0:48], in1=vecs.t[:, c0:c0 + 48], op=ALU.add),
                 reads=mps.b + vecs.b, writes=modraw.b)
            sh = modraw.t[:, 0:KC]
            sc = modraw.t[:, KC:2 * KC]
            gt = modraw.t[:, 2 * KC:3 * KC]
            g0 = vcol(l, "ng", j * KC)
            a0 = mcol(l, 3 * j, 0)
            P.op("dve", lambda e: e.tensor_scalar(out=tmp16.t[:], in0=sc, scalar1=1.0, scalar2=None, op0=ALU.add), reads=modraw.b, writes=tmp16.b)
            P.op("dve", lambda e: e.tensor_tensor(out=modv.t[:, a0:a0 + KC], in0=tmp16.t[:], in1=vecs.t[:, g0:g0 + KC], op=ALU.mult),
                 reads=tmp16.b + vecs.b, writes=modv.b)
            P.op("dve", lambda e: e.tensor_copy(out=modv.t[:, a0 + KC:a0 + 2 * KC], in_=sh), reads=modraw.b, writes=modv.b)
            fac = 1.0 if j == 1 else 0.5
            P.op("dve", lambda e: e.tensor_scalar(out=modv.t[:, a0 + 2 * KC:a0 + 3 * KC], in0=gt, scalar1=fac, scalar2=None, op0=ALU.mult),
                 reads=modraw.b, writes=modv.b)

        def adaln_phase(l, j):
            with ExitStack() as ps:
                def sb(name, shape, dt):
                    return ps.enter_context(nc.sbuf_tensor(name, list(shape), dt))
                adaln_ops(l, j, sb, ps)
                P.flush()

        class BgAda:
            def __init__(self, l, j, sb, bank_fn):
                self.l, self.j, self.bank_fn = l, j, bank_fn
                self.ring = [TL(sb("bga%d" % k, [128, KC, 128], BF16)) for k in range(4)]
                self.modraw = TL(sb("bgmodraw", [128, 48], F32))
                self.tmp16 = TL(sb("bgtmp16", [128, KC], F32))
                self.nl = 0
                self.nm = 0
                self._load(2)

            def _load(self, n):
                l, j = self.l, self.j
                for _ in range(n):
                    ch = self.nl
                    if ch >= 48:
                        return
                    wt = self.ring[ch % 4]
                    P.dma("sp", lambda e, wt=wt, ch=ch: e.dma_start(out=wt.t[:], in_=ada_b[l][j][ch]), reads=[ada_bufs[l][j][ch]], writes=wt.b)
                    self.nl += 1

            def step(self):
                if self.nm >= 48:
                    return
                bank = self.bank_fn()
                modraw = self.modraw
                ch0 = self.nm
                n = 0
                for _ in range(2):
                    ch = self.nm
                    if ch >= 48:
                        break
                    wt = self.ring[ch % 4]
                    mm_group(P, bank.t[:, n:n + 1], [(wt.t[:, k, :], cact.t[:, k:k + 1]) for k in range(KC)], reads=wt.b + cact.b, writes=bank.b)
                    self.nm += 1
                    n += 1
                P.op("dve", lambda e, bank=bank, ch0=ch0, n=n: e.tensor_copy(out=modraw.t[:, ch0:ch0 + n], in_=bank.t[:, 0:n]),
                     reads=bank.b, writes=modraw.b)
                self._load(2)

            def finish(self):
                while self.nm < 48:
                    self.step()
                l, j, modraw, tmp16 = self.l, self.j, self.modraw, self.tmp16
                c0 = vcol(l, "ada", 48 * j)
                P.op("dve", lambda e: e.tensor_tensor(out=modraw.t[:], in0=modraw.t[:], in1=vecs.t[:, c0:c0 + 48], op=ALU.add),
                     reads=modraw.b + vecs.b, writes=modraw.b)
                sh = modraw.t[:, 0:KC]
                sc = modraw.t[:, KC:2 * KC]
                gt = modraw.t[:, 2 * KC:3 * KC]
                g0 = vcol(l, "ng", j * KC)
                a0 = mcol(l, 3 * j, 0)
                P.op("dve", lambda e: e.tensor_scalar(out=tmp16.t[:], in0=sc, scalar1=1.0, scalar2=None, op0=ALU.add), reads=modraw.b, writes=tmp16.b)
                P.op("dve", lambda e: e.tensor_tensor(out=modv.t[:, a0:a0 + KC], in0=tmp16.t[:], in1=vecs.t[:, g0:g0 + KC], op=ALU.mult),
                     reads=tmp16.b + vecs.b, writes=modv.b)
                P.op("dve", lambda e: e.tensor_copy(out=modv.t[:, a0 + KC:a0 + 2 * KC], in_=sh), reads=modraw.b, writes=modv.b)
                fac = 1.0 if j == 1 else 0.5
                P.op("dve", lambda e: e.tensor_scalar(out=modv.t[:, a0 + 2 * KC:a0 + 3 * KC], in0=gt, scalar1=fac, scalar2=None, op0=ALU.mult),
                     reads=modraw.b, writes=modv.b)

        stage_of = {0: "ffn1", 1: "mixer", 2: "ffn2"}
        pieces = [(l, j) for l in range(DEPTH) for j in range(3) if run(stage_of[j])]
        first_piece = pieces[0][1]

        with ExitStack() as ps:
            def sb(name, shape, dt):
                return ps.enter_context(nc.sbuf_tensor(name, list(shape), dt))

            P.dma("sp", lambda e: e.dma_start(out=vecs.t[:], in_=vecs_d[:, :]), writes=vecs.b)
            P.dma("sp", lambda e: e.dma_start(out=consts.t[:], in_=consts_d[:, :]), writes=consts.b)
            P.op("dve", lambda e: e.memset(ones_bf.t[:], 1.0), writes=ones_bf.b)
            P.op("dve", lambda e: e.memset(ones_f.t[:], 1.0), writes=ones_f.b)
            cin = TL(sb("cin", [128, KC], F32))
            P.dma("sp", lambda e: e.dma_start(out=cin.t[:], in_=cT[:, :]), writes=cin.b)
            P.op("act", lambda e: e.activation(out=cact.t[:], in_=cin.t[:], func=AF.Silu), reads=cin.b, writes=cact.b)
            if run("tables"):
                HS = S // 2
                pos_i = TL(sb("pos_i", [128, HS], I32))
                posf = TL(sb("posf", [128, HS], F32))
                u = TL(sb("u", [128, HS], F32))
                t1 = TL(sb("t1", [128, HS], F32))
                kk = TL(sb("kk", [128, HS], F32))
                tb = [TL(sb("tb%d" % i, [128, HS], F32)) for i in range(2)]
                n = 0
                for hf in range(2):
                    P.dma("sp", lambda e, hf=hf: e.dma_start(out=pos_i.t[:], in_=posb[:, hf * HS:(hf + 1) * HS]), writes=pos_i.b)
                    P.op("dve", lambda e: e.tensor_copy(out=posf.t[:], in_=pos_i.t[:]), reads=pos_i.b, writes=posf.b)
                    for fam in range(2):
                        for cs in range(2):
                            P.op("dve", lambda e, fam=fam, cs=cs: e.tensor_scalar(
                                out=u.t[:], in0=posf.t[:], scalar1=consts.t[:, fam:fam + 1], scalar2=INV2PI, op0=ALU.mult, op1=ALU.mult),
                                reads=posf.b + consts.b, writes=u.b)
                            if cs == 0:
                                P.op("dve", lambda e: e.tensor_scalar(out=u.t[:], in0=u.t[:], scalar1=0.25, scalar2=None, op0=ALU.add),
                                     reads=u.b, writes=u.b)
                            P.op("dve", lambda e: e.tensor_scalar(out=t1.t[:], in0=u.t[:], scalar1=MAGIC, scalar2=None, op0=ALU.add),
                                 reads=u.b, writes=t1.b)
                            P.op("dve", lambda e: e.tensor_scalar(out=kk.t[:], in0=t1.t[:], scalar1=MAGIC, scalar2=None, op0=ALU.subtract),
                                 reads=t1.b, writes=kk.b)
                            P.op("dve", lambda e: e.tensor_tensor(out=u.t[:], in0=u.t[:], in1=kk.t[:], op=ALU.subtract),
                                 reads=u.b + kk.b, writes=u.b)
                            ot = tb[n % 2]
                            if cs == 0:
                                P.op("act", lambda e, ot=ot: e.activation(out=ot.t[:], in_=u.t[:], func=AF.Sin, scale=TWOPI_S),
                                     reads=u.b, writes=ot.b)
                            else:
                                P.op("act", lambda e, ot=ot, fam=fam: e.activation(out=ot.t[:], in_=u.t[:], func=AF.Sin, scale=consts.t[:, 2 + fam:3 + fam]),
                                     reads=u.b + consts.b, writes=ot.b)
                            ti = fam * 2 + cs
                            bb = Buf()
                            P.dma("sp", lambda e, ot=ot, ti=ti, hf=hf: e.dma_start(out=tabs_d[ti][:, hf * HS:(hf + 1) * HS], in_=ot.t[:]), reads=ot.b, writes=[bb])
                            for k, v in bb.w.items():
                                tabs_buf.w[k] = max(tabs_buf.w.get(k, 0), v)
                            n += 1
            adaln_ops(0, first_piece, sb, ps)
            P.flush()

        class Env:
            pass

        phase_ctr = [0]

        def make_env(ps, nxr=6, ntt=4, nxo=3):
            phase_ctr[0] += 1
            pfx = "p%d_" % phase_ctr[0]

            def sb(name, shape, dt):
                return ps.enter_context(nc.sbuf_tensor(pfx + name, list(shape), dt))

            E = Env()
            E.sb = sb
            E.xr = Ring([TL(sb("xr%d" % i, [128, T], F32)) for i in range(nxr)])
            E.sq = Ring([TL(sb("sq%d" % i, [128, T], BF16)) for i in range(2)])
            E.tt = Ring([TL(sb("tt%d" % i, [128, T], F32)) for i in range(ntt)])
            E.xo = Ring([TL(sb("xo%d" % i, [128, T], F32)) for i in range(nxo)])
            E.lnv = TL(sb("lnv", [128, T], F32))
            E.rstd = TL(sb("rstd", [128, T], F32))
            E.psum = [TL(ps.enter_context(nc.psum_tensor(pfx + "ps%d" % i, [128, 512], F32))) for i in range(8)]
            return E

        def ld_x(E, src_ap, src_bufs, i, c):
            xs = E.xr.next()
            rb = [src_bufs[i][c]] if src_bufs is not None else []
            P.dma("sp", lambda e, xs=xs: e.dma_start(out=xs.t[:], in_=src_ap[c * 128:(c + 1) * 128, i * T:(i + 1) * T]),
                  reads=rb, writes=xs.b)
            return xs

        def rstd_from(E, ss, n_feat):
            P.op("act", lambda e: e.activation(out=E.lnv.t[:], in_=ss.t[:], func=AF.Ln, bias=EPS, scale=1.0 / n_feat),
                 reads=ss.b, writes=E.lnv.b)
            P.op("act", lambda e: e.activation(out=E.rstd.t[:], in_=E.lnv.t[:], func=AF.Exp, scale=-0.5),
                 reads=E.lnv.b, writes=E.rstd.b)

        class _XV:
            def __init__(self, ap, b):
                self.ap = ap
                self.b = b

        def emit_norm(E, i, src_ap, src_bufs, ss, hb, acol0, bcol0, xpre=None):
            if xpre is not None:
                for c in range(KC):
                    s = E.sq.next()
                    P.op("act", lambda e, c=c, s=s: e.activation(out=s.t[:], in_=xpre.t[:, c, :], func=AF.Square), reads=[xpre.b[c]], writes=s.b)
                    P.op("pe", lambda e, s=s, c=c: e.matmul(ss.t[:], ones_bf.t[:], s.t[:], start=(c == 0), stop=(c == KC - 1)),
                         reads=s.b + ones_bf.b, writes=ss.b)
                rstd_from(E, ss, D)
                for c in range(KC):
                    t = E.tt.next()
                    P.op("dve", lambda e, t=t, c=c: e.scalar_tensor_tensor(
                        out=t.t[:], in0=xpre.t[:, c, :], scalar=modv.t[:, acol0 + c:acol0 + c + 1], in1=E.rstd.t[:], op0=ALU.mult, op1=ALU.mult),
                        reads=[xpre.b[c]] + modv.b + E.rstd.b, writes=t.b)
                    P.op("act", lambda e, t=t, c=c: e.activation(out=hb.t[:, c, :], in_=t.t[:], func=AF.Identity,
                                                                 bias=modv.t[:, bcol0 + c:bcol0 + c + 1], scale=1.0),
                         reads=t.b + modv.b, writes=[hb.b[c]])
                return
            for c in range(KC):
                xs = ld_x(E, src_ap, src_bufs, i, c)
                s = E.sq.next()
                P.op("act", lambda e, xs=xs, s=s: e.activation(out=s.t[:], in_=xs.t[:], func=AF.Square), reads=xs.b, writes=s.b)
                P.op("pe", lambda e, s=s, c=c: e.matmul(ss.t[:], ones_bf.t[:], s.t[:], start=(c == 0), stop=(c == KC - 1)),
                     reads=s.b + ones_bf.b, writes=ss.b)
            rstd_from(E, ss, D)
            for c in range(KC):
                xs = ld_x(E, src_ap, src_bufs, i, c)
                t = E.tt.next()
                P.op("dve", lambda e, xs=xs, t=t, c=c: e.scalar_tensor_tensor(
                    out=t.t[:], in0=xs.t[:], scalar=modv.t[:, acol0 + c:acol0 + c + 1], in1=E.rstd.t[:], op0=ALU.mult, op1=ALU.mult),
                    reads=xs.b + modv.b + E.rstd.b, writes=t.b)
                P.op("act", lambda e, t=t, c=c: e.activation(out=hb.t[:, c, :], in_=t.t[:], func=AF.Identity,
                                                             bias=modv.t[:, bcol0 + c:bcol0 + c + 1], scale=1.0),
                     reads=t.b + modv.b, writes=[hb.b[c]])

        def ffn_phase(l, f, src_ap, src_bufs, bgp=None):
            with ExitStack() as ps:
                emit_next_conv()
                E = make_env(ps, nxr=4)
                sb = E.sb
                h = [TL(sb("h%d" % i, [128, KC, T], BF16), KC) for i in range(2)]
                a = TL(sb("a", [128, FC, T], BF16), FC)
                NW = 4
                wgu = [TL(sb("wgu%d" % i, [128, KC, 256], BF16)) for i in range(NW)]
                wd = [TL(sb("wd%d" % i, [128, FC, 128], BF16)) for i in range(NW)]
                ss = E.psum[0]
                pgu = [(E.psum[1], E.psum[2]), (E.psum[3], E.psum[4])]
                pyr = Ring([E.psum[5], E.psum[6], E.psum[7]])
                bg = BgAda(bgp[0], bgp[1], sb, pyr.next) if bgp is not None else None
                jm = 3 * (0 if f == 0 else 2)
                acol0, bcol0, gcol0 = mcol(l, jm, 0), mcol(l, jm + 1, 0), mcol(l, jm + 2, 0)

                seq = []
                for i in range(NT):
                    seq += [("gu", t) for t in range(FC)] + [("d", r) for r in range(KC)]
                cnt = {"gu": 0, "d": 0}
                slot_of = {}

                def issue(u):
                    if u >= len(seq):
                        return
                    kind, idx = seq[u]
                    n = cnt[kind]
                    cnt[kind] += 1
                    if kind == "gu":
                        wt = wgu[n % NW]
                        P.dma("sp", lambda e, wt=wt, idx=idx: e.dma_start(out=wt.t[:], in_=gu_b[l][f][idx]),
                              reads=list(gu_bufs[l][f][idx]), writes=wt.b)
                    else:
                        wt = wd[n % NW]
                        P.dma("sp", lambda e, wt=wt, idx=idx: e.dma_start(out=wt.t[:], in_=d_b[l][f][idx]),
                              reads=[d_bufs[l][f][idx]], writes=wt.b)
                    slot_of[u] = wt

                LOOK = 3
                for u in range(LOOK):
                    issue(u)
                upos = [0]

                def next_w():
                    u = upos[0]
                    upos[0] += 1
                    issue(u + LOOK)
                    return slot_of.pop(u)

                emit_norm(E, 0, src_ap, src_bufs, ss, h[0], acol0, bcol0)
                npair = 0
                ny = 0
                for i in range(NT):
                    hb = h[i % 2]
                    for t in range(FC):
                        wt = next_w()
                        pg, pu = pgu[npair % 2]
                        npair += 1
                        mm_group(P, pg.t[:], [(wt.t[:, k, 0:128], hb.t[:, k, :]) for k in range(KC)], reads=wt.b + hb.b, writes=pg.b)
                        mm_group(P, pu.t[:], [(wt.t[:, k, 128:256], hb.t[:, k, :]) for k in range(KC)], reads=wt.b + hb.b, writes=pu.b)
                        sg = E.tt.next()
                        P.op("act", lambda e, sg=sg, pg=pg: e.activation(out=sg.t[:], in_=pg.t[:], func=AF.Silu), reads=pg.b, writes=sg.b)
                        P.op("dve", lambda e, sg=sg, pu=pu, t=t: e.tensor_tensor(out=a.t[:, t, :], in0=sg.t[:], in1=pu.t[:], op=ALU.mult),
                             reads=sg.b + pu.b, writes=[a.b[t]])
                        if bg is not None and t in (10, 25, 40):
                            bg.step()
                    if i + 1 < NT:
                        emit_norm(E, i + 1, src_ap, src_bufs, ss, h[(i + 1) % 2], acol0, bcol0)
                    for r in range(KC):
                        wt = next_w()
                        yp = pyr.next()
                        mm_group(P, yp.t[:], [(wt.t[:, k, :], a.t[:, k, :]) for k in range(FC)], reads=wt.b + a.b, writes=yp.b)
                        xs = ld_x(E, src_ap, src_bufs, i, r)
                        xo = E.xo.next()
                        P.op("dve", lambda e, xo=xo, yp=yp, xs=xs, r=r: e.scalar_tensor_tensor(
                            out=xo.t[:], in0=yp.t[:], scalar=modv.t[:, gcol0 + r:gcol0 + r + 1], in1=xs.t[:], op0=ALU.mult, op1=ALU.add),
                            reads=yp.b + xs.b + modv.b, writes=xo.b)
                        P.dma("sp", lambda e, xo=xo, r=r, i=i: e.dma_start(out=xres[r * 128:(r + 1) * 128, i * T:(i + 1) * T], in_=xo.t[:]),
                              reads=xo.b, writes=[xres_b[i][r]])
                if bg is not None:
                    bg.finish()
                P.flush()

        def final_phase(src_ap, src_bufs):
            out_bufs_l = []
            with ExitStack() as ps:
                E = make_env(ps)
                ss = E.psum[0]
                xt = TL(E.sb("xt", [128, KC, T], F32), KC)
                for i in range(NT):
                    for c in range(KC):
                        rb = [src_bufs[i][c]] if src_bufs is not None else []
                        P.dma("sp", lambda e, i=i, c=c: e.dma_start(out=xt.t[:, c, :], in_=src_ap[c * 128:(c + 1) * 128, i * T:(i + 1) * T]),
                              reads=rb, writes=[xt.b[c]])
                        s = E.sq.next()
                        P.op("act", lambda e, c=c, s=s: e.activation(out=s.t[:], in_=xt.t[:, c, :], func=AF.Square), reads=[xt.b[c]], writes=s.b)
                        P.op("pe", lambda e, s=s, c=c: e.matmul(ss.t[:], ones_bf.t[:], s.t[:], start=(c == 0), stop=(c == KC - 1)),
                             reads=s.b + ones_bf.b, writes=ss.b)
                    rstd_from(E, ss, D)
                    for c in range(KC):
                        xo = E.xo.next()
                        P.op("dve", lambda e, xo=xo, c=c: e.scalar_tensor_tensor(
                            out=xo.t[:], in0=xt.t[:, c, :], scalar=vecs.t[:, GFIN + c:GFIN + c + 1], in1=E.rstd.t[:], op0=ALU.mult, op1=ALU.mult),
                            reads=[xt.b[c]] + vecs.b + E.rstd.b, writes=xo.b)
                        ob = Buf()
                        out_bufs_l.append(ob)
                        P.dma("sp", lambda e, xo=xo, c=c, i=i: e.dma_start(out=outT[c * 128:(c + 1) * 128, i * T:(i + 1) * T], in_=xo.t[:]),
                              reads=xo.b, writes=[ob])
                P.flush()

        def rope3(E, p1, p2, tc, tsn, out_ap, out_bufs):
            t1 = E.tt.next()
            t2 = E.tt.next()
            P.op("dve", lambda e: e.tensor_tensor(out=t1.t[:], in0=p1.t[:], in1=tc.t[:], op=ALU.mult), reads=p1.b + tc.b, writes=t1.b)
            P.op("dve", lambda e: e.tensor_tensor(out=t2.t[:], in0=p2.t[:], in1=tsn.t[:], op=ALU.mult), reads=p2.b + tsn.b, writes=t2.b)
            P.op("dve", lambda e: e.tensor_tensor(out=out_ap, in0=t1.t[:], in1=t2.t[:], op=ALU.add), reads=t1.b + t2.b, writes=out_bufs)

        def load_tabs(tabs, i):
            for k in range(4):
                P.dma("sp", lambda e, k=k: e.dma_start(out=tabs[k].t[:], in_=tabs_d[k][:, i * T:(i + 1) * T]),
                      reads=[tabs_buf], writes=tabs[k].b)

        def mixer_pass_a(l, src_ap, src_bufs, bgp=None):
            with ExitStack() as ps:
                emit_next_conv()
                E = make_env(ps, nxr=4, ntt=4, nxo=1)
                sb = E.sb
                hh = [TL(sb("h%d" % k_, [128, KC, T], BF16), KC) for k_ in range(2)]
                win = {}
                for j in [IN_KA, IN_KA + 1, IN_KASW, IN_KASW + 1, IN_CKV, IN_CKV + 1, IN_CKV + 2, IN_CKV + 3, IN_KR, IN_KRSW]:
                    w = TL(sb("win%d" % j, [128, KC, 128], BF16))
                    P.dma("sp", lambda e, w=w, j=j: e.dma_start(out=w.t[:], in_=in_b[l][j]), reads=[in_bufs[l][j]], writes=w.b)
                    win[j] = w
                wva = TL(sb("wva", [128, KC, 256], BF16))
                P.dma("sp", lambda e: e.dma_start(out=wva.t[:], in_=va_b[l]), reads=[va_bufs[l]], writes=wva.b)
                wkn = []
                for hd in range(8):
                    w = TL(sb("wkn%d" % hd, [128, 4, 128], BF16))
                    P.dma("sp", lambda e, w=w, hd=hd: e.dma_start(out=w.t[:], in_=kn_b[l][hd]), reads=[kn_bufs[l][hd]], writes=w.b)
                    wkn.append(w)
                wvb = TL(sb("wvb", [128, 4, 1024], BF16))
                P.dma("sp", lambda e: e.dma_start(out=wvb.t[:], in_=vbw_b[l]), reads=[vbw_bufs[l]], writes=wvb.b)
                tabs = [TL(sb("tab%d" % k, [128, T], F32)) for k in range(4)]
                ckvf = TL(sb("ckvf", [128, 4, T], F32), 4)
                ckvn = TL(sb("ckvn", [128, 4, T], BF16), 4)
                stg = Ring([TL(sb("stg%d" % k, [128, T], BF16)) for k in range(4)])
                vstg = Ring([TL(sb("vstg%d" % k, [128, 1024], BF16), 2) for k in range(3)])
                ss = E.psum[0]
                pr = Ring(E.psum[1:8])
                bg = BgAda(bgp[0], bgp[1], sb, pr.next) if bgp is not None else None
                acol0, bcol0 = mcol(l, 3, 0), mcol(l, 4, 0)
                xset = TL(sb("xset", [128, KC, T], F32), KC)

                def load_xset(i_):
                    for c in range(KC):
                        rb = [src_bufs[i_][c]] if src_bufs is not None else []
                        P.dma("sp", lambda e, c=c, i_=i_: e.dma_start(out=xset.t[:, c, :], in_=src_ap[c * 128:(c + 1) * 128, i_ * T:(i_ + 1) * T]),
                              reads=rb, writes=[xset.b[c]])

                load_xset(0)
                emit_norm(E, 0, src_ap, src_bufs, ss, hh[0], acol0, bcol0, xpre=xset)
                for i in range(NT):
                    h = hh[i % 2]
                    if i + 1 < NT:
                        load_xset(i + 1)
                    load_tabs(tabs, i)

                    def proj(j, h=h):
                        p = pr.next()
                        w = win[j]
                        mm_group(P, p.t[:], [(w.t[:, k, :], h.t[:, k, :]) for k in range(KC)], reads=w.b + h.b, writes=p.b)
                        return p

                    for kv in range(2):
                        p1 = proj(IN_KA + kv)
                        p2 = proj(IN_KASW + kv)
                        o = stg.next()
                        rope3(E, p1, p2, tabs[0], tabs[1], o.t[:], o.b)
                        P.dma("sp", lambda e, o=o, kv=kv, i=i: e.dma_start(out=kaT_d[kv][:, i * T:(i + 1) * T], in_=o.t[:]),
                              reads=o.b, writes=[kaT_buf[i][kv]])
                    p1 = proj(IN_KR)
                    p2 = proj(IN_KRSW)
                    o = stg.next()
                    rope3(E, p1, p2, tabs[2], tabs[3], o.t[:], o.b)
                    P.dma("sp", lambda e, o=o, i=i: e.dma_start(out=krT_d[:, i * T:(i + 1) * T], in_=o.t[:]), reads=o.b, writes=[krT_buf[i]])
                    for j in range(4):
                        p = pr.next()
                        mm_group(P, p.t[:, 0:256], [(h.t[:, k, j * 128:(j + 1) * 128], wva.t[:, k, :]) for k in range(KC)],
                                 reads=wva.b + h.b, writes=p.b)
                        o = vstg.next()
                        P.op("act", lambda e, o=o, p=p: e.activation(out=o.t[:, 0:256], in_=p.t[:, 0:256], func=AF.Identity),
                             reads=p.b, writes=o.b)
                        P.dma("act", lambda e, o=o, i=i, j=j: e.dma_start(out=va_d[4 * i + j], in_=o.t[:, 0:256]), reads=o.b, writes=[va_buf[i][j]])
                    for j in range(4):
                        p = proj(IN_CKV + j)
                        P.op("act", lambda e, p=p, j=j: e.activation(out=ckvf.t[:, j, :], in_=p.t[:], func=AF.Identity), reads=p.b, writes=[ckvf.b[j]])
                        s = E.sq.next()
                        P.op("act", lambda e, s=s, j=j: e.activation(out=s.t[:], in_=ckvf.t[:, j, :], func=AF.Square), reads=[ckvf.b[j]], writes=s.b)
                        P.op("pe", lambda e, s=s, j=j: e.matmul(ss.t[:], ones_bf.t[:], s.t[:], start=(j == 0), stop=(j == 3)),
                             reads=s.b + ones_bf.b, writes=ss.b)
                    rstd_from(E, ss, 512)
                    for j in range(4):
                        gc = vcol(l, "gckv", j)
                        P.op("dve", lambda e, j=j, gc=gc: e.scalar_tensor_tensor(
                            out=ckvn.t[:, j, :], in0=ckvf.t[:, j, :], scalar=vecs.t[:, gc:gc + 1], in1=E.rstd.t[:], op0=ALU.mult, op1=ALU.mult),
                            reads=[ckvf.b[j]] + vecs.b + E.rstd.b, writes=[ckvn.b[j]])
                    if bg is not None:
                        bg.step()
                        bg.step()
                    if i + 1 < NT:
                        emit_norm(E, i + 1, src_ap, src_bufs, ss, hh[(i + 1) % 2], acol0, bcol0, xpre=xset)
                    if bg is not None:
                        bg.step()
                    for hd in range(8):
                        p = pr.next()
                        mm_group(P, p.t[:], [(wkn[hd].t[:, k, :], ckvn.t[:, k, :]) for k in range(4)], reads=wkn[hd].b + ckvn.b, writes=p.b)
                        o = stg.next()
                        P.op("act", lambda e, o=o, p=p: e.activation(out=o.t[:], in_=p.t[:], func=AF.Identity), reads=p.b, writes=o.b)
                        P.dma("act", lambda e, o=o, hd=hd, i=i: e.dma_start(out=knT_d[hd][:, i * T:(i + 1) * T], in_=o.t[:]),
                              reads=o.b, writes=[knT_buf[i][hd]])
                    for j in range(4):
                        o = vstg.next()
                        for half in range(2):
                            p = pr.next()
                            mm_group(P, p.t[:], [(ckvn.t[:, k, j * 128:(j + 1) * 128], wvb.t[:, k, half * 512:(half + 1) * 512]) for k in range(4)],
                                     reads=wvb.b + ckvn.b, writes=p.b)
                            P.op("act", lambda e, o=o, p=p, half=half: e.activation(out=o.t[:, half * 512:(half + 1) * 512], in_=p.t[:], func=AF.Identity),
                                 reads=p.b, writes=[o.b[half]])
                        P.dma("act", lambda e, o=o, i=i, j=j: e.dma_start(
                            out=vb_d[:, :, 4 * i + j, :].rearrange("h p d -> p h d"), in_=o.t[:].rearrange("p (h d) -> p h d", h=8)),
                            reads=o.b, writes=[vb_buf[i][j]])
                if bg is not None:
                    bg.finish()
                P.flush()

        def mixer_pass_b(l, src_ap, src_bufs):
            with ExitStack() as ps:
                E = make_env(ps, nxr=4, ntt=5, nxo=2)
                sb = E.sb
                hh = [TL(sb("h%d" % k_, [128, KC, T], BF16), KC) for k_ in range(2)]
                accE = TL(sb("accE", [128, T], F32))
                accO = TL(sb("accO", [128, T], F32))
                dhi = TL(sb("dhi", [128, T], BF16))
                dlo = TL(sb("dlo", [128, T], BF16))
                rlnv = TL(sb("rlnv", [128, T], F32))
                rrec = TL(sb("rrec", [128, T], F32))
                tabs = [TL(sb("tab%d" % k, [128, T], F32)) for k in range(4)]
                scr = TL(sb("scr", [128, KC, T], BF16), KC)
                cqf = TL(sb("cqf", [128, 4, T], F32), 4)
                cqn = TL(sb("cqn", [128, 4, T], BF16), 4)
                qn = TL(sb("qn", [128, 8, T], BF16), 8)
                qrp = TL(sb("qrp", [128, 8, T], BF16), 8)
                Oa = TL(sb("Oa", [128, 8, T], BF16), 8)
                Ob = TL(sb("Ob", [128, 8, T], BF16), 8)
                kawin = TL(sb("kawin", [128, 2, 768], BF16), 2)
                vawin = TL(sb("vawin", [128, 6, 256], BF16))
                krT = TL(sb("krT", [128, S], BF16))
                NKV = 4
                knq = [TL(sb("knq%d" % k, [128, 1024], BF16)) for k in range(NKV)]
                vbq = [TL(sb("vbq%d" % k, [128, 8, 128], BF16)) for k in range(NKV)]
                NWR = 4
                wring = [TL(sb("wr%d" % k, [128, KC, 128], BF16)) for k in range(NWR)]
                maskb = TL(sb("maskb", [128, 2, 512], BF16))
                ss = E.psum[0]
                pr = Ring(E.psum[1:5] + [E.psum[7]])
                Xacc = E.psum[5]
                prw = Ring(E.psum[1:4])
                wacc = [(E.psum[6], E.psum[7]), (E.psum[4], E.psum[5])]
                ptrw = Ring([TL(sb("ptw%d" % k, [128, T], BF16)) for k in range(7)])
                ptr = Ring(ptrw.t[0:7])
                pr7 = Ring(E.psum[1:8])
                Oacc, Dacc = E.psum[6], E.psum[7]
                acol0, bcol0, gcol0 = mcol(l, 3, 0), mcol(l, 4, 0), mcol(l, 5, 0)

                P.dma("sp", lambda e: e.dma_start(out=krT.t[:], in_=krT_d[:, :]), reads=list(krT_buf), writes=krT.b)
                P.op("dve", lambda e: e.memset(qrp.t[:], 0.0), writes=qrp.b)
                for which in range(2):
                    mt = E.tt.next()
                    P.dma("sp", lambda e, mt=mt, which=which: e.dma_start(out=mt.t[:], in_=masks_d[:, which, :]), writes=mt.b)
                    P.op("dve", lambda e, mt=mt, which=which: e.tensor_copy(out=maskb.t[:, which, :], in_=mt.t[:]), reads=mt.b, writes=maskb.b)
                es_hi = TL(sb("es_hi", [1, 1024], BF16), 2)
                es_lo = TL(sb("es_lo", [1, 1024], BF16), 2)
                for kv in range(2):
                    stg_ = E.lnv if kv == 0 else E.rstd
                    tmp_ = E.tt.next()
                    c0_ = l * 1024 + kv * 512
                    P.dma("sp", lambda e, stg_=stg_, c0_=c0_: e.dma_start(out=stg_.t[0:1, :], in_=sinkrow_d[0:1, c0_:c0_ + 512]), writes=stg_.b)
                    P.op("act", lambda e, stg_=stg_: e.activation(out=stg_.t[0:1, :], in_=stg_.t[0:1, :], func=AF.Exp), reads=stg_.b, writes=stg_.b)
                    P.op("dve", lambda e, stg_=stg_, kv=kv: e.tensor_copy(out=es_hi.t[0:1, kv * 512:(kv + 1) * 512], in_=stg_.t[0:1, :]),
                         reads=stg_.b, writes=[es_hi.b[kv]])
                    P.op("dve", lambda e, stg_=stg_, tmp_=tmp_, kv=kv: e.tensor_tensor(out=tmp_.t[0:1, :], in0=stg_.t[0:1, :], in1=es_hi.t[0:1, kv * 512:(kv + 1) * 512], op=ALU.subtract),
                         reads=stg_.b + [es_hi.b[kv]], writes=tmp_.b)
                    P.op("dve", lambda e, tmp_=tmp_, kv=kv: e.tensor_copy(out=es_lo.t[0:1, kv * 512:(kv + 1) * 512], in_=tmp_.t[0:1, :]),
                         reads=tmp_.b, writes=[es_lo.b[kv]])

                wseq = []
                for i in range(NT):
                    for hd in range(8):
                        wseq.append((in_b[l][IN_QA + hd], in_bufs[l][IN_QA + hd], KC))
                        wseq.append((in_b[l][IN_QASW + hd], in_bufs[l][IN_QASW + hd], KC))
                    for j in range(4):
                        wseq.append((in_b[l][IN_CQ + j], in_bufs[l][IN_CQ + j], KC))
                    for hd in range(8):
                        wseq.append((uq_b[l][UQ_QN + hd], uq_bufs[l][UQ_QN + hd], 4))
                    for jj in range(4):
                        wseq.append((uq_b[l][UQ_QR + jj], uq_bufs[l][UQ_QR + jj], 4))
                        wseq.append((uq_b[l][UQ_QRSW + jj], uq_bufs[l][UQ_QRSW + jj], 4))
                    for c in range(KC):
                        wseq.append((in_b[l][IN_GATE + c], in_bufs[l][IN_GATE + c], KC))
                        wseq.append((in_b[l][IN_GATE + KC + c], in_bufs[l][IN_GATE + KC + c], KC))
                        wseq.append((oa_b[l][c], oa_bufs[l][c], 8))
                        wseq.append((ob_b[l][c], ob_bufs[l][c], 8))
                    for c in range(KC):
                        wseq.append((out_b[l][c], out_bufs[l][c], KC))
                wslot = {}
                wpos = [0]
                LOOKW = NWR - 1

                def wissue(u):
                    if u >= len(wseq):
                        return
                    ap, bf, kc = wseq[u]
                    wt = wring[u % NWR]
                    P.dma("sp", lambda e, wt=wt, ap=ap, kc=kc: e.dma_start(out=wt.t[:, 0:kc, :], in_=ap), reads=[bf], writes=wt.b)
                    wslot[u] = wt

                for u in range(LOOKW):
                    wissue(u)

                def next_w():
                    u = wpos[0]
                    wpos[0] += 1
                    wissue(u + LOOKW)
                    return wslot.pop(u)

                kvseq = [(i, hd, q4) for i in range(NT) for hd in range(8) for q4 in range(4)]
                kvslot = {}
                kvpos = [0]
                LOOKKV = NKV - 2

                def kvissue(u):
                    if u >= len(kvseq):
                        return
                    _, hd, q4 = kvseq[u]
                    kt = knq[u % NKV]
                    vt = vbq[u % NKV]
                    P.dma("sp", lambda e, kt=kt, hd=hd, q4=q4: e.dma_start(out=kt.t[:], in_=knT_d[hd][:, q4 * 1024:(q4 + 1) * 1024]),
                          reads=[knT_buf[2 * q4][hd], knT_buf[2 * q4 + 1][hd]], writes=kt.b)
                    P.dma("sp", lambda e, vt=vt, hd=hd, q4=q4: e.dma_start(out=vt.t[:], in_=vb_d[hd][:, 8 * q4:8 * q4 + 8, :]),
                          reads=vb_buf[2 * q4] + vb_buf[2 * q4 + 1], writes=vt.b)
                    kvslot[u] = (kt, vt)

                for u in range(LOOKKV):
                    kvissue(u)

                def next_kv():
                    u = kvpos[0]
                    kvpos[0] += 1
                    kvissue(u + LOOKKV)
                    return kvslot.pop(u)

                def rden_from(dacc):
                    P.op("act", lambda e: e.activation(out=rlnv.t[:], in_=dacc.t[:], func=AF.Ln), reads=dacc.b, writes=rlnv.b)
                    P.op("act", lambda e: e.activation(out=rrec.t[:], in_=rlnv.t[:], func=AF.Exp, scale=-1.0), reads=rlnv.b, writes=rrec.b)

                emit_norm(E, 0, src_ap, src_bufs, ss, hh[0], acol0, bcol0)
                for i in range(NT):
                    h = hh[i % 2]
                    load_tabs(tabs, i)
                    kb0 = max(4 * i - 1, 0)
                    kb1 = min(4 * i + 4, 31)
                    nkb = kb1 - kb0 + 1
                    tl_ = sorted(set(kb // 4 for kb in range(kb0, kb1 + 1)))
                    for kv in range(2):
                        P.dma("sp", lambda e, kv=kv, kb0=kb0, nkb=nkb: e.dma_start(out=kawin.t[:, kv, 0:nkb * 128], in_=kaT_d[kv][:, kb0 * 128:(kb0 + nkb) * 128]),
                              reads=[kaT_buf[t_][kv] for t_ in tl_], writes=[kawin.b[kv]])
                    P.dma("sp", lambda e, kb0=kb0, nkb=nkb: e.dma_start(out=vawin.t[:, 0:nkb, :], in_=va_d[kb0:kb0 + nkb].rearrange("b p d -> p b d")),
                          reads=[b_ for t_ in tl_ for b_ in va_buf[t_]], writes=vawin.b)

                    def projw(w, h=h):
                        p = pr.next()
                        mm_group(P, p.t[:], [(w.t[:, k, :], h.t[:, k, :]) for k in range(KC)], reads=w.b + h.b, writes=p.b)
                        return p

                    for hd in range(8):
                        p1 = projw(next_w())
                        p2 = projw(next_w())
                        rope3(E, p1, p2, tabs[0], tabs[1], scr.t[:, hd, :], [scr.b[hd]])
                    for j in range(4):
                        p = projw(next_w())
                        P.op("act", lambda e, p=p, j=j: e.activation(out=cqf.t[:, j, :], in_=p.t[:], func=AF.Identity), reads=p.b, writes=[cqf.b[j]])
                        s = E.sq.next()
                        P.op("act", lambda e, s=s, j=j: e.activation(out=s.t[:], in_=cqf.t[:, j, :], func=AF.Square), reads=[cqf.b[j]], writes=s.b)
                        P.op("pe", lambda e, s=s, j=j: e.matmul(ss.t[:], ones_bf.t[:], s.t[:], start=(j == 0), stop=(j == 3)),
                             reads=s.b + ones_bf.b, writes=ss.b)
                    rstd_from(E, ss, 512)
                    for j in range(4):
                        gc = vcol(l, "gcq", j)
                        P.op("dve", lambda e, j=j, gc=gc: e.scalar_tensor_tensor(
                            out=cqn.t[:, j, :], in0=cqf.t[:, j, :], scalar=vecs.t[:, gc:gc + 1], in1=E.rstd.t[:], op0=ALU.mult, op1=ALU.mult),
                            reads=[cqf.b[j]] + vecs.b + E.rstd.b, writes=[cqn.b[j]])
                    for hd in range(8):
                        p = pr.next()
                        w = next_w()
                        mm_group(P, p.t[:], [(w.t[:, k, :], cqn.t[:, k, :]) for k in range(4)], reads=w.b + cqn.b, writes=p.b)
                        P.op("act", lambda e, p=p, hd=hd: e.activation(out=qn.t[:, hd, :], in_=p.t[:], func=AF.Identity), reads=p.b, writes=[qn.b[hd]])
                    for jj in range(4):
                        p1 = pr.next()
                        w = next_w()
                        mm_group(P, p1.t[:], [(w.t[:, k, :], cqn.t[:, k, :]) for k in range(4)], reads=w.b + cqn.b, writes=p1.b)
                        p2 = pr.next()
                        w = next_w()
                        mm_group(P, p2.t[:], [(w.t[:, k, :], cqn.t[:, k, :]) for k in range(4)], reads=w.b + cqn.b, writes=p2.b)
                        t1_ = E.tt.next()
                        t2_ = E.tt.next()
                        P.op("dve", lambda e, t1_=t1_, p1=p1: e.tensor_tensor(out=t1_.t[:], in0=p1.t[:], in1=tabs[2].t[:], op=ALU.mult), reads=p1.b + tabs[2].b, writes=t1_.b)
                        P.op("dve", lambda e, t2_=t2_, p2=p2: e.tensor_tensor(out=t2_.t[:], in0=p2.t[:], in1=tabs[3].t[:], op=ALU.mult), reads=p2.b + tabs[3].b, writes=t2_.b)
                        for par in range(2):
                            lo_ = 64 * par
                            P.op("dve", lambda e, t1_=t1_, t2_=t2_, lo_=lo_, jj=jj, par=par: e.tensor_tensor(
                                out=qrp.t[lo_:lo_ + 64, 2 * jj + par, :], in0=t1_.t[lo_:lo_ + 64, :], in1=t2_.t[lo_:lo_ + 64, :], op=ALU.add),
                                reads=t1_.b + t2_.b, writes=[qrp.b[2 * jj + par]])

                    combos = [(kv, j) for kv in range(2) for j in range(4)] if (DBG_B is None or "win" in DBG_B) else []

                    def win_front(k):
                        kv, j = combos[k]
                        n = 4 * i + j
                        outl = []
                        for kb in [kb for kb in (n - 1, n, n + 1) if 0 <= kb <= 31]:
                            sp_ = prw.next()
                            ko = (kb - kb0) * 128
                            P.op("pe", lambda e, sp_=sp_, kv=kv, ko=ko, j=j: e.matmul(
                                sp_.t[:].rearrange("p (g q) -> p g q", g=4), kawin.t[:, kv, ko:ko + 128],
                                scr.t[:, 4 * kv:4 * kv + 4, j * 128:(j + 1) * 128], start=True, stop=True),
                                reads=[kawin.b[kv]] + scr.b[4 * kv:4 * kv + 4], writes=sp_.b)
                            pt = ptrw.next()
                            P.op("act", lambda e, pt=pt, sp_=sp_: e.activation(out=pt.t[:], in_=sp_.t[:], func=AF.Exp, scale=SC_A), reads=sp_.b, writes=pt.b)
                            if kb != n:
                                which = 0 if kb == n - 1 else 1
                                P.op("dve", lambda e, pt=pt, which=which: e.tensor_tensor(out=pt.t[:], in0=pt.t[:], in1=maskb.t[:, which, :], op=ALU.mult),
                                     reads=pt.b + maskb.b, writes=pt.b)
                            outl.append((kb, pt))
                        return outl

                    def win_back(k, pts):
                        kv, j = combos[k]
                        Oa_, Da_ = wacc[k % 2]
                        nk = len(pts)
                        for idx, (kb, pt) in enumerate(pts):
                            vo = kb - kb0
                            P.op("pe", lambda e, pt=pt, vo=vo, kv=kv, idx=idx, nk=nk: e.matmul(
                                Oa_.t[:], vawin.t[:, vo, kv * 128:(kv + 1) * 128], pt.t[:], start=(idx == 0), stop=(idx == nk - 1)),
                                reads=pt.b + vawin.b, writes=Oa_.b)
                            P.op("pe", lambda e, pt=pt, idx=idx: e.matmul(Da_.t[:], ones_bf.t[:], pt.t[:], start=(idx == 0), stop=False),
                                 reads=pt.b + ones_bf.b, writes=Da_.b)
                        P.op("pe", lambda e, kv=kv: e.matmul(Da_.t[:], ones_bf.t[0:1, :], es_hi.t[0:1, kv * 512:(kv + 1) * 512], start=False, stop=False),
                             reads=[es_hi.b[kv]] + ones_bf.b, writes=Da_.b)
                        P.op("pe", lambda e, kv=kv: e.matmul(Da_.t[:], ones_bf.t[0:1, :], es_lo.t[0:1, kv * 512:(kv + 1) * 512], start=False, stop=True),
                             reads=[es_lo.b[kv]] + ones_bf.b, writes=Da_.b)
                        rden_from(Da_)
                        P.op("dve", lambda e, kv=kv, j=j: e.tensor_tensor(
                            out=Oa.t[:, 4 * kv:4 * kv + 4, j * 128:(j + 1) * 128], in0=Oa_.t[:].rearrange("p (g q) -> p g q", g=4),
                            in1=rrec.t[:].rearrange("p (g q) -> p g q", g=4), op=ALU.mult),
                            reads=Oa_.b + rrec.b, writes=Oa.b[4 * kv:4 * kv + 4])

                    if combos:
                        front = win_front(0)
                        for k in range(len(combos)):
                            nxt_ = win_front(k + 1) if k + 1 < len(combos) else None
                            win_back(k, front)
                            front = nxt_

                    for hd in (range(MLA_HEADS) if (DBG_B is None or "mla" in DBG_B) else []):
                        ph = 64 * (hd % 2)
                        slots = [next_kv() for _ in range(1)]
                        cur = {"q4": 0, "kv": slots[0]}

                        def emit_s(kb, hd=hd, ph=ph):
                            q4, kbl = kb // 8, kb % 8
                            if q4 != cur["q4"]:
                                cur["q4"] = q4
                                cur["kv"] = next_kv()
                            kt, vt = cur["kv"]
                            sp_ = pr.next()
                            mm_group(P, sp_.t[:], [(kt.t[:, kbl * 128:(kbl + 1) * 128], qn.t[:, hd, :]),
                                                   (krT.t[:, kb * 128:(kb + 1) * 128], qrp.t[:, hd, :])],
                                     reads=kt.b + [qn.b[hd], qrp.b[hd]] + krT.b, writes=sp_.b)
                            return sp_, vt, kbl

                        sq_ = [emit_s(0), emit_s(1), emit_s(2)]
                        for kb in range(32):
                            sp_, vt, kbl = sq_.pop(0)
                            if kb + 3 < 32:
                                sq_.append(emit_s(kb + 3))
                            pt = ptr.next()
                            P.op("act", lambda e, pt=pt, sp_=sp_: e.activation(out=pt.t[:], in_=sp_.t[:], func=AF.Exp, scale=SC_B), reads=sp_.b, writes=pt.b)
                            P.op("pe", lambda e, pt=pt, vt=vt, kbl=kbl, kb=kb: e.matmul(Oacc.t[:], vt.t[:, kbl, :], pt.t[:], start=(kb == 0), stop=(kb == 31)),
                                 reads=pt.b + vt.b, writes=Oacc.b)
                            if kb % 4 == 3:
                                ae_, an_, first_ = accO, "pool", (kb == 3)
                            else:
                                ae_, an_, first_ = Xacc, "dve", (kb == 0)
                            if first_:
                                P.op(an_, lambda e, ae_=ae_, pt=pt: e.tensor_copy(out=ae_.t[:], in_=pt.t[:]), reads=pt.b, writes=ae_.b)
                            else:
                                P.op(an_, lambda e, ae_=ae_, pt=pt: e.tensor_tensor(out=ae_.t[:], in0=ae_.t[:], in1=pt.t[:], op=ALU.add),
                                     reads=pt.b + ae_.b, writes=ae_.b)
                        P.op("dve", lambda e: e.tensor_tensor(out=accE.t[:], in0=Xacc.t[:], in1=accO.t[:], op=ALU.add), reads=Xacc.b + accO.b, writes=accE.b)
                        P.op("dve", lambda e: e.tensor_copy(out=dhi.t[:], in_=accE.t[:]), reads=accE.b, writes=dhi.b)
                        P.op("dve", lambda e: e.tensor_tensor(out=accO.t[:], in0=accE.t[:], in1=dhi.t[:], op=ALU.subtract), reads=accE.b + dhi.b, writes=accO.b)
                        P.op("dve", lambda e: e.tensor_copy(out=dlo.t[:], in_=accO.t[:]), reads=accO.b, writes=dlo.b)
                        dps_ = pr.next()
                        mm_group(P, dps_.t[:], [(ones_bf.t[:], dhi.t[:]), (ones_bf.t[:], dlo.t[:])], reads=dhi.b + dlo.b + ones_bf.b, writes=dps_.b)
                        rden_from(dps_)
                        P.op("dve", lambda e, hd=hd: e.tensor_tensor(out=Ob.t[:, hd, :], in0=Oacc.t[:], in1=rrec.t[:], op=ALU.mult),
                             reads=Oacc.b + rrec.b, writes=[Ob.b[hd]])
                        if hd == 3 and i + 1 < NT:
                            emit_norm(E, i + 1, src_ap, src_bufs, ss, hh[(i + 1) % 2], acol0, bcol0)

                    pr_save = pr
                    pr = pr7
                    for c in (range(KC) if (DBG_B is None or "merge" in DBG_B) else []):
                        pga = projw(next_w())
                        pgb = projw(next_w())
                        wa_ = next_w()
                        pya = pr.next()
                        mm_group(P, pya.t[:], [(wa_.t[:, k, :], Oa.t[:, k, :]) for k in range(8)], reads=wa_.b + Oa.b, writes=pya.b)
                        wb_ = next_w()
                        pyb = pr.next()
                        mm_group(P, pyb.t[:], [(wb_.t[:, k, :], Ob.t[:, k, :]) for k in range(8)], reads=wb_.b + Ob.b, writes=pyb.b)
                        ga = E.tt.next()
                        gb = E.tt.next()
                        bga = vcol(l, "bg", c)
                        bgb = vcol(l, "bg", KC + c)
                        P.op("act", lambda e, ga=ga, pga=pga, bga=bga: e.activation(out=ga.t[:], in_=pga.t[:], func=AF.Sigmoid, bias=vecs.t[:, bga:bga + 1], scale=1.0),
                             reads=pga.b + vecs.b, writes=ga.b)
                        P.op("act", lambda e, gb=gb, pgb=pgb, bgb=bgb: e.activation(out=gb.t[:], in_=pgb.t[:], func=AF.Sigmoid, bias=vecs.t[:, bgb:bgb + 1], scale=1.0),
                             reads=pgb.b + vecs.b, writes=gb.b)
                        P.op("dve", lambda e, ga=ga, pya=pya: e.tensor_tensor(out=ga.t[:], in0=pya.t[:], in1=ga.t[:], op=ALU.mult), reads=pya.b + ga.b, writes=ga.b)
                        P.op("dve", lambda e, gb=gb, pyb=pyb: e.tensor_tensor(out=gb.t[:], in0=pyb.t[:], in1=gb.t[:], op=ALU.mult), reads=pyb.b + gb.b, writes=gb.b)
                        P.op("dve", lambda e, ga=ga, gb=gb, c=c: e.tensor_tensor(out=scr.t[:, c, :], in0=ga.t[:], in1=gb.t[:], op=ALU.add),
                             reads=ga.b + gb.b, writes=[scr.b[c]])
                    xs_q = [ld_x(E, src_ap, src_bufs, i, c) for c in range(3)]
                    for c in (range(KC) if (DBG_B is None or "out" in DBG_B) else []):
                        wo_ = next_w()
                        if c + 3 < KC:
                            xs_q.append(ld_x(E, src_ap, src_bufs, i, c + 3))
                        po = pr.next()
                        mm_group(P, po.t[:], [(wo_.t[:, k, :], scr.t[:, k, :]) for k in range(KC)], reads=wo_.b + scr.b, writes=po.b)
                        xs = xs_q.pop(0)
                        xo = E.xo.next()
                        P.op("dve", lambda e, xo=xo, po=po, xs=xs, c=c: e.scalar_tensor_tensor(
                            out=xo.t[:], in0=po.t[:], scalar=modv.t[:, gcol0 + c:gcol0 + c + 1], in1=xs.t[:], op0=ALU.mult, op1=ALU.add),
                            reads=po.b + xs.b + modv.b, writes=xo.b)
                        P.dma("sp", lambda e, xo=xo, c=c, i=i: e.dma_start(out=xres[c * 128:(c + 1) * 128, i * T:(i + 1) * T], in_=xo.t[:]),
                              reads=xo.b, writes=[xres_b[i][c]])
                    pr = pr_save
                P.flush()

        cur_ap, cur_bufs = xT, None
        FOLD = STAGES is None
        for l in range(DEPTH):
            if run("ffn1"):
                if (l, 0) != pieces[0] and not FOLD:
                    adaln_phase(l, 0)
                ffn_phase(l, 0, cur_ap, cur_bufs, bgp=((l, 1) if FOLD else None))
                cur_ap, cur_bufs = xres, xres_b
            if run("mixer"):
                if (l, 1) != pieces[0] and not FOLD:
                    adaln_phase(l, 1)
                if run("mixerA"):
                    mixer_pass_a(l, cur_ap, cur_bufs, bgp=((l, 2) if FOLD else None))
                if run("mixerB"):
                    mixer_pass_b(l, cur_ap, cur_bufs)
                cur_ap, cur_bufs = xres, xres_b
            if run("ffn2"):
                if (l, 2) != pieces[0] and not FOLD:
                    adaln_phase(l, 2)
                ffn_phase(l, 1, cur_ap, cur_bufs, bgp=((l + 1, 0) if (FOLD and l + 1 < DEPTH) else None))
                cur_ap, cur_bufs = xres, xres_b
        final_phase(cur_ap, cur_bufs)
    return nc


def _cols(v):
    v = np.asarray(v, np.float32).reshape(-1, 128)
    return np.ascontiguousarray(v.T)


def make_in_maps(inputs):
    x = np.asarray(inputs["x"], np.float32)
    c = np.asarray(inputs["c"], np.float32)
    pos = np.asarray(inputs["positions"], np.int32)
    cols = []
    for l in range(DEPTH):
        cols.append(_cols(inputs["norm_g"][l].reshape(-1)))
        cols.append(_cols(inputs["b_ada"][l]))
        cols.append(_cols(inputs["b_gate"][l]))
        cols.append(_cols(inputs["g_cq"][l]))
        cols.append(_cols(inputs["g_ckv"][l]))
    cols.append(_cols(inputs["g_final"]))
    vecs = np.ascontiguousarray(np.concatenate(cols, axis=1), np.float32)
    fa = (np.float32(10000.0) ** (-np.arange(0, 128, 2, dtype=np.float32) / np.float32(128))).astype(np.float32)
    fr = (np.float32(10000.0) ** (-np.arange(0, 64, 2, dtype=np.float32) / np.float32(64))).astype(np.float32)
    p = np.arange(128)
    consts = np.zeros((128, 4), np.float32)
    consts[:, 0] = fa[p % 64]
    consts[:, 1] = fr[p % 32]
    consts[:, 2] = np.where((p // 64) % 2 == 0, -TWOPI_S, TWOPI_S)
    consts[:, 3] = np.where((p // 32) % 2 == 0, -TWOPI_S, TWOPI_S)
    kk = np.arange(128)[:, None]
    qq = np.arange(128)[None, :]
    m_prev = (kk >= qq).astype(np.float32)
    m_next = (kk <= qq).astype(np.float32)
    masks = np.stack([np.tile(m_prev, (1, 4)), np.tile(m_next, (1, 4))], axis=1).astype(np.float32)
    sink = np.asarray(inputs["sink"], np.float32)
    sinkrow = np.repeat(sink.reshape(DEPTH, 2, 4), 128, axis=2).reshape(1, -1).astype(np.float32)
    shared = {
        "vecs": vecs, "consts": consts, "masks": np.ascontiguousarray(masks), "sinkrow": np.ascontiguousarray(sinkrow),
    }
    for k in ("w_ada", "w_ffn1_gu", "w_ffn1_d", "w_ffn2_gu", "w_ffn2_d", "w_in", "w_uq", "w_ukv", "w_oa", "w_ob", "w_out"):
        shared[k] = np.ascontiguousarray(np.asarray(inputs[k], np.float32))
    maps = []
    for b in range(NCORES):
        m = dict(shared)
        m["xT"] = np.ascontiguousarray(x[b].T)
        m["cT"] = _cols(c[b])
        m["posb"] = np.ascontiguousarray(np.broadcast_to(pos[b][None, :], (128, S)))
        maps.append(m)
    return maps


_NC_CACHE = {}


def kernel(**inputs):
    maps = make_in_maps(inputs)
    if "nc" not in _NC_CACHE:
        _NC_CACHE["nc"] = build_nc()
    nc = _NC_CACHE["nc"]
    maps = [{k: v for k, v in m.items() if k in nc._declared_inputs} for m in maps]
    res = run_bass_kernel_spmd(nc, maps, core_ids=list(range(NCORES)))
    out = np.stack([np.ascontiguousarray(np.asarray(r["outT"], np.float32).T) for r in res.results], axis=0)
    return out
```
